# Optimizing a Trainium2 kernel written in Bass

```python
import math
import jax, jax.numpy as jnp
from jax import lax
import numpy as np

D_MODEL = 1024
BATCH = 32
SEQ = 256
DEPTH = 2
DEC_BATCH = 2
DEC_SEQ = 4096
PAST_LEN = 256

GRID_W = 64
N_EVEN = (DEPTH + 1) // 2
N_ODD = DEPTH // 2
EPS = 1e-6
S5_WIDTH = D_MODEL // 2
S5_CH = 16
S5_GROUPS = S5_WIDTH // S5_CH
S5_STATE = 64
LRU_WIDTH = D_MODEL // 2
LRU_HEADS = 8
LRU_BW = LRU_WIDTH // LRU_HEADS
LRU_CONV = 4
LRU_C = 8.0
HY_WIDTH = D_MODEL
HY_SHORT = 3
HY_BANDS = 16
HY_EMB = 2 * HY_BANDS + 1
HY_FF = 64
HY_SHIFT = 0.05
PEER_HEADS = 8
PEER_DK = 256
PEER_DKH = PEER_DK // 2
PEER_NKEYS = 128
PEER_EXPERTS = PEER_NKEYS ** 2
PEER_TOPK = 16
PEER_BLOCK = 128

kernel_name = 'hybrid_s5_rglru_hyena_peer_diffusion'


def rmsnorm(x, g):
    xf = x.astype(jnp.float32)
    y = xf * lax.rsqrt(jnp.mean(xf * xf, axis=-1, keepdims=True) + EPS)
    return y.astype(x.dtype) * g


def dwconv_rows(x, w, b, grid, pad_left):
    bsz, length, ch = x.shape
    rows, row_len = grid
    taps = w.shape[0]
    xr = x.reshape(bsz, rows, row_len, ch)
    xp = jnp.pad(xr, ((0, 0), (0, 0), (pad_left, taps - 1 - pad_left), (0, 0)))
    y = sum(xp[:, :, k:k + row_len, :] * w[k] for k in range(taps)) + b
    return y.reshape(bsz, length, ch)


def _complex_combine(e1, e2):
    ar1, ai1, br1, bi1 = e1
    ar2, ai2, br2, bi2 = e2
    return (ar2 * ar1 - ai2 * ai1, ar2 * ai1 + ai2 * ar1,
            ar2 * br1 - ai2 * bi1 + br2, ar2 * bi1 + ai2 * br1 + bi2)


def _real_combine(e1, e2):
    a1, b1 = e1
    a2, b2 = e2
    return (a2 * a1, a2 * b1 + b2)


def s5_scan(u, a_re, a_im, log_dt, b_re, b_im, c_re, c_im, h0, reverse):
    f32 = jnp.float32
    a_re = a_re.astype(f32)
    a_im = a_im.astype(f32)
    dt = jnp.exp(log_dt.astype(f32))[:, None]
    mag = jnp.exp(a_re * dt)
    abar_r = mag * jnp.cos(a_im * dt)
    abar_i = mag * jnp.sin(a_im * dt)
    den = a_re * a_re + a_im * a_im
    num_r = abar_r - 1.0
    coef_r = ((num_r * a_re + abar_i * a_im) / den)[..., None]
    coef_i = ((abar_i * a_re - num_r * a_im) / den)[..., None]
    bbar_r = coef_r * b_re - coef_i * b_im
    bbar_i = coef_r * b_im + coef_i * b_re
    bu_r = jnp.einsum('gpc,blgc->blgp', bbar_r, u)
    bu_i = jnp.einsum('gpc,blgc->blgp', bbar_i, u)
    first, last = (-1, 0) if reverse else (0, -1)
    if h0 is not None:
        h0_r = h0[0].astype(f32)
        h0_i = h0[1].astype(f32)
        bu_r = bu_r.at[:, first].add(abar_r * h0_r - abar_i * h0_i)
        bu_i = bu_i.at[:, first].add(abar_r * h0_i + abar_i * h0_r)
    elems = (jnp.broadcast_to(abar_r, bu_r.shape), jnp.broadcast_to(abar_i, bu_r.shape), bu_r, bu_i)
    _, _, h_r, h_i = lax.associative_scan(_complex_combine, elems, reverse=reverse, axis=1)
    y = jnp.einsum('gcp,blgp->blgc', c_re, h_r) - jnp.einsum('gcp,blgp->blgc', c_im, h_i)
    return y, h_r[:, last], h_i[:, last]


def rglru_scan(xb, w_a, b_a, w_x, b_x, lam, h0, reverse):
    bsz, length, width = xb.shape
    xh = xb.reshape(bsz, length, LRU_HEADS, LRU_BW)
    r = jax.nn.sigmoid(jnp.einsum('blhi,hij->blhj', xh, w_a).reshape(bsz, length, width) + b_a)
    i = jax.nn.sigmoid(jnp.einsum('blhi,hij->blhj', xh, w_x).reshape(bsz, length, width) + b_x)
    log_a = -LRU_C * r * jax.nn.softplus(-lam.astype(jnp.float32))
    a = jnp.exp(log_a)
    b = jnp.sqrt(-jnp.expm1(2.0 * log_a)) * (i * xb)
    first, last = (-1, 0) if reverse else (0, -1)
    if h0 is not None:
        b = b.at[:, first].add(a[:, first] * h0.astype(jnp.float32))
    _, h = lax.associative_scan(_real_combine, (a, b), reverse=reverse, axis=1)
    return h, h[:, last]


def mixer_ab(h, P, le, grid, h0):
    f32 = jnp.float32
    bsz, length, _ = h.shape
    z = h @ P['w_in_ab'][le]
    u, xr, xg = jnp.split(z, [S5_WIDTH, S5_WIDTH + LRU_WIDTH], axis=-1)
    uf = u.astype(f32).reshape(bsz, length, S5_GROUPS, S5_CH)
    y = uf * P['s5_d'][le].astype(f32).reshape(S5_GROUPS, S5_CH)
    xb = dwconv_rows(xr, P['lru_conv_w'][le], P['lru_conv_b'][le], grid, 2).astype(f32)
    hsum = jnp.zeros_like(xb)
    st_re, st_im, st_lru = [], [], []
    for d in range(2):
        rev = d == 1
        h0_s5 = None if h0 is None else (h0[0][:, d], h0[1][:, d])
        yd, hr, hi = s5_scan(uf, P['s5_a_re'][le, d], P['s5_a_im'][le, d], P['s5_log_dt'][le, d],
                             P['s5_b_re'][le, d], P['s5_b_im'][le, d], P['s5_c_re'][le, d], P['s5_c_im'][le, d],
                             h0_s5, rev)
        y = y + yd
        hd, hl = rglru_scan(xb, P['lru_w_a'][le, d], P['lru_b_a'][le, d], P['lru_w_x'][le, d], P['lru_b_x'][le, d],
                            P['lru_lambda'][le, d], None if h0 is None else h0[2][:, d], rev)
        hsum = hsum + hd
        st_re.append(hr)
        st_im.append(hi)
        st_lru.append(hl)
    zs = jax.nn.gelu(y.reshape(bsz, length, S5_WIDTH))
    s5_out = zs * jax.nn.sigmoid(zs @ P['s5_w_glu'][le] + P['s5_b_glu'][le])
    lru_out = hsum * jax.nn.gelu(xg.astype(f32))
    out = jnp.concatenate([s5_out, lru_out], axis=-1).astype(h.dtype) @ P['w_out_ab'][le]
    return out, (jnp.stack(st_re, axis=1), jnp.stack(st_im, axis=1), jnp.stack(st_lru, axis=1))


def hyena_filters(length, w1, b1, f1, w2, b2, f2, w3, decay):
    f32 = jnp.float32
    pos = jnp.arange(length, dtype=f32)
    t = (pos / length)[:, None]
    bands = jnp.linspace(1e-4, HY_BANDS - 1, HY_BANDS, dtype=f32)
    ang = (2.0 * math.pi / length) * pos[:, None] * bands[None, :]
    emb = jnp.concatenate([t, jnp.cos(ang), -jnp.sin(ang)], axis=-1)
    z = jnp.sin(f1 * (emb @ w1 + b1))
    z = jnp.sin(f2 * (z @ w2 + b2))
    filt = (z @ w3).astype(f32).reshape(length, 2, HY_WIDTH)
    filt = filt * (jnp.exp(-t[:, :, None] * jnp.abs(decay.astype(f32))) + HY_SHIFT)
    circ = jnp.concatenate([filt[:, 0], jnp.zeros((1, HY_WIDTH), f32), filt[:0:-1, 1]], axis=0)
    return circ * lax.rsqrt(jnp.sum(circ * circ, axis=0, keepdims=True) + EPS)


def bidir_long_conv(v, circ):
    length = v.shape[1]
    n = 2 * length
    vf = jnp.fft.rfft(v, n=n, axis=1)
    kf = jnp.fft.rfft(circ, n=n, axis=0)
    return jnp.fft.irfft(vf * kf[None], n=n, axis=1)[:, :length]


def mixer_c(h, P, lo, grid):
    length = h.shape[1]
    z = h @ P['w_in_c'][lo] + P['b_in_c'][lo]
    z = dwconv_rows(z, P['hy_conv_w'][lo], P['hy_conv_b'][lo], grid, 1)
    x0, x1, v = jnp.split(z, 3, axis=-1)
    v = (v * x1).astype(jnp.float32)
    circ = hyena_filters(length, P['hy_w1'][lo], P['hy_b1'][lo], P['hy_freq1'][lo], P['hy_w2'][lo],
                         P['hy_b2'][lo], P['hy_freq2'][lo], P['hy_w3'][lo], P['hy_decay'][lo])
    v = bidir_long_conv(v, circ) + P['hy_bias'][lo] * v
    y = v.astype(h.dtype) * x0
    return y @ P['w_out_c'][lo] + P['b_out_c'][lo]


def peer(h, wq, keys, u_tab, v_tab):
    bsz, length, dm = h.shape
    ntok = bsz * length
    x = h.reshape(ntok, dm)
    q = (x @ wq).reshape(ntok, PEER_HEADS, 2, PEER_DKH)
    s = jnp.einsum('thpk,hpnk->thpn', q, keys).astype(jnp.float32)
    sv, si = lax.top_k(s, PEER_TOPK)
    cand = (sv[:, :, 0, :, None] + sv[:, :, 1, None, :]).reshape(ntok, PEER_HEADS, PEER_TOPK * PEER_TOPK)
    cidx = (si[:, :, 0, :, None] * PEER_NKEYS + si[:, :, 1, None, :]).reshape(ntok, PEER_HEADS, PEER_TOPK * PEER_TOPK)
    fv, fpos = lax.top_k(cand, PEER_TOPK)
    eidx = jnp.take_along_axis(cidx, fpos, axis=-1)
    g = jax.nn.softmax(fv, axis=-1).astype(h.dtype)
    nblk = ntok // PEER_BLOCK
    eidx = eidx.reshape(nblk, PEER_BLOCK, PEER_HEADS * PEER_TOPK)
    g = g.reshape(nblk, PEER_BLOCK, PEER_HEADS * PEER_TOPK)
    xb = x.reshape(nblk, PEER_BLOCK, dm)

    def block(args):
        xb_, idx_, g_ = args
        act = jax.nn.gelu(jnp.einsum('td,ted->te', xb_, u_tab[idx_]))
        return jnp.einsum('te,ted->td', g_ * act, v_tab[idx_])

    out = lax.map(block, (xb, eidx, g))
    return out.reshape(bsz, length, dm)


def trunk(x, cond, grid, P, cache):
    collect = cache is None
    st_re, st_im, st_lru = [], [], []
    for l in range(DEPTH):
        mod = jax.nn.silu(cond) @ P['w_mod'][l] + P['b_mod'][l]
        sh1, sc1, g1, sh2, sc2, g2 = jnp.split(mod[:, None, :], 6, axis=-1)
        h = rmsnorm(x, P['norm1_g'][l]) * (1 + sc1) + sh1
        if l % 2 == 0:
            le = l // 2
            h0 = None if collect else (cache[0][:, le], cache[1][:, le], cache[2][:, le])
            out, (s_re, s_im, s_lru) = mixer_ab(h, P, le, grid, h0)
            if collect:
                st_re.append(s_re)
                st_im.append(s_im)
                st_lru.append(s_lru)
        else:
            out = mixer_c(h, P, l // 2, grid)
        x = x + g1 * out
        h = rmsnorm(x, P['norm2_g'][l]) * (1 + sc2) + sh2
        x = x + g2 * peer(h, P['peer_wq'][l], P['peer_keys'][l], P['peer_u'][l], P['peer_v'][l])
    y = rmsnorm(x, P['final_g'])
    if collect:
        return y, (jnp.stack(st_re, axis=1), jnp.stack(st_im, axis=1), jnp.stack(st_lru, axis=1))
    return y, None


def setup_inputs(seed: int = 0) -> dict:
    key = jax.random.key(seed)
    ks = iter(jax.random.split(key, 64))
    f32 = jnp.float32

    def nrm(shape, scale):
        return scale * jax.random.normal(next(ks), shape, f32)

    def uni(shape, lo, hi):
        return jax.random.uniform(next(ks), shape, f32, lo, hi)

    D = D_MODEL
    lru_a = uni((N_EVEN, 2, LRU_WIDTH), 0.9, 0.999)
    s5_shape = (N_EVEN, 2, S5_GROUPS, S5_STATE)
    return {
        'x_prompt': nrm((BATCH, SEQ, D), 1.0),
        'x_sample': nrm((DEC_BATCH, DEC_SEQ, D), 1.0),
        'state_s5_re': nrm((DEC_BATCH, N_EVEN, 2, S5_GROUPS, S5_STATE), 0.1),
        'state_s5_im': nrm((DEC_BATCH, N_EVEN, 2, S5_GROUPS, S5_STATE), 0.1),
        'state_lru': nrm((DEC_BATCH, N_EVEN, 2, LRU_WIDTH), 0.5),
        'c': nrm((DEC_BATCH, D), 1.0),
        'c_ctx': nrm((D,), 1.0),
        'norm1_g': 1.0 + nrm((DEPTH, D), 0.01),
        'norm2_g': 1.0 + nrm((DEPTH, D), 0.01),
        'w_mod': nrm((DEPTH, D, 6 * D), 0.5 * D ** -0.5),
        'b_mod': nrm((DEPTH, 6 * D), 0.01),
        'w_in_ab': nrm((N_EVEN, D, S5_WIDTH + 2 * LRU_WIDTH), D ** -0.5),
        's5_a_re': -0.5 + nrm(s5_shape, 0.01),
        's5_a_im': jnp.pi * jnp.arange(S5_STATE, dtype=f32) + nrm(s5_shape, 0.01),
        's5_log_dt': uni((N_EVEN, 2, S5_GROUPS), math.log(1e-3), math.log(1e-1)),
        's5_b_re': nrm((N_EVEN, 2, S5_GROUPS, S5_STATE, S5_CH), (2 * S5_CH) ** -0.5),
        's5_b_im': nrm((N_EVEN, 2, S5_GROUPS, S5_STATE, S5_CH), (2 * S5_CH) ** -0.5),
        's5_c_re': nrm((N_EVEN, 2, S5_GROUPS, S5_CH, S5_STATE), S5_STATE ** -0.5),
        's5_c_im': nrm((N_EVEN, 2, S5_GROUPS, S5_CH, S5_STATE), S5_STATE ** -0.5),
        's5_d': nrm((N_EVEN, S5_WIDTH), 1.0),
        's5_w_glu': nrm((N_EVEN, S5_WIDTH, S5_WIDTH), S5_WIDTH ** -0.5),
        's5_b_glu': nrm((N_EVEN, S5_WIDTH), 0.01),
        'lru_conv_w': nrm((N_EVEN, LRU_CONV, LRU_WIDTH), LRU_CONV ** -0.5),
        'lru_conv_b': nrm((N_EVEN, LRU_WIDTH), 0.01),
        'lru_w_a': nrm((N_EVEN, 2, LRU_HEADS, LRU_BW, LRU_BW), LRU_BW ** -0.5),
        'lru_b_a': nrm((N_EVEN, 2, LRU_WIDTH), 0.01),
        'lru_w_x': nrm((N_EVEN, 2, LRU_HEADS, LRU_BW, LRU_BW), LRU_BW ** -0.5),
        'lru_b_x': nrm((N_EVEN, 2, LRU_WIDTH), 0.01),
        'lru_lambda': jnp.log(lru_a) - jnp.log1p(-lru_a),
        'w_out_ab': nrm((N_EVEN, S5_WIDTH + LRU_WIDTH, D), (S5_WIDTH + LRU_WIDTH) ** -0.5),
        'w_in_c': nrm((N_ODD, D, 3 * HY_WIDTH), D ** -0.5),
        'b_in_c': nrm((N_ODD, 3 * HY_WIDTH), 0.01),
        'hy_conv_w': nrm((N_ODD, HY_SHORT, 3 * HY_WIDTH), HY_SHORT ** -0.5),
        'hy_conv_b': nrm((N_ODD, 3 * HY_WIDTH), 0.01),
        'hy_w1': nrm((N_ODD, HY_EMB, HY_FF), HY_EMB ** -0.5),
        'hy_b1': nrm((N_ODD, HY_FF), 0.1),
        'hy_freq1': 1.0 + nrm((N_ODD, HY_FF), 0.01),
        'hy_w2': nrm((N_ODD, HY_FF, HY_FF), HY_FF ** -0.5),
        'hy_b2': nrm((N_ODD, HY_FF), 0.1),
        'hy_freq2': 1.0 + nrm((N_ODD, HY_FF), 0.01),
        'hy_w3': nrm((N_ODD, HY_FF, 2 * HY_WIDTH), HY_FF ** -0.5),
        'hy_decay': uni((N_ODD, 2, HY_WIDTH), 3.0, 15.0),
        'hy_bias': nrm((N_ODD, HY_WIDTH), 1.0),
        'w_out_c': nrm((N_ODD, HY_WIDTH, D), HY_WIDTH ** -0.5),
        'b_out_c': nrm((N_ODD, D), 0.01),
        'peer_wq': nrm((DEPTH, D, PEER_HEADS * PEER_DK), D ** -0.5),
        'peer_keys': nrm((DEPTH, PEER_HEADS, 2, PEER_NKEYS, PEER_DKH), PEER_DKH ** -0.5),
        'peer_u': nrm((DEPTH, PEER_EXPERTS, D), D ** -0.5),
        'peer_v': nrm((DEPTH, PEER_EXPERTS, D), PEER_HEADS ** -0.5),
        'final_g': 1.0 + nrm((D,), 0.01),
    }


def reference(x_prompt, x_sample, state_s5_re, state_s5_im, state_lru, c, c_ctx,
              norm1_g, norm2_g, w_mod, b_mod, w_in_ab,
              s5_a_re, s5_a_im, s5_log_dt, s5_b_re, s5_b_im, s5_c_re, s5_c_im, s5_d, s5_w_glu, s5_b_glu,
              lru_conv_w, lru_conv_b, lru_w_a, lru_b_a, lru_w_x, lru_b_x, lru_lambda, w_out_ab,
              w_in_c, b_in_c, hy_conv_w, hy_conv_b, hy_w1, hy_b1, hy_freq1, hy_w2, hy_b2, hy_freq2, hy_w3,
              hy_decay, hy_bias, w_out_c, b_out_c,
              peer_wq, peer_keys, peer_u, peer_v, final_g):
    P = dict(norm1_g=norm1_g, norm2_g=norm2_g, w_mod=w_mod, b_mod=b_mod, w_in_ab=w_in_ab,
             s5_a_re=s5_a_re, s5_a_im=s5_a_im, s5_log_dt=s5_log_dt, s5_b_re=s5_b_re, s5_b_im=s5_b_im,
             s5_c_re=s5_c_re, s5_c_im=s5_c_im, s5_d=s5_d, s5_w_glu=s5_w_glu, s5_b_glu=s5_b_glu,
             lru_conv_w=lru_conv_w, lru_conv_b=lru_conv_b, lru_w_a=lru_w_a, lru_b_a=lru_b_a,
             lru_w_x=lru_w_x, lru_b_x=lru_b_x, lru_lambda=lru_lambda, w_out_ab=w_out_ab,
             w_in_c=w_in_c, b_in_c=b_in_c, hy_conv_w=hy_conv_w, hy_conv_b=hy_conv_b,
             hy_w1=hy_w1, hy_b1=hy_b1, hy_freq1=hy_freq1, hy_w2=hy_w2, hy_b2=hy_b2, hy_freq2=hy_freq2,
             hy_w3=hy_w3, hy_decay=hy_decay, hy_bias=hy_bias, w_out_c=w_out_c, b_out_c=b_out_c,
             peer_wq=peer_wq, peer_keys=peer_keys, peer_u=peer_u, peer_v=peer_v, final_g=final_g)
    ctx_grid = (1, x_prompt.shape[1])
    y_prompt, (new_s5_re, new_s5_im, new_lru) = trunk(x_prompt, c_ctx[None, :], ctx_grid, P, None)
    rows = x_sample.shape[1] // GRID_W
    y_sample, _ = trunk(x_sample, c, (rows, GRID_W), P, (state_s5_re, state_s5_im, state_lru))
    return (y_prompt, y_sample, new_s5_re, new_s5_im, new_lru)
```

```python
import math
import contextlib
import numpy as np
import ml_dtypes
import concourse.bass as bass
import concourse.mybir as mybir
from concourse.alu_op_type import AluOpType as ALU
from concourse.bass_utils import run_bass_kernel_spmd

F32 = mybir.dt.float32
BF16 = mybir.dt.bfloat16
I32 = mybir.dt.int32
U32 = mybir.dt.uint32
AF = mybir.ActivationFunctionType
AX = mybir.AxisListType

D = 1024
NCORES = 8
TWO_PI = 2.0 * math.pi


class Buf:
    __slots__ = ("name", "w", "r", "dsem", "dcnt", "t")

    def __init__(self, name, t=None):
        self.name = name
        self.w = None
        self.r = []
        self.dsem = None
        self.dcnt = 0
        self.t = t

    def __getitem__(self, idx):
        return self.t[idx]


class Ctx:
    def __init__(self, nc, same_engine_sync=True):
        self.nc = nc
        self.es = contextlib.ExitStack()
        self.eng = {"pe": nc.tensor, "act": nc.scalar, "dve": nc.vector, "pool": nc.gpsimd, "sp": nc.sync}
        self.sem = {}
        self.cnt = {}
        self.waited = {e: {} for e in self.eng}
        for e in self.eng:
            self.sem[e] = self.es.enter_context(nc.semaphore("sem_" + e))
            self.cnt[e] = 0
        self.same = same_engine_sync
        self.bufs = []
        self.dsems = []
        self.free_dsems = []
        self.nbuf = 0
        self.ninstr = 0
        self.psl = []
        self.psi = 0

    def sbuf(self, shape, dt=F32, name=None, stack=None):
        self.nbuf += 1
        name = f"{name or 'sb'}_{self.nbuf}"
        t = (stack or self.es).enter_context(self.nc.sbuf_tensor(name, list(shape), dt))
        b = Buf(name, t)
        self.bufs.append(b)
        return b

    def psum(self, shape, dt=F32, name=None):
        self.nbuf += 1
        name = f"{name or 'ps'}_{self.nbuf}"
        t = self.es.enter_context(self.nc.psum_tensor(name, list(shape), dt))
        b = Buf(name, t)
        self.bufs.append(b)
        return b

    def next_ps(self):
        b = self.psl[self.psi % len(self.psl)]
        self.psi += 1
        return b

    def vbuf(self, name):
        b = Buf(name)
        self.bufs.append(b)
        return b

    def _wait(self, e, key, val):
        if val <= 0:
            return
        if self.waited[e].get(key, 0) >= val:
            return
        sem = self.sem[key] if isinstance(key, str) else key
        self.eng[e].wait_ge(sem, val)
        self.waited[e][key] = val

    def _deps(self, e, ins, outs, nosame=False):
        deps = []
        for b in ins:
            if b.w is not None:
                deps.append(b.w)
        for b in outs:
            if b.w is not None:
                deps.append(b.w)
            deps.extend(b.r)
        for (k, c) in deps:
            if isinstance(k, str):
                if k == e and (nosame or not self.same):
                    continue
                self._wait(e, k, c)
            else:
                sem, cnt = k
                self._wait(e, sem, cnt[0])

    def op(self, e, fn, outs=(), ins=(), nosame=False):
        self._deps(e, ins, outs, nosame)
        ins_ = fn(self.eng[e])
        self.cnt[e] += 1
        c = self.cnt[e]
        ins_.then_inc(self.sem[e], 1)
        self.ninstr += 1
        for b in ins:
            b.r.append((e, c))
            if len(b.r) > 48:
                b.r = b.r[-48:]
        for b in outs:
            b.w = (e, c)
            b.r = []
        return ins_

    def dma(self, q, out_ap, in_ap, outs=(), ins=(), sb=None, indirect=None, **kw):
        if sb.dsem is None:
            if self.free_dsems:
                sb.dsem = self.free_dsems.pop()
            else:
                sem = self.es.enter_context(self.nc.semaphore("d_" + sb.name))
                sb.dsem = (sem, [0])
                self.dsems.append(sb.dsem)
        self._deps(q, ins, outs)
        if indirect is None:
            ins_ = self.eng[q].dma_start(out=out_ap, in_=in_ap, **kw)
        else:
            ins_ = self.eng[q].indirect_dma_start(out=out_ap, in_=in_ap, **indirect)
        sem, cnt = sb.dsem
        cnt[0] += 16
        ins_.then_inc(sem, 16)
        self.ninstr += 1
        key = (sb.dsem, cnt[0])
        for b in ins:
            b.r.append(key)
            if len(b.r) > 48:
                b.r = b.r[-48:]
        for b in outs:
            b.w = key
            b.r = []
        return ins_

    def barrier(self):
        for e in self.eng:
            for k in self.eng:
                if k != e:
                    self._wait(e, k, self.cnt[k])
            for (sem, cnt) in self.dsems:
                self._wait(e, sem, cnt[0])
        for b in self.bufs:
            b.w = None
            b.r = []
            if b.dsem is not None:
                self.free_dsems.append(b.dsem)
                b.dsem = None


def _dft_consts(L):
    n = 2 * L
    k = np.arange(L, dtype=np.float64)
    w = 2.0 * np.pi * (k + 0.5) / n
    t = np.arange(L, dtype=np.float64)
    ang = np.outer(t, w)
    cs, sn = np.cos(ang), np.sin(ang)
    G = np.concatenate([cs, -sn], axis=1)
    Gb = np.concatenate([cs, sn], axis=1)
    Gb[0, :] = 0.0
    Gi = np.concatenate([cs.T, -sn.T], axis=0) / L
    bf = ml_dtypes.bfloat16
    RW = min(512, L); KH = min(L // 128, 16); RH = min(2 * L // 128, 16)
    Gt = G.astype(np.float32).reshape((L // 128) // KH, KH, 128, 2 * L // RW, RW).transpose(3, 0, 2, 1, 4)
    Git = Gi.astype(np.float32).reshape((2 * L // 128) // RH, RH, 128, L // RW, RW).transpose(3, 0, 2, 1, 4)
    return np.ascontiguousarray(Gt).astype(bf), None, np.ascontiguousarray(Git).astype(bf)


def _emb_const(L):
    pos = np.arange(L, dtype=np.float32)
    t = (pos / np.float32(L))[:, None]
    bands = np.linspace(1e-4, 15, 16, dtype=np.float32)
    ang = (np.float32(2.0 * math.pi / L)) * pos[:, None] * bands[None, :]
    emb = np.concatenate([t, np.cos(ang), -np.sin(ang)], axis=-1).astype(np.float32)
    return np.ascontiguousarray(emb.T)


_CONST_CACHE = {}


def _consts():
    if not _CONST_CACHE:
        for L in (256, 4096):
            G, Gb, Gi = _dft_consts(L)
            _CONST_CACHE[f"G{L}"] = G
            _CONST_CACHE[f"Gi{L}"] = Gi
            _CONST_CACHE[f"emb{L}"] = _emb_const(L)
    return _CONST_CACHE


WEIGHT_SPECS = {
    "norm1_g": [2, 1024], "norm2_g": [2, 1024], "w_mod": [2, 1024, 6144], "b_mod": [2, 6144],
    "w_in_ab": [1, 1024, 1536], "s5_a_re": [1, 2, 32, 64], "s5_a_im": [1, 2, 32, 64], "s5_log_dt": [1, 2, 32],
    "s5_b_re": [1, 2, 32, 64, 16], "s5_b_im": [1, 2, 32, 64, 16], "s5_c_re": [1, 2, 32, 16, 64],
    "s5_c_im": [1, 2, 32, 16, 64], "s5_d": [1, 512], "s5_w_glu": [1, 512, 512], "s5_b_glu": [1, 512],
    "lru_conv_w": [1, 4, 512], "lru_conv_b": [1, 512], "lru_w_a": [1, 2, 8, 64, 64], "lru_b_a": [1, 2, 512],
    "lru_w_x": [1, 2, 8, 64, 64], "lru_b_x": [1, 2, 512], "lru_lambda": [1, 2, 512], "w_out_ab": [1, 1024, 1024],
    "w_in_c": [1, 1024, 3072], "b_in_c": [1, 3072], "hy_conv_w": [1, 3, 3072], "hy_conv_b": [1, 3072],
    "hy_w1": [1, 33, 64], "hy_b1": [1, 64], "hy_freq1": [1, 64], "hy_w2": [1, 64, 64], "hy_b2": [1, 64],
    "hy_freq2": [1, 64], "hy_w3": [1, 64, 2048], "hy_decay": [1, 2, 1024], "hy_bias": [1, 1024],
    "w_out_c": [1, 1024, 1024], "b_out_c": [1, 1024], "peer_wq": [2, 1024, 2048],
    "peer_keys": [2, 8, 2, 128, 128], "final_g": [1024],
}


def build(NP=4, do_P=True, do_S=True, stages=None, dbg=()):
    nc = bass.Bass("TRN2", target_bir_lowering=False)
    c = Ctx(nc)
    TP = NP * 256
    TS = 4096
    W = {k: nc.dram_tensor(k, v, F32, kind="ExternalInput") for k, v in WEIGHT_SPECS.items()}
    I = {}
    I["xp"] = nc.dram_tensor("xp", [TP, D], F32, kind="ExternalInput")
    I["xs"] = nc.dram_tensor("xs", [TS, D], F32, kind="ExternalInput")
    I["cond"] = nc.dram_tensor("cond", [2, D], F32, kind="ExternalInput")
    I["h0re"] = nc.dram_tensor("h0re", [2, 32, 64], F32, kind="ExternalInput")
    I["h0im"] = nc.dram_tensor("h0im", [2, 32, 64], F32, kind="ExternalInput")
    I["h0lru"] = nc.dram_tensor("h0lru", [2, 512], F32, kind="ExternalInput")
    for L in (256, 4096):
        _RW = min(512, L); _KH = min(L // 128, 16); _RH = min(2 * L // 128, 16)
        I[f"G{L}"] = nc.dram_tensor(f"G{L}", [2 * L // _RW, (L // 128) // _KH, 128, _KH, _RW], BF16, kind="ExternalInput")
        I[f"Gi{L}"] = nc.dram_tensor(f"Gi{L}", [L // _RW, (2 * L // 128) // _RH, 128, _RH, _RW], BF16, kind="ExternalInput")
        I[f"emb{L}"] = nc.dram_tensor(f"emb{L}", [33, L], F32, kind="ExternalInput")
    O = {}
    O["yp"] = nc.dram_tensor("yp", [TP, D], F32, kind="ExternalOutput")
    O["ysq"] = nc.dram_tensor("ysq", [1024, D], F32, kind="ExternalOutput")
    I["qrows"] = nc.dram_tensor("qrows", [128, 8], I32, kind="ExternalInput")
    I["peer_uv"] = nc.dram_tensor("peer_uv", [2 * 16384, 2 * D], F32, kind="ExternalInput")
    O["s5re"] = nc.dram_tensor("s5re", [NP, 2, 32, 64], F32, kind="ExternalOutput")
    O["s5im"] = nc.dram_tensor("s5im", [NP, 2, 32, 64], F32, kind="ExternalOutput")
    O["lru"] = nc.dram_tensor("lru", [NP, 2, 512], F32, kind="ExternalOutput")

    def scratch(name, shape, dt=F32):
        kind = "ExternalOutput" if name in dbg else "Internal"
        return nc.dram_tensor(name, shape, dt, kind=kind)

    mod_d = scratch("mod_d", [2, 2, 6144])
    secs = []
    if do_P:
        secs.append(dict(n="P", T=TP, nseq=NP, L=256, row=256, x=I["xp"], ci=0, h0=False, y=O["yp"]))
    if do_S:
        secs.append(dict(n="S", T=TS, nseq=1, L=4096, row=64, x=I["xs"], ci=1, h0=True, y=None))
    for s in secs:
        n, T = s["n"], s["T"]
        s["zT"] = scratch(f"zT_{n}", [3072, T])
        s["mixT"] = scratch(f"mixT_{n}", [1024, T], BF16)
        s["ys5T"] = scratch(f"ys5T_{n}", [512, T])
        s["xa"] = scratch(f"xa_{n}", [T, D])
        s["xb"] = scratch(f"xb_{n}", [T, D])
        s["filt"] = scratch(f"filt_{n}", [s["L"], 2048], BF16)
        s["xq"] = scratch(f"xq_{n}", [1024, D])

    def want(st):
        return stages is None or st in stages

    for i in range(7):
        c.psl.append(c.psum([128, 512], F32, name=f"pb{i}"))
    psb = c.psum([128, 1024], BF16, name="psb")
    ident = c.sbuf([128, 128], F32, name="ident")
    identb = c.sbuf([128, 128], BF16, name="identb")
    ones = c.sbuf([128, 2], F32, name="ones")
    iota_f = c.sbuf([128, 128], F32, name="iota_f")
    tcol = c.sbuf([128, 32], F32, name="tcol")
    c.op("pool", lambda e: e.iota(iota_f[:], pattern=[[1, 128]], base=0, channel_multiplier=-1,
                                   allow_small_or_imprecise_dtypes=True), outs=[iota_f])
    c.op("dve", lambda e: e.tensor_single_scalar(ident[:], iota_f[:], 0.0, ALU.is_equal), outs=[ident], ins=[iota_f])
    c.op("dve", lambda e: e.tensor_copy(identb[:], ident[:]), outs=[identb], ins=[ident])
    c.op("dve", lambda e: e.memset(ones[:], 1.0), outs=[ones])
    c.op("pool", lambda e: e.iota(iota_f[:], pattern=[[1, 128]], base=0, channel_multiplier=0,
                                   allow_small_or_imprecise_dtypes=True), outs=[iota_f])
    c.op("pool", lambda e: e.iota(tcol[:], pattern=[[128, 32]], base=0, channel_multiplier=1,
                                   allow_small_or_imprecise_dtypes=True), outs=[tcol])

    def gelu(dst, dst_b, src, src_b, tmp, tmp_b):
        c.op("dve", lambda e: e.tensor_tensor(out=tmp, in0=src, in1=src, op=ALU.mult), outs=[tmp_b], ins=[src_b])
        c.op("dve", lambda e: e.tensor_scalar(out=tmp, in0=tmp, scalar1=0.044715, scalar2=1.0, op0=ALU.mult, op1=ALU.add),
             outs=[tmp_b], ins=[tmp_b])
        c.op("dve", lambda e: e.tensor_tensor(out=tmp, in0=tmp, in1=src, op=ALU.mult), outs=[tmp_b], ins=[tmp_b, src_b])
        c.op("act", lambda e: e.activation(tmp, tmp, AF.Sigmoid, scale=1.5957691216057308), outs=[tmp_b], ins=[tmp_b])
        c.op("dve", lambda e: e.tensor_tensor(out=dst, in0=src, in1=tmp, op=ALU.mult), outs=[dst_b], ins=[src_b, tmp_b])

    def fm_load(q, dst_b, dst_ap, src_1d):
        c.dma(q, dst_ap, src_1d.rearrange("(k p) -> p k", p=128), outs=[dst_b], sb=dst_b,
              allow_slow_non_contiguous=True)

    def stage_mod():
        with contextlib.ExitStack() as st:
            condT = c.sbuf([128, 8, 2], F32, "condT", st)
            for ci in range(2):
                c.dma("sp", condT[:, :, ci], I["cond"][ci, :].rearrange("(k p) -> p k", p=128), outs=[condT], sb=condT,
                      allow_slow_non_contiguous=True)
            c.op("act", lambda e: e.activation(condT[:], condT[:], AF.Silu), outs=[condT], ins=[condT])
            wts = [c.sbuf([128, 8, 512], F32, f"wmod{i}", st) for i in range(2)]
            brow = c.sbuf([2, 6144], F32, "brow", st)
            mrow = c.sbuf([2, 6144], F32, "mrow", st)
            for l in range(2):
                c.dma("sp", brow[:], W["b_mod"][l, :].partition_broadcast(2), outs=[brow], sb=brow)
                for n in range(12):
                    wt = wts[n % 2]
                    c.dma("sp", wt[:], W["w_mod"][l, :, n * 512:(n + 1) * 512].rearrange("(k p) n -> p k n", p=128),
                          outs=[wt], sb=wt)
                    ps = c.next_ps()
                    for k in range(8):
                        c.op("pe", lambda e, k=k: e.matmul(ps[0:2, :], condT[:, k, :], wt[:, k, :], start=(k == 0), stop=(k == 7)),
                             outs=[ps], ins=[condT, wt])
                    c.op("dve", lambda e: e.tensor_tensor(out=mrow[:, n * 512:(n + 1) * 512], in0=ps[0:2, :],
                                                          in1=brow[:, n * 512:(n + 1) * 512], op=ALU.add),
                         outs=[mrow], ins=[ps, brow])
                c.dma("sp", mod_d[l, :, :], mrow[:], ins=[mrow], sb=mrow)
            c.barrier()

    def stage_inproj(s, x_src, l, Wd, bias_d, N, which):
        T = s["T"]
        nm = N // 128
        with contextlib.ExitStack() as st:
            Wsb = c.sbuf([128, 8, N], BF16, "Wsb", st)
            wstg = [c.sbuf([128, N], F32, f"wstg{i}", st) for i in range(2)]
            for k in range(8):
                ws_ = wstg[k % 2]
                c.dma("sp", ws_[:], Wd[k * 128:(k + 1) * 128, :], outs=[ws_], sb=ws_)
                c.op("act" if k % 2 == 0 else "dve",
                     (lambda e, k=k, ws_=ws_: e.copy(Wsb[:, k, :], ws_[:])) if k % 2 == 0 else (lambda e, k=k, ws_=ws_: e.tensor_copy(Wsb[:, k, :], ws_[:])),
                     outs=[Wsb], ins=[ws_])
            gv = c.sbuf([128, 8], F32, "gv", st)
            scv = c.sbuf([128, 8], F32, "scv", st)
            shv = c.sbuf([128, 8], F32, "shv", st)
            fm_load("sp", gv, gv[:], W["norm1_g" if which == 1 else "norm2_g"][l, :])
            base = 0 if which == 1 else 3 * D
            fm_load("sp", shv, shv[:], mod_d[l, s["ci"], base:base + D])
            fm_load("sp", scv, scv[:], mod_d[l, s["ci"], base + D:base + 2 * D])
            c.op("dve", lambda e: e.scalar_tensor_tensor(out=scv[:], in0=scv[:], scalar=1.0, in1=gv[:], op0=ALU.add, op1=ALU.mult),
                 outs=[scv], ins=[scv, gv])
            bv = None
            if bias_d is not None:
                bv = c.sbuf([128, nm], F32, "bv", st)
                fm_load("sp", bv, bv[:], bias_d)
            xts = [c.sbuf([128, 4, D], F32, f"xt{i}", st) for i in range(2)]
            sq = c.sbuf([128, 4, D], F32, "sq", st)
            ss = c.sbuf([128, 4], F32, "ss", st)
            hT = c.sbuf([128, 8, 512], BF16, "hT", st)
            zos = [c.sbuf([128, 4, 512], F32, f"zo{i}", st) for i in range(2)]
            nt = T // 512
            zi = 0
            for t in range(nt):
                xt = xts[t % 2]
                c.dma("sp", xt[:], x_src[t * 512:(t + 1) * 512, :].rearrange("(j p) d -> p j d", p=128), outs=[xt], sb=xt)
                c.op("dve", lambda e: e.tensor_tensor(out=sq[:], in0=xt[:], in1=xt[:], op=ALU.mult), outs=[sq], ins=[xt])
                c.op("dve", lambda e: e.tensor_reduce(out=ss[:], in_=sq[:], axis=AX.X, op=ALU.add), outs=[ss], ins=[sq])
                c.op("dve", lambda e: e.tensor_scalar(out=ss[:], in0=ss[:], scalar1=1.0 / D, scalar2=1e-6, op0=ALU.mult, op1=ALU.add),
                     outs=[ss], ins=[ss])
                c.op("act", lambda e: e.activation(ss[:], ss[:], AF.Sqrt), outs=[ss], ins=[ss])
                c.op("dve", lambda e: e.reciprocal(ss[:], ss[:]), outs=[ss], ins=[ss])
                for j in range(4):
                    c.op("act", lambda e, j=j: e.activation(sq[:, j, :], xt[:, j, :], AF.Copy, scale=ss[:, j:j + 1]),
                         outs=[sq], ins=[xt, ss])
                for k in range(8):
                    ps = c.next_ps()
                    for j in range(4):
                        c.op("pe", lambda e, j=j, k=k: e.transpose(ps[:, j * 128:(j + 1) * 128], sq[:, j, k * 128:(k + 1) * 128], ident[:]),
                             outs=[ps], ins=[sq, ident])
                    c.op("act", lambda e, k=k: e.activation(hT[:, k, :], ps[:], AF.Identity, scale=scv[:, k:k + 1], bias=shv[:, k:k + 1]),
                         outs=[hT], ins=[ps, scv, shv], nosame=True)
                for m0 in range(0, nm, 4):
                    zo = zos[zi % 2]
                    zi += 1
                    for mm in range(4):
                        m = m0 + mm
                        ps = c.next_ps()
                        for k in range(8):
                            c.op("pe", lambda e, k=k, m=m: e.matmul(ps[:], Wsb[:, k, m * 128:(m + 1) * 128], hT[:, k, :], start=(k == 0), stop=(k == 7)),
                                 outs=[ps], ins=[Wsb, hT])
                        if bv is not None:
                            c.op("act", lambda e, m=m, mm=mm: e.activation(zo[:, mm, :], ps[:], AF.Identity, bias=bv[:, m:m + 1]),
                                 outs=[zo], ins=[ps, bv])
                        elif mm % 2 == 0:
                            c.op("act", lambda e, mm=mm: e.copy(zo[:, mm, :], ps[:]), outs=[zo], ins=[ps])
                        else:
                            c.op("dve", lambda e, mm=mm: e.tensor_copy(zo[:, mm, :], ps[:]), outs=[zo], ins=[ps])
                    c.dma("sp", s["zT"][m0 * 128:(m0 + 4) * 128, t * 512:(t + 1) * 512].rearrange("(m p) t -> p m t", p=128),
                          zo[:], ins=[zo], sb=zo)
            c.barrier()

    def stage_outproj(s, x_src, x_dst, l, Wd, bias_d):
        T = s["T"]
        with contextlib.ExitStack() as st:
            Wsb = c.sbuf([128, 8, D], BF16, "Wo", st)
            wstg = [c.sbuf([128, D], F32, f"wostg{i}", st) for i in range(2)]
            for k in range(8):
                ws_ = wstg[k % 2]
                c.dma("sp", ws_[:], Wd[k * 128:(k + 1) * 128, :], outs=[ws_], sb=ws_)
                c.op("act" if k % 2 == 0 else "dve",
                     (lambda e, k=k, ws_=ws_: e.copy(Wsb[:, k, :], ws_[:])) if k % 2 == 0 else (lambda e, k=k, ws_=ws_: e.tensor_copy(Wsb[:, k, :], ws_[:])),
                     outs=[Wsb], ins=[ws_])
            g1 = c.sbuf([128, 8], F32, "g1", st)
            fm_load("sp", g1, g1[:], mod_d[l, s["ci"], 2 * D:3 * D])
            gb = None
            if bias_d is not None:
                gb = c.sbuf([128, 8], F32, "gb", st)
                fm_load("sp", gb, gb[:], bias_d)
                c.op("dve", lambda e: e.tensor_tensor(out=gb[:], in0=gb[:], in1=g1[:], op=ALU.mult), outs=[gb], ins=[gb, g1])
            mts = [c.sbuf([128, 8, 512], BF16, f"mt{i}", st) for i in range(2)]
            xts = [c.sbuf([128, 4, D], F32, f"xo{i}", st) for i in range(2)]
            oT = c.sbuf([128, 8, 512], F32, "oT", st)
            for t in range(T // 512):
                mt, xt = mts[t % 2], xts[t % 2]
                c.dma("sp", mt[:], s["mixT"][:, t * 512:(t + 1) * 512].rearrange("(k p) t -> p k t", p=128), outs=[mt], sb=mt)
                c.dma("sp", xt[:], x_src[t * 512:(t + 1) * 512, :].rearrange("(j p) d -> p j d", p=128), outs=[xt], sb=xt)
                for m in range(8):
                    ps = c.next_ps()
                    for k in range(8):
                        c.op("pe", lambda e, k=k, m=m: e.matmul(ps[:], Wsb[:, k, m * 128:(m + 1) * 128], mt[:, k, :], start=(k == 0), stop=(k == 7)),
                             outs=[ps], ins=[Wsb, mt])
                    if gb is not None:
                        c.op("act", lambda e, m=m: e.activation(oT[:, m, :], ps[:], AF.Identity, scale=g1[:, m:m + 1], bias=gb[:, m:m + 1]),
                             outs=[oT], ins=[ps, g1, gb], nosame=True)
                    else:
                        c.op("act", lambda e, m=m: e.activation(oT[:, m, :], ps[:], AF.Copy, scale=g1[:, m:m + 1]),
                             outs=[oT], ins=[ps, g1], nosame=True)
                for j in range(4):
                    for half in range(2):
                        ps = c.next_ps()
                        for mm in range(4):
                            m = half * 4 + mm
                            c.op("pe", lambda e, m=m, mm=mm, j=j: e.transpose(ps[:, mm * 128:(mm + 1) * 128], oT[:, m, j * 128:(j + 1) * 128], ident[:]),
                                 outs=[ps], ins=[oT, ident])
                        c.op("dve", lambda e, j=j, half=half: e.tensor_tensor(out=xt[:, j, half * 512:(half + 1) * 512], in0=xt[:, j, half * 512:(half + 1) * 512],
                                                                              in1=ps[:], op=ALU.add), outs=[xt], ins=[xt, ps])
                c.dma("sp", x_dst[t * 512:(t + 1) * 512, :].rearrange("(j p) d -> p j d", p=128), xt[:], ins=[xt], sb=xt)
            c.barrier()

    def stage_s5(s):
        T, nseq, L = s["T"], s["nseq"], s["L"]
        K = int(math.log2(L))
        with contextlib.ExitStack() as st:
            mt_ = c.sbuf([128, 1], F32, "mt_", st)
            mb_ = c.sbuf([128, 1], F32, "mb_", st)
            C1 = c.sbuf([128, K, 64], F32, "C1", st)
            C2 = c.sbuf([128, K, 64], F32, "C2", st)
            J = c.sbuf([128, 64], F32, "J", st)
            BT = c.sbuf([16, 64, 128], BF16, "BT", st)
            CT = c.sbuf([128, 64, 16], BF16, "CT", st)
            Dv = c.sbuf([16, 32], F32, "Dv", st)
            hh = c.sbuf([128, 64], F32, "hh", st) if s["h0"] else None
            stS = c.sbuf([128, nseq, 64], F32, "stS", st) if not s["h0"] else None
            stp = contextlib.ExitStack()
            are = c.sbuf([128, 64], F32, "are", stp)
            aim = c.sbuf([128, 64], F32, "aim", stp)
            dtt = c.sbuf([128, 64], F32, "dtt", stp)
            for h in range(2):
                c.dma("sp", are[h * 64:(h + 1) * 64, :], W["s5_a_re"][0].rearrange("d g p -> p (d g)"), outs=[are], sb=are,
                      allow_slow_non_contiguous=True)
                c.dma("sp", aim[h * 64:(h + 1) * 64, :], W["s5_a_im"][0].rearrange("d g p -> p (d g)"), outs=[aim], sb=aim,
                      allow_slow_non_contiguous=True)
            c.dma("sp", dtt[:], W["s5_log_dt"][0].rearrange("d g -> (d g)").partition_broadcast(128), outs=[dtt], sb=dtt)
            c.op("act", lambda e: e.activation(dtt[:], dtt[:], AF.Exp), outs=[dtt], ins=[dtt])
            lr = c.sbuf([128, 64], F32, "lr", stp)
            li = c.sbuf([128, 64], F32, "li", stp)
            c.op("dve", lambda e: e.tensor_tensor(out=lr[:], in0=are[:], in1=dtt[:], op=ALU.mult), outs=[lr], ins=[are, dtt])
            c.op("dve", lambda e: e.tensor_tensor(out=li[:], in0=aim[:], in1=dtt[:], op=ALU.mult), outs=[li], ins=[aim, dtt])
            mag = c.sbuf([128, 64], F32, "mag", stp)
            c.op("act", lambda e: e.activation(mag[:], lr[:], AF.Exp), outs=[mag], ins=[lr])
            kf = c.sbuf([128, 64], F32, "kf", stp)
            ki = c.sbuf([128, 64], I32, "ki", stp)
            sn = c.sbuf([128, 64], F32, "sn", stp)
            cs = c.sbuf([128, 64], F32, "cs", stp)

            def sincos(arg_b):
                for (dst, shift) in ((sn, 0.0), (cs, math.pi / 2)):
                    c.op("dve", lambda e: e.tensor_scalar(out=kf[:], in0=arg_b[:], scalar1=shift, scalar2=1.0 / TWO_PI, op0=ALU.add, op1=ALU.mult),
                         outs=[kf], ins=[arg_b])
                    c.op("dve", lambda e: e.tensor_copy(ki[:], kf[:]), outs=[ki], ins=[kf])
                    c.op("dve", lambda e: e.tensor_copy(kf[:], ki[:]), outs=[kf], ins=[ki])
                    c.op("dve", lambda e: e.scalar_tensor_tensor(out=kf[:], in0=kf[:], scalar=-TWO_PI, in1=arg_b[:], op0=ALU.mult, op1=ALU.add),
                         outs=[kf], ins=[kf, arg_b])
                    c.op("dve", lambda e: e.tensor_scalar(out=kf[:], in0=kf[:], scalar1=shift, scalar2=None, op0=ALU.add), outs=[kf], ins=[kf])
                    c.op("dve", lambda e: e.tensor_scalar(out=kf[:], in0=kf[:], scalar1=math.pi, scalar2=-math.pi, op0=ALU.min, op1=ALU.max),
                         outs=[kf], ins=[kf])
                    c.op("act", lambda e: e.activation(dst[:], kf[:], AF.Sin), outs=[dst], ins=[kf])

            sincos(li)
            ar = c.sbuf([128, 64], F32, "ar", stp)
            ai = c.sbuf([128, 64], F32, "ai", stp)
            c.op("dve", lambda e: e.tensor_tensor(out=ar[:], in0=mag[:], in1=cs[:], op=ALU.mult), outs=[ar], ins=[mag, cs])
            c.op("dve", lambda e: e.tensor_tensor(out=ai[:], in0=mag[:], in1=sn[:], op=ALU.mult), outs=[ai], ins=[mag, sn])
            den = c.sbuf([128, 64], F32, "den", stp)
            t1 = c.sbuf([128, 64], F32, "t1", stp)
            t2 = c.sbuf([128, 64], F32, "t2", stp)
            cr = c.sbuf([128, 64], F32, "cr", stp)
            cim = c.sbuf([128, 64], F32, "cim", stp)
            nr = c.sbuf([128, 64], F32, "nr", stp)
            c.op("dve", lambda e: e.tensor_tensor(out=den[:], in0=are[:], in1=are[:], op=ALU.mult), outs=[den], ins=[are])
            c.op("dve", lambda e: e.tensor_tensor(out=t1[:], in0=aim[:], in1=aim[:], op=ALU.mult), outs=[t1], ins=[aim])
            c.op("dve", lambda e: e.tensor_tensor(out=den[:], in0=den[:], in1=t1[:], op=ALU.add), outs=[den], ins=[den, t1])
            c.op("dve", lambda e: e.reciprocal(den[:], den[:]), outs=[den], ins=[den])
            c.op("dve", lambda e: e.tensor_scalar(out=nr[:], in0=ar[:], scalar1=-1.0, scalar2=None, op0=ALU.add), outs=[nr], ins=[ar])
            c.op("dve", lambda e: e.tensor_tensor(out=t1[:], in0=nr[:], in1=are[:], op=ALU.mult), outs=[t1], ins=[nr, are])
            c.op("dve", lambda e: e.tensor_tensor(out=t2[:], in0=ai[:], in1=aim[:], op=ALU.mult), outs=[t2], ins=[ai, aim])
            c.op("dve", lambda e: e.tensor_tensor(out=t1[:], in0=t1[:], in1=t2[:], op=ALU.add), outs=[t1], ins=[t1, t2])
            c.op("dve", lambda e: e.tensor_tensor(out=cr[:], in0=t1[:], in1=den[:], op=ALU.mult), outs=[cr], ins=[t1, den])
            c.op("dve", lambda e: e.tensor_tensor(out=t1[:], in0=ai[:], in1=are[:], op=ALU.mult), outs=[t1], ins=[ai, are])
            c.op("dve", lambda e: e.tensor_tensor(out=t2[:], in0=nr[:], in1=aim[:], op=ALU.mult), outs=[t2], ins=[nr, aim])
            c.op("dve", lambda e: e.tensor_tensor(out=t1[:], in0=t1[:], in1=t2[:], op=ALU.subtract), outs=[t1], ins=[t1, t2])
            c.op("dve", lambda e: e.tensor_tensor(out=cim[:], in0=t1[:], in1=den[:], op=ALU.mult), outs=[cim], ins=[t1, den])
            c.op("dve", lambda e: e.memset(mt_[:], 0.0), outs=[mt_])
            c.op("dve", lambda e: e.memset(mt_[0:64, :], 1.0), outs=[mt_])
            c.op("dve", lambda e: e.memset(mb_[:], 1.0), outs=[mb_])
            c.op("dve", lambda e: e.memset(mb_[0:64, :], 0.0), outs=[mb_])
            pr = c.sbuf([128, 64], F32, "pr", stp)
            pi_ = c.sbuf([128, 64], F32, "pi_", stp)
            c.op("dve", lambda e: e.tensor_copy(pr[:], ar[:]), outs=[pr], ins=[ar])
            c.op("dve", lambda e: e.tensor_copy(pi_[:], ai[:]), outs=[pi_], ins=[ai])
            for k in range(K):
                c.op("dve", lambda e, k=k: e.tensor_scalar(out=t1[:], in0=pi_[:], scalar1=mb_[:, 0:1], scalar2=-1.0, op0=ALU.mult, op1=ALU.mult),
                     outs=[t1], ins=[pi_, mb_])
                c.op("dve", lambda e, k=k: e.scalar_tensor_tensor(out=C1[:, k, :], in0=pr[:], scalar=mt_[:, 0:1], in1=t1[:], op0=ALU.mult, op1=ALU.add),
                     outs=[C1], ins=[pr, mt_, t1])
                c.op("dve", lambda e, k=k: e.tensor_scalar(out=t1[:], in0=pr[:], scalar1=mb_[:, 0:1], scalar2=None, op0=ALU.mult),
                     outs=[t1], ins=[pr, mb_])
                c.op("dve", lambda e, k=k: e.scalar_tensor_tensor(out=C2[:, k, :], in0=pi_[:], scalar=mt_[:, 0:1], in1=t1[:], op0=ALU.mult, op1=ALU.add),
                     outs=[C2], ins=[pi_, mt_, t1])
                if k < K - 1:
                    c.op("dve", lambda e: e.tensor_tensor(out=t1[:], in0=pr[:], in1=pr[:], op=ALU.mult), outs=[t1], ins=[pr])
                    c.op("dve", lambda e: e.tensor_tensor(out=t2[:], in0=pi_[:], in1=pi_[:], op=ALU.mult), outs=[t2], ins=[pi_])
                    c.op("dve", lambda e: e.scalar_tensor_tensor(out=pi_[:], in0=pr[:], scalar=2.0, in1=pi_[:], op0=ALU.mult, op1=ALU.mult),
                         outs=[pi_], ins=[pr, pi_])
                    c.op("dve", lambda e: e.tensor_tensor(out=pr[:], in0=t1[:], in1=t2[:], op=ALU.subtract), outs=[pr], ins=[t1, t2])
            c.op("dve", lambda e: e.tensor_tensor(out=J[:], in0=ident[:, 0:64], in1=ident[:, 64:128], op=ALU.add), outs=[J], ins=[ident])
            bre = c.sbuf([64, 64, 16], F32, "bre", stp)
            bim = c.sbuf([64, 64, 16], F32, "bim", stp)
            for d in range(2):
                c.dma("sp", bre[:, d * 32:(d + 1) * 32, :], W["s5_b_re"][0, d].rearrange("g p c -> p g c"), outs=[bre], sb=bre)
                c.dma("sp", bim[:, d * 32:(d + 1) * 32, :], W["s5_b_im"][0, d].rearrange("g p c -> p g c"), outs=[bim], sb=bim)
            bbr = c.sbuf([64, 64, 16], F32, "bbr", stp)
            bbi = c.sbuf([64, 64, 16], F32, "bbi", stp)
            tb = c.sbuf([64, 64, 16], F32, "tb", stp)
            crb = cr[0:64, :].unsqueeze(2).broadcast_to([64, 64, 16])
            cib = cim[0:64, :].unsqueeze(2).broadcast_to([64, 64, 16])
            c.op("dve", lambda e: e.tensor_tensor(out=bbr[:], in0=bre[:], in1=crb, op=ALU.mult), outs=[bbr], ins=[bre, cr])
            c.op("dve", lambda e: e.tensor_tensor(out=tb[:], in0=bim[:], in1=cib, op=ALU.mult), outs=[tb], ins=[bim, cim])
            c.op("dve", lambda e: e.tensor_tensor(out=bbr[:], in0=bbr[:], in1=tb[:], op=ALU.subtract), outs=[bbr], ins=[bbr, tb])
            c.op("dve", lambda e: e.tensor_tensor(out=bbi[:], in0=bim[:], in1=crb, op=ALU.mult), outs=[bbi], ins=[bim, cr])
            c.op("dve", lambda e: e.tensor_tensor(out=tb[:], in0=bre[:], in1=cib, op=ALU.mult), outs=[tb], ins=[bre, cim])
            c.op("dve", lambda e: e.tensor_tensor(out=bbi[:], in0=bbi[:], in1=tb[:], op=ALU.add), outs=[bbi], ins=[bbi, tb])
            for dg0 in range(0, 64, 4):
                ps = c.next_ps()
                for q in range(4):
                    dg = dg0 + q
                    c.op("pe", lambda e, dg=dg, q=q: e.transpose(ps[0:16, q * 128:q * 128 + 64], bbr[:, dg, :], ident[0:64, 0:64]),
                         outs=[ps], ins=[bbr, ident])
                    c.op("pe", lambda e, dg=dg, q=q: e.transpose(ps[0:16, q * 128 + 64:q * 128 + 128], bbi[:, dg, :], ident[0:64, 0:64]),
                         outs=[ps], ins=[bbi, ident])
                c.op("act", lambda e, dg0=dg0: e.copy(BT[:, dg0:dg0 + 4, :], ps[0:16, :].rearrange("p (q s) -> p q s", q=4)), outs=[BT], ins=[ps])
            Cn = c.sbuf([16, 64, 128], F32, "Cn", stp)
            for d in range(2):
                c.dma("sp", Cn[:, d * 32:(d + 1) * 32, 0:64], W["s5_c_re"][0, d].rearrange("g c p -> c g p"), outs=[Cn], sb=Cn)
                c.dma("sp", Cn[:, d * 32:(d + 1) * 32, 64:128], W["s5_c_im"][0, d].rearrange("g c p -> c g p"), outs=[Cn], sb=Cn)
            c.op("act", lambda e: e.mul(Cn[:, :, 64:128], Cn[:, :, 64:128], -1.0), outs=[Cn], ins=[Cn])
            for dg0 in range(0, 64, 32):
                ps = c.next_ps()
                for q in range(32):
                    c.op("pe", lambda e, q=q, dg0=dg0: e.transpose(ps[:, q * 16:(q + 1) * 16], Cn[:, dg0 + q, :], ident[0:16, 0:16]),
                         outs=[ps], ins=[Cn, ident])
                c.op("act", lambda e, dg0=dg0: e.copy(CT[:, dg0:dg0 + 32, :], ps[:].rearrange("p (q s) -> p q s", q=32)), outs=[CT], ins=[ps])
            c.dma("sp", Dv[:], W["s5_d"][0, :].rearrange("(g c) -> c g", c=16), outs=[Dv], sb=Dv, allow_slow_non_contiguous=True)
            if s["h0"]:
                h0r = c.sbuf([128, 64], F32, "h0r", stp)
                h0s = c.sbuf([128, 64], F32, "h0s", stp)
                c.dma("sp", h0r[0:64, :], I["h0re"].ap().rearrange("d g p -> p (d g)"), outs=[h0r], sb=h0r, allow_slow_non_contiguous=True)
                c.dma("sp", h0r[64:128, :], I["h0im"].ap().rearrange("d g p -> p (d g)"), outs=[h0r], sb=h0r, allow_slow_non_contiguous=True)
                c.dma("sp", h0s[0:64, :], I["h0im"].ap().rearrange("d g p -> p (d g)"), outs=[h0s], sb=h0s, allow_slow_non_contiguous=True)
                c.dma("sp", h0s[64:128, :], I["h0re"].ap().rearrange("d g p -> p (d g)"), outs=[h0s], sb=h0s, allow_slow_non_contiguous=True)
                c.op("dve", lambda e: e.tensor_tensor(out=hh[:], in0=ar[:], in1=h0r[:], op=ALU.mult), outs=[hh], ins=[ar, h0r])
                c.op("dve", lambda e: e.tensor_tensor(out=t1[:], in0=ai[:], in1=h0s[:], op=ALU.mult), outs=[t1], ins=[ai, h0s])
                c.op("dve", lambda e: e.tensor_scalar(out=t2[:], in0=t1[:], scalar1=mb_[:, 0:1], scalar2=None, op0=ALU.mult), outs=[t2], ins=[t1, mb_])
                c.op("dve", lambda e: e.tensor_tensor(out=hh[:], in0=hh[:], in1=t2[:], op=ALU.add), outs=[hh], ins=[hh, t2])
                c.op("dve", lambda e: e.tensor_scalar(out=t2[:], in0=t1[:], scalar1=mt_[:, 0:1], scalar2=None, op0=ALU.mult), outs=[t2], ins=[t1, mt_])
                c.op("dve", lambda e: e.tensor_tensor(out=hh[:], in0=hh[:], in1=t2[:], op=ALU.subtract), outs=[hh], ins=[hh, t2])
            if not s["h0"]:
                pass
            c.barrier()
            stp.close()
            ug = [c.sbuf([16, T], F32, f"ug{i}", st) for i in range(2)]
            ugb = c.sbuf([16, T], BF16, "ugb", st)
            sst = [c.sbuf([128, T], F32, f"sst{d}", st) for d in range(2)]
            sbb = [[c.sbuf([128, T], BF16, f"sbb{d}{i}", st) for i in range(2)] for d in range(2)]
            hfin = [c.sbuf([128, T], BF16, f"hfin{i}", st) for i in range(2)]
            Ms = [[c.sbuf([128, K, 128], BF16, f"Ms{d}{i}", st) for i in range(2)] for d in range(2)]
            yg = [c.sbuf([16, T], F32, f"yg{i}", st) for i in range(2)]
            CW = 512 if L >= 512 else L
            PS_STATE = False
            for g in (range(32) if PS_STATE else ()):
                if g == 0:
                    saved_psl5 = c.psl
                    c.psl = saved_psl5[:3]
                    SB = [[saved_psl5[3], saved_psl5[4]], [saved_psl5[5], saved_psl5[6]]]
                u = ug[g % 2]
                c.dma("sp", u[:], s["zT"][g * 16:(g + 1) * 16, :], outs=[u], sb=u)
                c.op("act", lambda e: e.copy(ugb[:], u[:]), outs=[ugb], ins=[u])
                Md = [Ms[d][g % 2] for d in range(2)]
                cur = [sbb[d][0] for d in range(2)]
                for d in range(2):
                    dg = d * 32 + g
                    M = Md[d]
                    for k in range(K):
                        c.op("pool", lambda e, k=k, dg=dg, M=M: e.tensor_scalar(out=M[:, k, 0:64], in0=J[:], scalar1=C1[:, k, dg:dg + 1], scalar2=None, op0=ALU.mult),
                             outs=[M], ins=[J, C1])
                        c.op("pool", lambda e, k=k, dg=dg, M=M: e.tensor_scalar(out=M[:, k, 64:128], in0=J[:], scalar1=C2[:, k, dg:dg + 1], scalar2=None, op0=ALU.mult),
                             outs=[M], ins=[J, C2])
                for d in range(2):
                    dg = d * 32 + g
                    for bk in range(2):
                        c.op("pe", lambda e, bk=bk, dg=dg, d=d: e.matmul(SB[d][bk][:], BT[:, dg, :], ugb[:, bk * 512:(bk + 1) * 512], start=True, stop=True),
                             outs=[SB[d][bk]], ins=[BT, ugb])
                        c.op("act", lambda e, bk=bk, d=d: e.copy(cur[d][:, bk * 512:(bk + 1) * 512], SB[d][bk][:]), outs=[cur[d]], ins=[SB[d][bk]], nosame=True)
                for k in range(K):
                    sh = 1 << k
                    last = (k == K - 1)
                    for d in range(2):
                        M = Md[d]
                        lo, hi = (sh, L) if d == 0 else (0, L - sh)
                        off = -sh if d == 0 else sh
                        for sq_ in range(nseq):
                            base = sq_ * L
                            pb_ = SB[d][sq_ // 2]
                            o0 = (sq_ % 2) * 256
                            c.op("pe", lambda e, k=k, M=M, cu=cur[d], pb_=pb_, o0=o0, base=base, lo=lo, hi=hi, off=off: e.matmul(
                                pb_[:, o0 + lo:o0 + hi], M[:, k, :], cu[:, base + lo + off:base + hi + off], start=False, stop=True, skip_group_check=True),
                                outs=[pb_], ins=[M, cur[d]], nosame=True)
                        nxt = hfin[d] if last else sbb[d][(k + 1) % 2]
                        for bk in range(2):
                            c.op("act", lambda e, bk=bk, d=d, nxt=nxt: e.copy(nxt[:, bk * 512:(bk + 1) * 512], SB[d][bk][:]), outs=[nxt], ins=[SB[d][bk]], nosame=True)
                        cur[d] = nxt
                if stS is not None:
                    for d in range(2):
                        dg = d * 32 + g
                        for sq_ in range(nseq):
                            col = (sq_ % 2) * 256 + (L - 1 if d == 0 else 0)
                            c.op("dve", lambda e, sq_=sq_, col=col, dg=dg, d=d: e.tensor_copy(stS[:, sq_, dg:dg + 1], SB[d][sq_ // 2][:, col:col + 1]),
                                 outs=[stS], ins=[SB[d][sq_ // 2]], nosame=True)
                y = yg[g % 2]
                for t0 in range(0, T, 512):
                    ps = c.next_ps()
                    c.op("pe", lambda e, t0=t0, g=g: e.matmul(ps[0:16, :], CT[:, g, :], hfin[0][:, t0:t0 + 512], start=True, stop=False),
                         outs=[ps], ins=[CT, hfin[0]])
                    c.op("pe", lambda e, t0=t0, g=g: e.matmul(ps[0:16, :], CT[:, 32 + g, :], hfin[1][:, t0:t0 + 512], start=False, stop=True),
                         outs=[ps], ins=[CT, hfin[1]])
                    c.op("dve", lambda e, t0=t0, g=g: e.scalar_tensor_tensor(out=y[:, t0:t0 + 512], in0=u[:, t0:t0 + 512], scalar=Dv[:, g:g + 1],
                                                                             in1=ps[0:16, :], op0=ALU.mult, op1=ALU.add),
                         outs=[y], ins=[u, Dv, ps])
                c.dma("sp", s["ys5T"][g * 16:(g + 1) * 16, :], y[:], ins=[y], sb=y)
                if g == 31:
                    c.barrier()
                    c.psl = saved_psl5
            for g in (() if PS_STATE else range(32)):
                u = ug[g % 2]
                c.dma("sp", u[:], s["zT"][g * 16:(g + 1) * 16, :], outs=[u], sb=u)
                c.op("act", lambda e: e.copy(ugb[:], u[:]), outs=[ugb], ins=[u])
                Md = [Ms[d][g % 2] for d in range(2)]
                cur = [sbb[d][0] for d in range(2)]
                for d in range(2):
                    dg = d * 32 + g
                    M = Md[d]
                    for k in range(K):
                        eng = "pool"
                        c.op(eng, lambda e, k=k, dg=dg, M=M: e.tensor_scalar(out=M[:, k, 0:64], in0=J[:], scalar1=C1[:, k, dg:dg + 1], scalar2=None, op0=ALU.mult),
                             outs=[M], ins=[J, C1])
                        c.op(eng, lambda e, k=k, dg=dg, M=M: e.tensor_scalar(out=M[:, k, 64:128], in0=J[:], scalar1=C2[:, k, dg:dg + 1], scalar2=None, op0=ALU.mult),
                             outs=[M], ins=[J, C2])
                for d in range(2):
                    dg = d * 32 + g
                    for t0 in range(0, T, 512):
                        ps = c.next_ps()
                        c.op("pe", lambda e, t0=t0, dg=dg: e.matmul(ps[:], BT[:, dg, :], ugb[:, t0:t0 + 512], start=True, stop=True),
                             outs=[ps], ins=[BT, ugb])
                        c.op("dve", lambda e, t0=t0, d=d: e.tensor_copy(sst[d][:, t0:t0 + 512], ps[:]), outs=[sst[d]], ins=[ps], nosame=True)
                    if hh is not None:
                        col = 0 if d == 0 else T - 1
                        c.op("dve", lambda e, col=col, dg=dg, d=d: e.tensor_tensor(out=sst[d][:, col:col + 1], in0=sst[d][:, col:col + 1], in1=hh[:, dg:dg + 1], op=ALU.add),
                             outs=[sst[d]], ins=[sst[d], hh])
                    c.op("act", lambda e, d=d: e.copy(cur[d][:], sst[d][:]), outs=[cur[d]], ins=[sst[d]])
                for k in range(K):
                    sh = 1 << k
                    last = (k == K - 1)
                    for d in range(2):
                        M = Md[d]
                        if nseq > 1 and nseq % 2 == 0 and 2 * L <= 512:
                            lo, hi = (sh, L) if d == 0 else (0, L - sh)
                            off = -sh if d == 0 else sh
                            w_ = hi - lo
                            cu3 = cur[d][:].rearrange("p (s l) -> p s l", l=L)
                            ss3 = sst[d][:].rearrange("p (s l) -> p s l", l=L)
                            for sq0 in range(0, nseq, 2):
                                ps = c.next_ps()
                                po = ps[:, 0:2 * w_].rearrange("p (s l) -> p s l", s=2)
                                c.op("pe", lambda e, k=k, M=M, sq0=sq0, po=po, cu3=cu3, lo=lo, hi=hi, off=off: e.matmul(
                                    po, M[:, k, :], cu3[:, sq0:sq0 + 2, lo + off:hi + off], start=True, stop=True),
                                    outs=[ps], ins=[M, cur[d]])
                                c.op("dve", lambda e, sq0=sq0, po=po, ss3=ss3, lo=lo, hi=hi, d=d: e.tensor_tensor(
                                    out=ss3[:, sq0:sq0 + 2, lo:hi], in0=ss3[:, sq0:sq0 + 2, lo:hi], in1=po, op=ALU.add),
                                    outs=[sst[d]], ins=[sst[d], ps], nosame=True)
                            nxt = hfin[d] if last else sbb[d][(k + 1) % 2]
                            c.op("act", lambda e, nxt=nxt, d=d: e.copy(nxt[:], sst[d][:]), outs=[nxt], ins=[sst[d]])
                            cur[d] = nxt
                            continue
                        for sq_ in range(nseq):
                            base = sq_ * L
                            lo, hi = (sh, L) if d == 0 else (0, L - sh)
                            off = -sh if d == 0 else sh
                            for a0 in range(lo, hi, CW):
                                a1 = min(a0 + CW, hi)
                                ps = c.next_ps()
                                c.op("pe", lambda e, a0=a0, a1=a1, base=base, off=off, k=k, M=M, cu=cur[d]: e.matmul(
                                    ps[:, 0:a1 - a0], M[:, k, :], cu[:, base + a0 + off:base + a1 + off], start=True, stop=True),
                                    outs=[ps], ins=[M, cur[d]])
                                c.op("dve", lambda e, a0=a0, a1=a1, base=base, d=d: e.tensor_tensor(
                                    out=sst[d][:, base + a0:base + a1], in0=sst[d][:, base + a0:base + a1], in1=ps[:, 0:a1 - a0], op=ALU.add),
                                    outs=[sst[d]], ins=[sst[d], ps], nosame=True)
                        nxt = hfin[d] if last else sbb[d][(k + 1) % 2]
                        c.op("act", lambda e, nxt=nxt, d=d: e.copy(nxt[:], sst[d][:]), outs=[nxt], ins=[sst[d]])
                        cur[d] = nxt
                if stS is not None:
                    for d in range(2):
                        dg = d * 32 + g
                        for sq_ in range(nseq):
                            col = sq_ * L + (L - 1 if d == 0 else 0)
                            c.op("pool", lambda e, sq_=sq_, col=col, dg=dg, d=d: e.tensor_copy(stS[:, sq_, dg:dg + 1], sst[d][:, col:col + 1]),
                                 outs=[stS], ins=[sst[d]])
                y = yg[g % 2]
                for t0 in range(0, T, 512):
                    ps = c.next_ps()
                    c.op("pe", lambda e, t0=t0, g=g: e.matmul(ps[0:16, :], CT[:, g, :], hfin[0][:, t0:t0 + 512], start=True, stop=False),
                         outs=[ps], ins=[CT, hfin[0]])
                    c.op("pe", lambda e, t0=t0, g=g: e.matmul(ps[0:16, :], CT[:, 32 + g, :], hfin[1][:, t0:t0 + 512], start=False, stop=True),
                         outs=[ps], ins=[CT, hfin[1]])
                    c.op("dve", lambda e, t0=t0, g=g: e.scalar_tensor_tensor(out=y[:, t0:t0 + 512], in0=u[:, t0:t0 + 512], scalar=Dv[:, g:g + 1],
                                                                             in1=ps[0:16, :], op0=ALU.mult, op1=ALU.add),
                         outs=[y], ins=[u, Dv, ps])
                c.dma("sp", s["ys5T"][g * 16:(g + 1) * 16, :], y[:], ins=[y], sb=y)
            if stS is not None:
                for sq_ in range(nseq):
                    c.dma("sp", O["s5re"][sq_].rearrange("d g p -> p (d g)"), stS[0:64, sq_, :], ins=[stS], sb=stS, allow_slow_non_contiguous=True)
                    c.dma("sp", O["s5im"][sq_].rearrange("d g p -> p (d g)"), stS[64:128, sq_, :], ins=[stS], sb=stS, allow_slow_non_contiguous=True)
            c.barrier()

    def stage_glu(s):
        T = s["T"]
        with contextlib.ExitStack() as st:
            Wg = c.sbuf([128, 4, 512], F32, "Wg", st)
            for k in range(4):
                c.dma("sp", Wg[:, k, :], W["s5_w_glu"][0, k * 128:(k + 1) * 128, :], outs=[Wg], sb=Wg)
            bg = c.sbuf([128, 4], F32, "bg", st)
            fm_load("sp", bg, bg[:], W["s5_b_glu"][0, :])
            yts = [c.sbuf([128, 4, 512], F32, f"yt{i}", st) for i in range(2)]
            zs = c.sbuf([128, 4, 512], F32, "zs", st)
            tmp = c.sbuf([128, 4, 512], F32, "tmpg", st)
            outs_ = [c.sbuf([128, 4, 512], BF16, f"og{i}", st) for i in range(2)]
            for t in range(T // 512):
                yt, ot = yts[t % 2], outs_[t % 2]
                c.dma("sp", yt[:], s["ys5T"][:, t * 512:(t + 1) * 512].rearrange("(k p) t -> p k t", p=128), outs=[yt], sb=yt)
                gelu(zs[:], zs, yt[:], yt, tmp[:], tmp)
                for m in range(4):
                    ps = c.next_ps()
                    for k in range(4):
                        c.op("pe", lambda e, k=k, m=m: e.matmul(ps[:], Wg[:, k, m * 128:(m + 1) * 128], zs[:, k, :], start=(k == 0), stop=(k == 3)),
                             outs=[ps], ins=[Wg, zs])
                    c.op("act", lambda e, m=m: e.activation(tmp[:, m, :], ps[:], AF.Sigmoid, bias=bg[:, m:m + 1]), outs=[tmp], ins=[ps, bg])
                c.op("dve", lambda e: e.tensor_tensor(out=ot[:], in0=zs[:], in1=tmp[:], op=ALU.mult), outs=[ot], ins=[zs, tmp])
                c.dma("sp", s["mixT"][0:512, t * 512:(t + 1) * 512].rearrange("(k p) t -> p k t", p=128), ot[:], ins=[ot], sb=ot)
            c.barrier()

    def stage_lru(s):
        T, nseq, L, row = s["T"], s["nseq"], s["L"], s["row"]
        nrow = T // row
        with contextlib.ExitStack() as st:
            cw = c.sbuf([128, 4, 4], F32, "cw", st)
            for k in range(4):
                c.dma("sp", cw[:, :, k], W["lru_conv_w"][0, k, :].rearrange("(c p) -> p c", p=128), outs=[cw], sb=cw, allow_slow_non_contiguous=True)
            cb = c.sbuf([128, 4], F32, "cb", st)
            fm_load("sp", cb, cb[:], W["lru_conv_b"][0, :])
            ba = c.sbuf([128, 2, 4], F32, "ba", st)
            bx = c.sbuf([128, 2, 4], F32, "bx", st)
            lam = c.sbuf([128, 2, 4], F32, "lam", st)
            for d in range(2):
                fm_load("sp", ba, ba[:, d, :], W["lru_b_a"][0, d, :])
                fm_load("sp", bx, bx[:, d, :], W["lru_b_x"][0, d, :])
                fm_load("sp", lam, lam[:, d, :], W["lru_lambda"][0, d, :])
            c.op("act", lambda e: e.activation(lam[:], lam[:], AF.Exp, scale=-1.0), outs=[lam], ins=[lam])
            c.op("act", lambda e: e.activation(lam[:], lam[:], AF.Ln, bias=1.0), outs=[lam], ins=[lam])
            c.op("act", lambda e: e.mul(lam[:], lam[:], -8.0), outs=[lam], ins=[lam])
            h0l = None
            if s["h0"]:
                h0l = c.sbuf([128, 2, 4], F32, "h0l", st)
                for d in range(2):
                    fm_load("sp", h0l, h0l[:, d, :], I["h0lru"][d, :])
            stL = None
            if not s["h0"]:
                stL = c.sbuf([128, nseq, 2, 4], F32, "stL", st)
            Wa = c.sbuf([128, 2, 4, 128], F32, "Wa", st)
            Wx = c.sbuf([128, 2, 4, 128], F32, "Wx", st)
            c.op("dve", lambda e: e.memset(Wa[:], 0.0), outs=[Wa])
            c.op("dve", lambda e: e.memset(Wx[:], 0.0), outs=[Wx])
            for d in range(2):
                for cc in range(4):
                    for hh_ in range(2):
                        c.dma("sp", Wa[hh_ * 64:(hh_ + 1) * 64, d, cc, hh_ * 64:(hh_ + 1) * 64], W["lru_w_a"][0, d, 2 * cc + hh_], outs=[Wa], sb=Wa)
                        c.dma("sp", Wx[hh_ * 64:(hh_ + 1) * 64, d, cc, hh_ * 64:(hh_ + 1) * 64], W["lru_w_x"][0, d, 2 * cc + hh_], outs=[Wx], sb=Wx)
            xr = c.sbuf([128, T], F32, "xr", st)
            xg = c.sbuf([128, T], F32, "xg", st)
            xb = c.sbuf([128, T], F32, "xb", st)
            ra = c.sbuf([128, T], F32, "ra", st)
            ib = c.sbuf([128, T], F32, "ib", st)
            hf = c.sbuf([128, T], F32, "hf", st)
            hb = c.sbuf([128, T], F32, "hb", st)
            lob = c.sbuf([128, T], BF16, "lob", st)
            for cc in range(4):
                c.dma("sp", xr[:], s["zT"][512 + cc * 128:512 + (cc + 1) * 128, :], outs=[xr], sb=xr)
                c.dma("sp", xg[:], s["zT"][1024 + cc * 128:1024 + (cc + 1) * 128, :], outs=[xg], sb=xg)
                c.op("dve", lambda e, cc=cc: e.tensor_scalar(out=xb[:], in0=xr[:], scalar1=cw[:, cc, 2:3], scalar2=cb[:, cc:cc + 1], op0=ALU.mult, op1=ALU.add),
                     outs=[xb], ins=[xr, cw, cb])
                xr3 = xr[:].rearrange("p (r w) -> p r w", w=row)
                xb3 = xb[:].rearrange("p (r w) -> p r w", w=row)
                for k in (0, 1, 3):
                    o = k - 2
                    dlo, dhi = max(0, -o), row - max(0, o)
                    c.op("dve", lambda e, cc=cc, k=k, o=o, dlo=dlo, dhi=dhi: e.scalar_tensor_tensor(
                        out=xb3[:, :, dlo:dhi], in0=xr3[:, :, dlo + o:dhi + o], scalar=cw[:, cc, k:k + 1], in1=xb3[:, :, dlo:dhi],
                        op0=ALU.mult, op1=ALU.add), outs=[xb], ins=[xr, cw, xb])
                for d in range(2):
                    h = hf if d == 0 else hb
                    for t0 in range(0, T, 512):
                        ps = c.next_ps()
                        c.op("pe", lambda e, t0=t0, d=d, cc=cc: e.matmul(ps[:], Wa[:, d, cc, :], xb[:, t0:t0 + 512], start=True, stop=True),
                             outs=[ps], ins=[Wa, xb])
                        c.op("act", lambda e, t0=t0, d=d, cc=cc: e.activation(ra[:, t0:t0 + 512], ps[:], AF.Sigmoid, bias=ba[:, d, cc:cc + 1]),
                             outs=[ra], ins=[ps, ba], nosame=True)
                        ps2 = c.next_ps()
                        c.op("pe", lambda e, t0=t0, d=d, cc=cc: e.matmul(ps2[:], Wx[:, d, cc, :], xb[:, t0:t0 + 512], start=True, stop=True),
                             outs=[ps2], ins=[Wx, xb])
                        c.op("act", lambda e, t0=t0, d=d, cc=cc: e.activation(ib[:, t0:t0 + 512], ps2[:], AF.Sigmoid, bias=bx[:, d, cc:cc + 1]),
                             outs=[ib], ins=[ps2, bx], nosame=True)
                    c.op("act", lambda e, d=d, cc=cc: e.activation(ra[:], ra[:], AF.Exp, scale=lam[:, d, cc:cc + 1]), outs=[ra], ins=[ra, lam])
                    c.op("dve", lambda e: e.tensor_tensor(out=h[:], in0=ra[:], in1=ra[:], op=ALU.mult), outs=[h], ins=[ra])
                    c.op("dve", lambda e: e.tensor_scalar(out=h[:], in0=h[:], scalar1=-1.0, scalar2=1.0, op0=ALU.mult, op1=ALU.add), outs=[h], ins=[h])
                    c.op("dve", lambda e: e.tensor_scalar(out=h[:], in0=h[:], scalar1=0.0, scalar2=None, op0=ALU.max), outs=[h], ins=[h])
                    c.op("act", lambda e: e.activation(h[:], h[:], AF.Sqrt), outs=[h], ins=[h])
                    c.op("dve", lambda e: e.tensor_tensor(out=ib[:], in0=ib[:], in1=h[:], op=ALU.mult), outs=[ib], ins=[ib, h])
                    c.op("dve", lambda e: e.tensor_tensor(out=ib[:], in0=ib[:], in1=xb[:], op=ALU.mult), outs=[ib], ins=[ib, xb])
                    if h0l is not None:
                        col = 0 if d == 0 else T - 1
                        c.op("dve", lambda e, col=col, d=d, cc=cc: e.scalar_tensor_tensor(
                            out=ib[:, col:col + 1], in0=ra[:, col:col + 1], scalar=h0l[:, d, cc:cc + 1], in1=ib[:, col:col + 1],
                            op0=ALU.mult, op1=ALU.add), outs=[ib], ins=[ra, h0l, ib])
                    for sq_ in range(nseq):
                        sl = slice(sq_ * L, (sq_ + 1) * L)
                        if d == 0:
                            c.op("dve", lambda e, sl=sl: e.tensor_tensor_scan(h[:, sl], ra[:, sl], ib[:, sl], 0.0, ALU.mult, ALU.add),
                                 outs=[h], ins=[ra, ib])
                        else:
                            c.op("dve", lambda e, sl=sl: e.tensor_tensor_scan(h[:, sl][:, ::-1], ra[:, sl][:, ::-1], ib[:, sl][:, ::-1], 0.0, ALU.mult, ALU.add),
                                 outs=[h], ins=[ra, ib])
                        if stL is not None:
                            col = sq_ * L + (L - 1 if d == 0 else 0)
                            c.op("dve", lambda e, sq_=sq_, d=d, cc=cc, col=col: e.tensor_copy(stL[:, sq_, d, cc:cc + 1], h[:, col:col + 1]),
                                 outs=[stL], ins=[h])
                c.op("dve", lambda e: e.tensor_tensor(out=hf[:], in0=hf[:], in1=hb[:], op=ALU.add), outs=[hf], ins=[hf, hb])
                gelu(ib[:], ib, xg[:], xg, ra[:], ra)
                c.op("dve", lambda e: e.tensor_tensor(out=lob[:], in0=hf[:], in1=ib[:], op=ALU.mult), outs=[lob], ins=[hf, ib])
                c.dma("sp", s["mixT"][512 + cc * 128:512 + (cc + 1) * 128, :], lob[:], ins=[lob], sb=lob)
            if stL is not None:
                for sq_ in range(nseq):
                    for d in range(2):
                        c.dma("sp", O["lru"][sq_, d, :].rearrange("(c p) -> p c", p=128), stL[:, sq_, d, :], ins=[stL], sb=stL,
                              allow_slow_non_contiguous=True)
            c.barrier()

    def wrap_pi(buf):
        with contextlib.ExitStack() as st2:
            m = c.sbuf(list(buf.t.shape), F32, "wrapm", st2)
            c.op("dve", lambda e: e.tensor_single_scalar(m[:], buf[:], math.pi, ALU.is_gt), outs=[m], ins=[buf])
            c.op("dve", lambda e: e.scalar_tensor_tensor(out=buf[:], in0=m[:], scalar=-TWO_PI, in1=buf[:], op0=ALU.mult, op1=ALU.add),
                 outs=[buf], ins=[m, buf])
            c.op("dve", lambda e: e.tensor_single_scalar(m[:], buf[:], -math.pi, ALU.is_lt), outs=[m], ins=[buf])
            c.op("dve", lambda e: e.scalar_tensor_tensor(out=buf[:], in0=m[:], scalar=TWO_PI, in1=buf[:], op0=ALU.mult, op1=ALU.add),
                 outs=[buf], ins=[m, buf])
            c.op("dve", lambda e: e.tensor_scalar(out=buf[:], in0=buf[:], scalar1=math.pi, scalar2=-math.pi, op0=ALU.min, op1=ALU.max),
                 outs=[buf], ins=[buf])
            c.barrier()

    def stage_hyfilt(s):
        L = s["L"]
        nj = L // 128
        rn_all = c.sbuf([128, 8], F32, "rn_all")
        s["rn"] = rn_all
        with contextlib.ExitStack() as st:
            emb = c.sbuf([33, L], F32, "emb", st)
            c.dma("sp", emb[:], I[f"emb{L}"].ap(), outs=[emb], sb=emb)
            w1 = c.sbuf([33, 64], F32, "w1", st)
            w2 = c.sbuf([64, 64], F32, "w2", st)
            w3 = c.sbuf([64, 2048], F32, "w3", st)
            c.dma("sp", w1[:], W["hy_w1"][0], outs=[w1], sb=w1)
            c.dma("sp", w2[:], W["hy_w2"][0], outs=[w2], sb=w2)
            c.dma("sp", w3[:], W["hy_w3"][0], outs=[w3], sb=w3)
            vec = c.sbuf([64, 4], F32, "hvec", st)
            for i, nm in enumerate(["hy_b1", "hy_freq1", "hy_b2", "hy_freq2"]):
                c.dma("sp", vec[:, i:i + 1], W[nm][0, :].rearrange("(p o) -> p o", o=1), outs=[vec], sb=vec, allow_slow_non_contiguous=True)
            fb = c.sbuf([64, 2], F32, "fb", st)
            c.op("dve", lambda e: e.tensor_tensor(out=fb[:, 0:1], in0=vec[:, 0:1], in1=vec[:, 1:2], op=ALU.mult), outs=[fb], ins=[vec])
            c.op("dve", lambda e: e.tensor_tensor(out=fb[:, 1:2], in0=vec[:, 2:3], in1=vec[:, 3:4], op=ALU.mult), outs=[fb], ins=[vec])
            z1 = c.sbuf([64, L], F32, "z1", st)
            z2 = c.sbuf([64, L], F32, "z2", st)
            CWL = min(512, L)
            for t0 in range(0, L, CWL):
                ps = c.next_ps()
                c.op("pe", lambda e, t0=t0: e.matmul(ps[0:64, 0:CWL], w1[:], emb[:, t0:t0 + CWL], start=True, stop=True), outs=[ps], ins=[w1, emb])
                c.op("dve", lambda e, t0=t0: e.tensor_scalar(out=z1[:, t0:t0 + CWL], in0=ps[0:64, 0:CWL], scalar1=vec[:, 1:2], scalar2=fb[:, 0:1], op0=ALU.mult, op1=ALU.add),
                     outs=[z1], ins=[ps, vec, fb])
            wrap_pi(z1)
            c.op("act", lambda e: e.activation(z1[:], z1[:], AF.Sin), outs=[z1], ins=[z1])
            for t0 in range(0, L, CWL):
                ps = c.next_ps()
                c.op("pe", lambda e, t0=t0: e.matmul(ps[0:64, 0:CWL], w2[:], z1[:, t0:t0 + CWL], start=True, stop=True), outs=[ps], ins=[w2, z1])
                c.op("dve", lambda e, t0=t0: e.tensor_scalar(out=z2[:, t0:t0 + CWL], in0=ps[0:64, 0:CWL], scalar1=vec[:, 3:4], scalar2=fb[:, 1:2], op0=ALU.mult, op1=ALU.add),
                     outs=[z2], ins=[ps, vec, fb])
            wrap_pi(z2)
            c.op("act", lambda e: e.activation(z2[:], z2[:], AF.Sin), outs=[z2], ins=[z2])
            dec = c.sbuf([128, 2048], F32, "dec", st)
            c.dma("sp", dec[:], W["hy_decay"][0].rearrange("a c -> (a c)").partition_broadcast(128), outs=[dec], sb=dec)
            c.op("act", lambda e: e.activation(dec[:], dec[:], AF.Abs), outs=[dec], ins=[dec])
            nt = c.sbuf([128, 32], F32, "nt", st)
            c.op("dve", lambda e: e.tensor_scalar(out=nt[:], in0=tcol[:], scalar1=-1.0 / L, scalar2=None, op0=ALU.mult), outs=[nt], ins=[tcol])
            win = c.sbuf([128, 2048], F32, "win", st)
            fl = c.sbuf([128, 2048], F32, "fl", st)
            flb = [c.sbuf([128, 2048], BF16, f"flb{i}", st) for i in range(2)]
            sqt = c.sbuf([128, 2048], F32, "sqt", st)
            accs = c.sbuf([128, 8], F32, "accs", st)
            c.op("dve", lambda e: e.memset(accs[:], 0.0), outs=[accs])
            for j in range(nj):
                for q in range(4):
                    ps = c.next_ps()
                    c.op("pe", lambda e, j=j, q=q: e.matmul(ps[:], z2[:, j * 128:(j + 1) * 128], w3[:, q * 512:(q + 1) * 512], start=True, stop=True),
                         outs=[ps], ins=[z2, w3])
                    c.op("act", lambda e, q=q: e.copy(fl[:, q * 512:(q + 1) * 512], ps[:]), outs=[fl], ins=[ps])
                c.op("act", lambda e, j=j: e.activation(win[:], dec[:], AF.Exp, scale=nt[:, j:j + 1]), outs=[win], ins=[dec, nt])
                c.op("dve", lambda e: e.scalar_tensor_tensor(out=fl[:], in0=win[:], scalar=0.05, in1=fl[:], op0=ALU.add, op1=ALU.mult),
                     outs=[fl], ins=[win, fl])
                if j == 0:
                    c.op("dve", lambda e: e.memset(fl[0:1, 1024:2048], 0.0), outs=[fl])
                c.op("dve", lambda e: e.tensor_tensor(out=sqt[:], in0=fl[:], in1=fl[:], op=ALU.mult), outs=[sqt], ins=[fl])
                ps = c.next_ps()
                for cbk in range(8):
                    c.op("pe", lambda e, cbk=cbk: e.matmul(ps[:, cbk:cbk + 1], sqt[:, cbk * 128:(cbk + 1) * 128], ones[:, 0:1], start=True, stop=False),
                         outs=[ps], ins=[sqt, ones])
                    c.op("pe", lambda e, cbk=cbk: e.matmul(ps[:, cbk:cbk + 1], sqt[:, 1024 + cbk * 128:1024 + (cbk + 1) * 128], ones[:, 0:1], start=False, stop=True),
                         outs=[ps], ins=[sqt, ones])
                c.op("dve", lambda e: e.tensor_tensor(out=accs[:], in0=accs[:], in1=ps[:, 0:8], op=ALU.add), outs=[accs], ins=[accs, ps])
                fb_ = flb[j % 2]
                c.op("pool", lambda e: e.tensor_tensor(out=fb_[:, 0:1024], in0=fl[:, 0:1024], in1=fl[:, 1024:2048], op=ALU.add), outs=[fb_], ins=[fl])
                c.op("dve", lambda e: e.tensor_tensor(out=fb_[:, 1024:2048], in0=fl[:, 0:1024], in1=fl[:, 1024:2048], op=ALU.subtract), outs=[fb_], ins=[fl])
                c.dma("sp", s["filt"][j * 128:(j + 1) * 128, :], fb_[:], ins=[fb_], sb=fb_)
            c.op("dve", lambda e: e.tensor_scalar(out=accs[:], in0=accs[:], scalar1=1e-6, scalar2=None, op0=ALU.add), outs=[accs], ins=[accs])
            c.op("act", lambda e: e.activation(accs[:], accs[:], AF.Sqrt), outs=[accs], ins=[accs])
            c.op("dve", lambda e: e.reciprocal(rn_all[:], accs[:]), outs=[rn_all], ins=[accs])
            c.barrier()

    def stage_hyena(s):
        T, nseq, L, row = s["T"], s["nseq"], s["L"], s["row"]
        nj = L // 128
        R = 2 * L
        RW = min(512, L)
        nrt2 = L // RW
        NB4 = RW // 128
        KH = min(nj, 16)
        nkh = nj // KH
        rn_all = s["rn"]
        Gd, Gid = I[f"G{L}"], I[f"Gi{L}"]
        with contextlib.ExitStack() as st:
            hw = c.sbuf([128, 24, 3], F32, "hw", st)
            for k in range(3):
                c.dma("sp", hw[:, :, k], W["hy_conv_w"][0, k, :].rearrange("(c p) -> p c", p=128), outs=[hw], sb=hw, allow_slow_non_contiguous=True)
            hcb = c.sbuf([128, 24], F32, "hcb", st)
            fm_load("sp", hcb, hcb[:], W["hy_conv_b"][0, :])
            hbias = c.sbuf([128, 8], F32, "hbias", st)
            fm_load("sp", hbias, hbias[:], W["hy_bias"][0, :])
            raw = c.sbuf([128, T], F32, "raw", st)
            bA = c.sbuf([128, T], F32, "bA", st)
            bB = c.sbuf([128, T], F32, "bB", st)
            vvb = c.sbuf([128, T], BF16, "vvb", st)
            vtm = c.sbuf([128, T // 128, 128], BF16, "vtm", st)
            ftm = c.sbuf([128, nj, 2, 128], BF16, "ftm", st)
            Gt = [c.sbuf([128, max(KH, min(R // 128, 16)), 512], BF16, f"Gt{i}", st) for i in range(3)]
            khat = c.sbuf([128, 2, 512], F32, "khat", st)
            tmpa = c.sbuf([128, 512], F32, "tmpa", st)
            tmpb = c.sbuf([128, 512], F32, "tmpb", st)
            yhat = c.sbuf([128, 2, 512], BF16, "yhat", st)
            yrm = c.sbuf([128, nseq, R // 128, 128], BF16, "yrm", st)
            ot = [c.sbuf([128, 512], F32, f"hyo{i}", st) for i in range(2)]
            otb = [c.sbuf([128, 512], BF16, f"hyob{i}", st) for i in range(2)]
            gi = 0

            def conv3(dst, chunk):
                c.op("dve", lambda e: e.tensor_scalar(out=dst[:], in0=raw[:], scalar1=hw[:, chunk, 1:2], scalar2=hcb[:, chunk:chunk + 1], op0=ALU.mult, op1=ALU.add),
                     outs=[dst], ins=[raw, hw, hcb])
                r3 = raw[:].rearrange("p (r w) -> p r w", w=row)
                d3 = dst[:].rearrange("p (r w) -> p r w", w=row)
                for k in (0, 2):
                    o = k - 1
                    dlo, dhi = max(0, -o), row - max(0, o)
                    c.op("dve", lambda e, k=k, o=o, dlo=dlo, dhi=dhi: e.scalar_tensor_tensor(
                        out=d3[:, :, dlo:dhi], in0=r3[:, :, dlo + o:dhi + o], scalar=hw[:, chunk, k:k + 1], in1=d3[:, :, dlo:dhi],
                        op0=ALU.mult, op1=ALU.add), outs=[dst], ins=[raw, hw, dst])

            for cbk in range(8):
                c.dma("sp", raw[:], s["zT"][2048 + cbk * 128:2048 + (cbk + 1) * 128, :], outs=[raw], sb=raw)
                conv3(bA, 16 + cbk)
                c.dma("sp", raw[:], s["zT"][1024 + cbk * 128:1024 + (cbk + 1) * 128, :], outs=[raw], sb=raw)
                conv3(bB, 8 + cbk)
                c.op("dve", lambda e: e.tensor_tensor(out=bA[:], in0=bA[:], in1=bB[:], op=ALU.mult), outs=[bA], ins=[bA, bB])
                c.op("act", lambda e: e.copy(vvb[:], bA[:]), outs=[vvb], ins=[bA])
                c.dma("sp", raw[:], s["zT"][cbk * 128:(cbk + 1) * 128, :], outs=[raw], sb=raw)
                conv3(bB, cbk)
                JB = min(8, T // 128)
                for j0 in range(0, T // 128, JB):
                    for jj in range(JB):
                        c.op("pe", lambda e, j0=j0, jj=jj: e.transpose(psb[:, jj * 128:(jj + 1) * 128], vvb[:, (j0 + jj) * 128:(j0 + jj + 1) * 128], identb[:]),
                             outs=[psb], ins=[vvb, identb])
                    c.op("act", lambda e, j0=j0: e.copy(vtm[:, j0:j0 + JB, :], psb[:, 0:JB * 128].rearrange("p (j s) -> p j s", j=JB)), outs=[vtm], ins=[psb])
                for a_ in range(2):
                    c.dma("sp", ftm[:, :, a_, :], s["filt"][:, a_ * 1024 + cbk * 128:a_ * 1024 + (cbk + 1) * 128].rearrange("(j p) c -> p j c", p=128),
                          outs=[ftm], sb=ftm)
                for i in range(nrt2):
                    for sq_ in range(nseq):
                        psv = [c.next_ps(), c.next_ps()]
                        psk = [c.next_ps(), c.next_ps()] if sq_ == 0 else None
                        for part in range(2):
                            r0 = part * L + i * RW
                            for kh in range(nkh):
                                if sq_ == 0:
                                    G_ = Gt[gi % 3]
                                    gi += 1
                                    c.dma("sp", G_[:, 0:KH, 0:RW], Gd[r0 // RW, kh], outs=[G_], sb=G_)
                                    s.setdefault("_gcache", {})[(part, kh)] = G_
                                else:
                                    G_ = s["_gcache"][(part, kh)]
                                for k in range(KH):
                                    tc_ = sq_ * nj + kh * KH + k
                                    first = (kh == 0 and k == 0)
                                    lastm = (kh == nkh - 1 and k == KH - 1)
                                    c.op("pe", lambda e, part=part, tc_=tc_, k=k, G_=G_, first=first, lastm=lastm: e.matmul(
                                        psv[part][:, 0:RW], vtm[:, tc_, :], G_[:, k, 0:RW], start=first, stop=lastm), outs=[psv[part]], ins=[vtm, G_])
                                    if sq_ == 0:
                                        jc = kh * KH + k
                                        c.op("pe", lambda e, part=part, jc=jc, k=k, G_=G_, first=first, lastm=lastm: e.matmul(
                                            psk[part][:, 0:RW], ftm[:, jc, part, :], G_[:, k, 0:RW], start=first, stop=lastm), outs=[psk[part]], ins=[ftm, G_])
                        if sq_ == 0:
                            for part in range(2):
                                c.op("act", lambda e, part=part: e.activation(khat[:, part, 0:RW], psk[part][:, 0:RW], AF.Copy, scale=rn_all[:, cbk:cbk + 1]),
                                     outs=[khat], ins=[psk[part], rn_all])
                        c.op("dve", lambda e: e.tensor_tensor(out=tmpa[:, 0:RW], in0=psv[0][:, 0:RW], in1=khat[:, 0, 0:RW], op=ALU.mult), outs=[tmpa], ins=[psv[0], khat])
                        c.op("dve", lambda e: e.tensor_tensor(out=tmpb[:, 0:RW], in0=psv[1][:, 0:RW], in1=khat[:, 1, 0:RW], op=ALU.mult), outs=[tmpb], ins=[psv[1], khat])
                        c.op("dve", lambda e: e.tensor_tensor(out=yhat[:, 0, 0:RW], in0=tmpa[:, 0:RW], in1=tmpb[:, 0:RW], op=ALU.subtract), outs=[yhat], ins=[tmpa, tmpb])
                        c.op("dve", lambda e: e.tensor_tensor(out=tmpa[:, 0:RW], in0=psv[0][:, 0:RW], in1=khat[:, 1, 0:RW], op=ALU.mult), outs=[tmpa], ins=[psv[0], khat])
                        c.op("dve", lambda e: e.tensor_tensor(out=tmpb[:, 0:RW], in0=psv[1][:, 0:RW], in1=khat[:, 0, 0:RW], op=ALU.mult), outs=[tmpb], ins=[psv[1], khat])
                        c.op("dve", lambda e: e.tensor_tensor(out=yhat[:, 1, 0:RW], in0=tmpa[:, 0:RW], in1=tmpb[:, 0:RW], op=ALU.add), outs=[yhat], ins=[tmpa, tmpb])
                        for part in range(2):
                            for q in range(NB4):
                                c.op("pe", lambda e, part=part, q=q: e.transpose(psb[:, (part * 4 + q) * 128:(part * 4 + q + 1) * 128], yhat[:, part, q * 128:(q + 1) * 128], identb[:]),
                                     outs=[psb], ins=[yhat, identb])
                        for part in range(2):
                            rc0 = (part * L + i * RW) // 128
                            c.op("act", lambda e, part=part, rc0=rc0, sq_=sq_: e.copy(yrm[:, sq_, rc0:rc0 + NB4, :], psb[:, part * 512:part * 512 + NB4 * 128].rearrange("p (q s) -> p q s", q=NB4)),
                                 outs=[yrm], ins=[psb])
                nrc = R // 128
                RH = min(nrc, 16)
                CWL = min(512, L)
                for tt in range(L // CWL):
                    pss_ = [c.next_ps() for _ in range(nseq)]
                    for rh in range(nrc // RH):
                        G_ = Gt[gi % 3]
                        gi += 1
                        c.dma("sp", G_[:, 0:RH, 0:CWL], Gid[tt, rh], outs=[G_], sb=G_)
                        for sq_ in range(nseq):
                            for k in range(RH):
                                rc = rh * RH + k
                                c.op("pe", lambda e, sq_=sq_, rc=rc, k=k, G_=G_: e.matmul(pss_[sq_][:, 0:CWL], yrm[:, sq_, rc, :], G_[:, k, 0:CWL],
                                                                                      start=(rc == 0), stop=(rc == nrc - 1)),
                                     outs=[pss_[sq_]], ins=[yrm, G_])
                    for sq_ in range(nseq):
                        t0 = sq_ * L + tt * CWL
                        o_ = ot[(tt * nseq + sq_) % 2]
                        c.op("dve", lambda e, t0=t0, sq_=sq_, o_=o_: e.scalar_tensor_tensor(out=o_[:, 0:CWL], in0=bA[:, t0:t0 + CWL], scalar=hbias[:, cbk:cbk + 1],
                                                                                      in1=pss_[sq_][:, 0:CWL], op0=ALU.mult, op1=ALU.add),
                             outs=[o_], ins=[bA, hbias, pss_[sq_]])
                        ob_ = otb[(tt * nseq + sq_) % 2]
                        c.op("dve", lambda e, t0=t0, o_=o_, ob_=ob_: e.tensor_tensor(out=ob_[:, 0:CWL], in0=o_[:, 0:CWL], in1=bB[:, t0:t0 + CWL], op=ALU.mult),
                             outs=[ob_], ins=[o_, bB])
                        c.dma("sp", s["mixT"][cbk * 128:(cbk + 1) * 128, t0:t0 + CWL], ob_[:, 0:CWL], ins=[ob_], sb=ob_)
            c.barrier()

    def stage_peer(s, x_src, x_dst, l, rows=None, nb=None):
        T = s["T"]
        nb = nb or T // 128
        with contextlib.ExitStack() as st:
            wq = c.sbuf([128, 8, 2048], F32, "wq", st)
            for k in range(8):
                c.dma("sp", wq[:, k, :], W["peer_wq"][l, k * 128:(k + 1) * 128, :], outs=[wq], sb=wq)
            kn = c.sbuf([128, 16, 128], F32, "kn", st)
            c.dma("sp", kn[:], W["peer_keys"][l].rearrange("h p n k -> n (h p) k"), outs=[kn], sb=kn)
            kT = c.sbuf([128, 16, 128], F32, "kT", st)
            for hp0 in range(0, 16, 4):
                ps = c.next_ps()
                for q in range(4):
                    c.op("pe", lambda e, q=q, hp0=hp0: e.transpose(ps[:, q * 128:(q + 1) * 128], kn[:, hp0 + q, :], ident[:]), outs=[ps], ins=[kn, ident])
                c.op("act", lambda e, hp0=hp0: e.copy(kT[:, hp0:hp0 + 4, :], ps[:].rearrange("p (q s) -> p q s", q=4)), outs=[kT], ins=[ps])
            A2 = c.sbuf([128, D], F32, "A2", st)
            S2 = c.sbuf([128, D], F32, "S2", st)
            G2 = c.sbuf([128, D], F32, "G2", st)
            xts = [c.sbuf([128, D], F32, f"px{i}", st) for i in range(2)]
            htms = [c.sbuf([128, D], F32, f"htm{i}", st) for i in range(2)]
            gr = htms[1]
            ci = s["ci"]
            c.dma("sp", S2[:], mod_d[l, ci, 3 * D:4 * D].partition_broadcast(128), outs=[S2], sb=S2)
            c.dma("sp", A2[:], mod_d[l, ci, 4 * D:5 * D].partition_broadcast(128), outs=[A2], sb=A2)
            c.dma("sp", G2[:], mod_d[l, ci, 5 * D:6 * D].partition_broadcast(128), outs=[G2], sb=G2)
            c.dma("sp", gr[:], W["norm2_g"][l, :].partition_broadcast(128), outs=[gr], sb=gr)
            c.op("dve", lambda e: e.scalar_tensor_tensor(out=A2[:], in0=A2[:], scalar=1.0, in1=gr[:], op0=ALU.add, op1=ALU.mult), outs=[A2], ins=[A2, gr])
            hT = c.sbuf([128, 8, 128], F32, "phT", st)
            qT = c.sbuf([128, 16, 128], F32, "qT", st)
            sc = kn
            sc2 = c.sbuf([128, 128], F32, "sc2", st)
            sv = c.sbuf([128, 16, 16], F32, "sv", st)
            si = c.sbuf([128, 16, 16], U32, "si", st)
            sif = c.sbuf([128, 16, 16], F32, "sif", st)
            cand2 = c.sbuf([128, 256], F32, "cand2", st)
            fv = c.sbuf([128, 8, 16], F32, "fv", st)
            fp_ = c.sbuf([128, 8, 16], U32, "fp_", st)
            j1 = c.sbuf([128, 8, 16], I32, "j1", st)
            j1f = c.sbuf([128, 8, 16], F32, "j1f", st)
            j2f = c.sbuf([128, 8, 16], F32, "j2f", st)
            ai_ = c.sbuf([128, 8, 16], F32, "ai_", st)
            bi_ = c.sbuf([128, 8, 16], F32, "bi_", st)
            eidxs = [c.sbuf([128, 128], I32, f"eidx{i}", st) for i in range(2)]
            gws = [c.sbuf([128, 8, 16], F32, f"gw{i}", st) for i in range(2)]
            zs_ = c.sbuf([128, 8], F32, "zs_", st)
            NV = 6
            vb16 = [c.sbuf([128, D], BF16, f"vb16_{i}", st) for i in range(NV)]
            dgs = [c.sbuf([128, 128], BF16, f"dgs{i}", st) for i in range(4)]
            saved_psl = c.psl
            c.psl = saved_psl[:5]
            psA, psB = saved_psl[5], saved_psl[6]
            junks = [c.sbuf([128, D], BF16, f"junkb{i}", st) for i in range(2)]
            GS = 2
            NBUF = 6
            uvs = [c.sbuf([128, 2 * D], F32, f"uvp{i}", st) for i in range(NBUF)]
            prods = [c.sbuf([128, D], F32, f"prod{i}", st) for i in range(2)] if _EXP.get("split", False) else None
            actc = [c.sbuf([128, GS], F32, f"actc{i}", st) for i in range(2)]
            gtc = [c.sbuf([128, GS], F32, f"gtc{i}", st) for i in range(2)]
            gac = [c.sbuf([128, GS], F32, f"gac{i}", st) for i in range(2)]
            uv_d = I["peer_uv"].ap()
            qr = None
            if rows is not None:
                qr = c.sbuf([128, nb], I32, "qr", st)
                c.dma("sp", qr[:], rows.ap(), outs=[qr], sb=qr)
            ss = c.sbuf([128, 1], F32, "pss", st)
            ss2 = c.sbuf([128, 1], F32, "pss2", st)
            iota16 = iota_f[:, 0:16]
            eqb, candb = sc, qT
            eq4 = sc[:].rearrange("p (h a) (b c) -> p h (a b) c", a=2, c=16)
            cand3 = qT[:].rearrange("p (h a) n -> p h (a n)", a=2)
            ac_tr = [[c.vbuf(f"actr{i}{j}") for j in range(GS)] for i in range(2)]

            def pre_ops(b):
                xt, htm, eidx, gw = xts[b % 2], htms[b % 2], eidxs[b % 2], gws[b % 2]
                if qr is None:
                    c.dma("sp", xt[:], x_src[b * 128:(b + 1) * 128, :], outs=[xt], sb=xt)
                else:
                    c.dma("pool", xt[:], x_src.ap(), outs=[xt], ins=[qr], sb=xt,
                          indirect=dict(out_offset=None, in_offset=bass.IndirectOffsetOnAxis(ap=qr[:, b:b + 1].bitcast(U32), axis=0)))
                yield
                yield
                yield
                jk = junks[0]
                c.op("dve", lambda e: e.scalar_tensor_tensor(out=jk[:], in0=xt[:], scalar=1.0, in1=xt[:], op0=ALU.mult, op1=ALU.mult, accum_out=ss[:]),
                     outs=[jk, ss], ins=[xt])
                yield
                c.op("dve", lambda e: e.tensor_scalar(out=ss2[:], in0=ss[:], scalar1=1.0 / D, scalar2=1e-6, op0=ALU.mult, op1=ALU.add), outs=[ss2], ins=[ss])
                c.op("act", lambda e: e.activation(ss2[:], ss2[:], AF.Sqrt), outs=[ss2], ins=[ss2])
                yield
                c.op("dve", lambda e: e.reciprocal(ss2[:], ss2[:]), outs=[ss2], ins=[ss2])
                yield
                c.op("dve", lambda e: e.scalar_tensor_tensor(out=htm[:], in0=xt[:], scalar=ss2[:, 0:1], in1=A2[:], op0=ALU.mult, op1=ALU.mult),
                     outs=[htm], ins=[xt, ss2, A2])
                yield
                c.op("dve", lambda e: e.tensor_tensor(out=htm[:], in0=htm[:], in1=S2[:], op=ALU.add), outs=[htm], ins=[htm, S2])
                yield
                for half in range(2):
                    ps = c.next_ps()
                    for kk in range(4):
                        k = half * 4 + kk
                        c.op("pe", lambda e, k=k, kk=kk: e.transpose(ps[:, kk * 128:(kk + 1) * 128], htm[:, k * 128:(k + 1) * 128], ident[:]), outs=[ps], ins=[htm, ident])
                    c.op("act", lambda e, half=half: e.copy(hT[:, half * 4:half * 4 + 4, :], ps[:].rearrange("p (q s) -> p q s", q=4)), outs=[hT], ins=[ps])
                    yield
                for m0 in range(0, 16, 4):
                    ps = c.next_ps()
                    for mm in range(4):
                        m = m0 + mm
                        for k in range(8):
                            c.op("pe", lambda e, m=m, mm=mm, k=k: e.matmul(ps[:, mm * 128:(mm + 1) * 128], wq[:, k, m * 128:(m + 1) * 128], hT[:, k, :], start=(k == 0), stop=(k == 7)),
                                 outs=[ps], ins=[wq, hT], nosame=True)
                        yield
                    c.op("act", lambda e, m0=m0: e.copy(qT[:, m0:m0 + 4, :], ps[:].rearrange("p (q s) -> p q s", q=4)), outs=[qT], ins=[ps])
                    yield
                for m0 in range(0, 16, 4):
                    ps = c.next_ps()
                    for mm in range(4):
                        m = m0 + mm
                        c.op("pe", lambda e, m=m, mm=mm: e.matmul(ps[:, mm * 128:(mm + 1) * 128], qT[:, m, :], kT[:, m, :], start=True, stop=True), outs=[ps], ins=[qT, kT])
                    c.op("act", lambda e, m0=m0: e.copy(sc[:, m0:m0 + 4, :], ps[:].rearrange("p (q s) -> p q s", q=4)), outs=[sc], ins=[ps])
                    yield
                for _ in range(24):
                    yield
                for m in range(16):
                    c.op("dve", lambda e, m=m: e.max(sv[:, m, 0:8], sc[:, m, :]), outs=[sv], ins=[sc])
                    yield
                    c.op("dve", lambda e, m=m: e.max_index(si[:, m, 0:8], sv[:, m, 0:8], sc[:, m, :]), outs=[si], ins=[sv, sc])
                    c.op("dve", lambda e, m=m: e.match_replace(sc2[:], sv[:, m, 0:8], sc[:, m, :], -1e30), outs=[sc2], ins=[sv, sc])
                    yield
                    c.op("dve", lambda e, m=m: e.max(sv[:, m, 8:16], sc2[:]), outs=[sv], ins=[sc2])
                    yield
                    c.op("dve", lambda e, m=m: e.max_index(si[:, m, 8:16], sv[:, m, 8:16], sc2[:]), outs=[si], ins=[sv, sc2])
                    yield
                c.op("dve", lambda e: e.tensor_copy(sif[:], si[:]), outs=[sif], ins=[si])
                yield
                sv4 = sv[:].rearrange("p (h a) j -> p h a j", a=2)
                sif4 = sif[:].rearrange("p (h a) j -> p h a j", a=2)
                c.op("dve", lambda e: e.tensor_tensor(out=cand3.rearrange("p h (a b) -> p h a b", b=16),
                                                      in0=sv4[:, :, 0, :].unsqueeze(3).broadcast_to([128, 8, 16, 16]),
                                                      in1=sv4[:, :, 1, :].unsqueeze(2).broadcast_to([128, 8, 16, 16]), op=ALU.add),
                     outs=[candb], ins=[sv])
                yield
                for h in range(8):
                    c.op("dve", lambda e, h=h: e.max(fv[:, h, 0:8], cand3[:, h, :]), outs=[fv], ins=[candb])
                    yield
                    c.op("dve", lambda e, h=h: e.max_index(fp_[:, h, 0:8], fv[:, h, 0:8], cand3[:, h, :]), outs=[fp_], ins=[fv, candb])
                    c.op("dve", lambda e, h=h: e.match_replace(cand2[:], fv[:, h, 0:8], cand3[:, h, :], -1e30), outs=[cand2], ins=[fv, candb])
                    yield
                    c.op("dve", lambda e, h=h: e.max(fv[:, h, 8:16], cand2[:]), outs=[fv], ins=[cand2])
                    yield
                    c.op("dve", lambda e, h=h: e.max_index(fp_[:, h, 8:16], fv[:, h, 8:16], cand2[:]), outs=[fp_], ins=[fv, cand2])
                    yield
                fpi = fp_[:].bitcast(I32)
                c.op("dve", lambda e: e.tensor_single_scalar(j1[:], fpi, 4, ALU.logical_shift_right), outs=[j1], ins=[fp_])
                yield
                c.op("dve", lambda e: e.tensor_copy(j1f[:], j1[:]), outs=[j1f], ins=[j1])
                yield
                c.op("dve", lambda e: e.tensor_single_scalar(j1[:], fpi, 15, ALU.bitwise_and), outs=[j1], ins=[fp_])
                yield
                c.op("dve", lambda e: e.tensor_copy(j2f[:], j1[:]), outs=[j2f], ins=[j1])
                yield
                for (jf, a, dst) in ((j1f, 0, ai_), (j2f, 1, bi_)):
                    for h in range(8):
                        c.op("dve", lambda e, jf=jf, h=h: e.tensor_tensor(out=eq4[:, h, :, :], in0=iota16.unsqueeze(1).broadcast_to([128, 16, 16]),
                                                                        in1=jf[:, h, :].unsqueeze(2).broadcast_to([128, 16, 16]), op=ALU.is_equal),
                             outs=[eqb], ins=[iota_f, jf])
                        yield
                    for h in range(8):
                        c.op("dve", lambda e, a=a, h=h: e.tensor_tensor(out=eq4[:, h, :, :], in0=eq4[:, h, :, :],
                                                                      in1=sif4[:, h, a, :].unsqueeze(1).broadcast_to([128, 16, 16]), op=ALU.mult),
                             outs=[eqb], ins=[eqb, sif])
                        yield
                    c.op("dve", lambda e, dst=dst: e.tensor_reduce(out=dst[:].rearrange("p h j -> p (h j)"), in_=eq4.rearrange("p h a b -> p (h a) b"), axis=AX.X, op=ALU.add),
                         outs=[dst], ins=[eqb])
                    yield
                c.op("dve", lambda e: e.scalar_tensor_tensor(out=ai_[:], in0=ai_[:], scalar=128.0, in1=bi_[:], op0=ALU.mult, op1=ALU.add), outs=[ai_], ins=[ai_, bi_])
                yield
                if l > 0:
                    c.op("dve", lambda e: e.tensor_scalar(out=ai_[:], in0=ai_[:], scalar1=float(l * 16384), scalar2=None, op0=ALU.add), outs=[ai_], ins=[ai_])
                    yield
                c.op("dve", lambda e: e.tensor_copy(eidx[:], ai_[:].rearrange("p h j -> p (h j)")), outs=[eidx], ins=[ai_])
                yield
                c.op("dve", lambda e: e.tensor_tensor(out=gw[:], in0=fv[:], in1=fv[:, :, 0:1].broadcast_to([128, 8, 16]), op=ALU.subtract), outs=[gw], ins=[fv])
                c.op("act", lambda e: e.activation(gw[:], gw[:], AF.Exp), outs=[gw], ins=[gw])
                yield
                yield
                yield
                c.op("dve", lambda e: e.tensor_reduce(out=zs_[:], in_=gw[:], axis=AX.X, op=ALU.add), outs=[zs_], ins=[gw])
                yield
                c.op("dve", lambda e: e.reciprocal(zs_[:], zs_[:]), outs=[zs_], ins=[zs_])
                yield
                c.op("dve", lambda e: e.tensor_tensor(out=gw[:], in0=gw[:], in1=zs_[:].unsqueeze(2).broadcast_to([128, 8, 16]), op=ALU.mult), outs=[gw], ins=[gw, zs_])
                yield

            def pump(gen, n):
                if gen is None:
                    return None
                for _ in range(n):
                    try:
                        next(gen)
                    except StopIteration:
                        return None
                return gen

            def slots(b, gen):
                xt, htm, eidx, gw = xts[b % 2], htms[b % 2], eidxs[b % 2], gws[b % 2]
                gwf = gw[:].rearrange("p h j -> p (h j)")
                NG = 128 // GS

                def gather(sl):
                    ub = uvs[sl % NBUF]
                    c.dma("pool", ub[:], uv_d, outs=[ub], ins=[eidx], sb=ub,
                          indirect=dict(out_offset=None, in_offset=bass.IndirectOffsetOnAxis(ap=eidx[:, sl:sl + 1].bitcast(U32), axis=0)))

                def dot(g, j):
                    sl = g * GS + j
                    ub, ac, jk = uvs[sl % NBUF], actc[g % 2], junks[sl % 2]
                    if j == 1 and _EXP.get("split", False):
                        pr_ = prods[g % 2]
                        c.op("pool", lambda e: e.tensor_tensor(out=pr_[:], in0=ub[:, 0:D], in1=htm[:], op=ALU.mult), outs=[pr_], ins=[ub, htm])
                        c.op("act", lambda e: e.activation(pr_[:], pr_[:], AF.Copy, accum_out=ac[:, j:j + 1]), outs=[pr_, ac_tr[g % 2][j]], ins=[pr_])
                    else:
                        c.op("dve", lambda e: e.scalar_tensor_tensor(out=jk[:], in0=ub[:, 0:D], scalar=1.0, in1=htm[:], op0=ALU.mult, op1=ALU.mult,
                                                                     accum_out=ac[:, j:j + 1]), outs=[jk, ac_tr[g % 2][j]], ins=[ub, htm])
                    vb = vb16[sl % NV]
                    c.op("act", lambda e: e.copy(vb[:], ub[:, D:2 * D]), outs=[vb], ins=[ub])

                def p1(g):
                    ac, gt = actc[g % 2], gtc[g % 2]
                    c.op("dve", lambda e: e.scalar_tensor_tensor(out=gt[:], in0=ac[:], scalar=0.044715, in1=ac[:], op0=ALU.mult, op1=ALU.mult),
                         outs=[gt], ins=ac_tr[g % 2])

                def p2(g):
                    ac, gt = actc[g % 2], gtc[g % 2]
                    c.op("dve", lambda e: e.scalar_tensor_tensor(out=gt[:], in0=gt[:], scalar=1.0, in1=ac[:], op0=ALU.add, op1=ALU.mult),
                         outs=[gt], ins=[gt] + ac_tr[g % 2])
                    c.op("act", lambda e: e.activation(gt[:], gt[:], AF.Sigmoid, scale=1.5957691216057308), outs=[gt], ins=[gt])

                def f1(g):
                    ac, gt, ga_ = actc[g % 2], gtc[g % 2], gac[g % 2]
                    c.op("dve", lambda e: e.tensor_tensor(out=ga_[:], in0=ac[:], in1=gt[:], op=ALU.mult), outs=[ga_], ins=[gt] + ac_tr[g % 2])

                def f2(g):
                    ga_ = gac[g % 2]
                    c.op("dve", lambda e: e.tensor_tensor(out=ga_[:], in0=ga_[:], in1=gwf[:, g * GS:(g + 1) * GS], op=ALU.mult), outs=[ga_], ins=[ga_, gw])

                def vmm(g, j):
                    sl = g * GS + j
                    vb, ga_, dg_ = vb16[sl % NV], gac[g % 2], dgs[sl % 4]
                    c.op("act", lambda e: e.activation(dg_[:], identb[:], AF.Copy, scale=ga_[:, j:j + 1]), outs=[dg_], ins=[identb, ga_])
                    c.op("pe", lambda e: e.matmul(psA[:], dg_[:], vb[:, 0:512], start=(sl == 0), stop=(sl == 127)), outs=[psA], ins=[dg_, vb], nosame=True)
                    c.op("pe", lambda e: e.matmul(psB[:], dg_[:], vb[:, 512:1024], start=(sl == 0), stop=(sl == 127)), outs=[psB], ins=[dg_, vb], nosame=True)

                for sl in range(NBUF):
                    gather(sl)
                dot(0, 0)
                gather(NBUF)
                dot(0, 1)
                gather(NBUF + 1)
                p1(0); p2(0)
                for g in range(NG):
                    nx = g + 1 < NG
                    if nx:
                        dot(g + 1, 0)
                        if (g + 1) * GS + NBUF < 128:
                            gather((g + 1) * GS + NBUF)
                    f1(g)
                    if nx:
                        dot(g + 1, 1)
                        if (g + 1) * GS + 1 + NBUF < 128:
                            gather((g + 1) * GS + 1 + NBUF)
                    f2(g)
                    if nx:
                        p1(g + 1)
                    vmm(g, 0)
                    vmm(g, 1)
                    if nx:
                        p2(g + 1)
                    gen = pump(gen, 4)
                pump(gen, 100000)
                for half, ps_ in enumerate((psA, psB)):
                    hs = slice(half * 512, (half + 1) * 512)
                    c.op("dve", lambda e, hs=hs, ps_=ps_: e.tensor_tensor(out=htm[:, hs], in0=ps_[:], in1=G2[:, hs], op=ALU.mult), outs=[htm], ins=[ps_, G2])
                c.op("dve", lambda e: e.tensor_tensor(out=xt[:], in0=xt[:], in1=htm[:], op=ALU.add), outs=[xt], ins=[xt, htm])
                c.dma("sp", x_dst[b * 128:(b + 1) * 128, :], xt[:], ins=[xt], sb=xt)

            pump(pre_ops(0), 100000)
            for b in range(nb):
                slots(b, pre_ops(b + 1) if b + 1 < nb else None)
            c.barrier()
            c.psl = saved_psl

    def stage_final(s, x_src, y_dst=None, T=None):
        T = T or s["T"]
        y_dst = y_dst if y_dst is not None else s["y"]
        with contextlib.ExitStack() as st:
            fg = c.sbuf([128, D], F32, "fg", st)
            c.dma("sp", fg[:], W["final_g"].ap().partition_broadcast(128), outs=[fg], sb=fg)
            xts = [c.sbuf([128, D], F32, f"fx{i}", st) for i in range(2)]
            junk = c.sbuf([128, D], F32, "fjunk", st)
            ss = c.sbuf([128, 1], F32, "fss", st)
            for b in range(T // 128):
                xt = xts[b % 2]
                c.dma("sp", xt[:], x_src[b * 128:(b + 1) * 128, :], outs=[xt], sb=xt)
                c.op("dve", lambda e: e.scalar_tensor_tensor(out=junk[:], in0=xt[:], scalar=1.0, in1=xt[:], op0=ALU.mult, op1=ALU.mult, accum_out=ss[:]),
                     outs=[junk, ss], ins=[xt])
                c.op("dve", lambda e: e.tensor_scalar(out=ss[:], in0=ss[:], scalar1=1.0 / D, scalar2=1e-6, op0=ALU.mult, op1=ALU.add), outs=[ss], ins=[ss])
                c.op("act", lambda e: e.activation(ss[:], ss[:], AF.Sqrt), outs=[ss], ins=[ss])
                c.op("dve", lambda e: e.reciprocal(ss[:], ss[:]), outs=[ss], ins=[ss])
                c.op("dve", lambda e: e.scalar_tensor_tensor(out=xt[:], in0=xt[:], scalar=ss[:, 0:1], in1=fg[:], op0=ALU.mult, op1=ALU.mult),
                     outs=[xt], ins=[xt, ss, fg])
                c.dma("sp", y_dst[b * 128:(b + 1) * 128, :], xt[:], ins=[xt], sb=xt)
            c.barrier()

    c.barrier()
    if want("mod"):
        stage_mod()
    for s in secs:
        xin = s["x"]
        if want("in0"):
            stage_inproj(s, xin, 0, W["w_in_ab"][0], None, 1536, 1)
        if want("s5"):
            stage_s5(s)
        if want("glu"):
            stage_glu(s)
        if want("lru"):
            stage_lru(s)
        if want("out0"):
            stage_outproj(s, xin, s["xa"], 0, W["w_out_ab"][0], None)
        if want("peer0"):
            stage_peer(s, s["xa"], s["xb"], 0)
        if want("in1"):
            stage_inproj(s, s["xb"], 1, W["w_in_c"][0], W["b_in_c"][0, :], 3072, 1)
        if want("hyfilt"):
            stage_hyfilt(s)
        if want("hyena"):
            stage_hyena(s)
        if want("out1"):
            stage_outproj(s, s["xb"], s["xa"], 1, W["w_out_c"][0], W["b_out_c"][0, :])
        if s["n"] == "S":
            if want("peer1"):
                stage_peer(s, s["xa"], s["xq"], 1, rows=I["qrows"], nb=8)
            if want("final"):
                stage_final(s, s["xq"], y_dst=O["ysq"], T=1024)
        else:
            if want("peer1"):
                stage_peer(s, s["xa"], s["xb"], 1)
            if want("final"):
                stage_final(s, s["xb"])
    c.barrier()
    return nc, c


_NC_CACHE = {}
_EXP = {}


def kernel(**inputs):
    NP = 4
    if "nc" not in _NC_CACHE:
        _NC_CACHE["nc"] = build(NP=NP)[0]
    nc = _NC_CACHE["nc"]
    cs = _consts()
    f32 = lambda a: np.ascontiguousarray(np.asarray(a, dtype=np.float32))
    shared = {k: f32(inputs[k]) for k in WEIGHT_SPECS}
    shared.update(cs)
    shared["peer_uv"] = np.ascontiguousarray(
        np.concatenate([f32(inputs["peer_u"]), f32(inputs["peer_v"])], axis=-1).reshape(2 * 16384, 2 * D))
    xp = f32(inputs["x_prompt"])
    xs = f32(inputs["x_sample"])
    c_ = f32(inputs["c"])
    cctx = f32(inputs["c_ctx"])
    sre, sim, slr = f32(inputs["state_s5_re"]), f32(inputs["state_s5_im"]), f32(inputs["state_lru"])
    in_maps = []
    for core in range(NCORES):
        b = core % 2
        m = dict(shared)
        m["xp"] = np.ascontiguousarray(xp[core * NP:(core + 1) * NP].reshape(NP * 256, D))
        m["xs"] = np.ascontiguousarray(xs[b])
        m["cond"] = np.ascontiguousarray(np.stack([cctx, c_[b]], axis=0))
        m["h0re"] = np.ascontiguousarray(sre[b, 0])
        m["h0im"] = np.ascontiguousarray(sim[b, 0])
        m["h0lru"] = np.ascontiguousarray(slr[b, 0])
        q = core // 2
        m["qrows"] = np.ascontiguousarray((q * 1024 + np.arange(8)[None, :] * 128 + np.arange(128)[:, None]).astype(np.int32))
        in_maps.append(m)
    res = run_bass_kernel_spmd(nc, in_maps, core_ids=list(range(NCORES)))
    r = res.results
    y_prompt = np.concatenate([r[i]["yp"].reshape(NP, 256, D) for i in range(NCORES)], axis=0).astype(np.float32)
    y_sample = np.stack([np.concatenate([r[b + 2 * q]["ysq"] for q in range(4)], axis=0) for b in range(2)], axis=0).astype(np.float32)
    s5re = np.concatenate([r[i]["s5re"] for i in range(NCORES)], axis=0)[:, None].astype(np.float32)
    s5im = np.concatenate([r[i]["s5im"] for i in range(NCORES)], axis=0)[:, None].astype(np.float32)
    lru = np.concatenate([r[i]["lru"] for i in range(NCORES)], axis=0)[:, None].astype(np.float32)
    return (y_prompt, y_sample, s5re, s5im, lru)
```

```python
import math
import contextlib
import numpy as np
import ml_dtypes
import concourse.bass as bass
import concourse.mybir as mybir
from concourse.alu_op_type import AluOpType as ALU
from concourse.bass_utils import run_bass_kernel_spmd

F32 = mybir.dt.float32
BF16 = mybir.dt.bfloat16
I32 = mybir.dt.int32
U32 = mybir.dt.uint32
AF = mybir.ActivationFunctionType
AX = mybir.AxisListType

D = 1024
NCORES = 8
TWO_PI = 2.0 * math.pi


class Buf:
    __slots__ = ("name", "w", "r", "dsem", "dcnt", "t")

    def __init__(self, name, t=None):
        self.name = name
        self.w = None
        self.r = []
        self.dsem = None
        self.dcnt = 0
        self.t = t

    def __getitem__(self, idx):
        return self.t[idx]


class Ctx:
    def __init__(self, nc, same_engine_sync=True):
        self.nc = nc
        self.es = contextlib.ExitStack()
        self.eng = {"pe": nc.tensor, "act": nc.scalar, "dve": nc.vector, "pool": nc.gpsimd, "sp": nc.sync}
        self.sem = {}
        self.cnt = {}
        self.waited = {e: {} for e in self.eng}
        for e in self.eng:
            self.sem[e] = self.es.enter_context(nc.semaphore("sem_" + e))
            self.cnt[e] = 0
        self.same = same_engine_sync
        self.bufs = []
        self.dsems = []
        self.free_dsems = []
        self.nbuf = 0
        self.ninstr = 0
        self.psl = []
        self.psi = 0

    def sbuf(self, shape, dt=F32, name=None, stack=None):
        self.nbuf += 1
        name = f"{name or 'sb'}_{self.nbuf}"
        t = (stack or self.es).enter_context(self.nc.sbuf_tensor(name, list(shape), dt))
        b = Buf(name, t)
        self.bufs.append(b)
        return b

    def psum(self, shape, dt=F32, name=None):
        self.nbuf += 1
        name = f"{name or 'ps'}_{self.nbuf}"
        t = self.es.enter_context(self.nc.psum_tensor(name, list(shape), dt))
        b = Buf(name, t)
        self.bufs.append(b)
        return b

    def next_ps(self):
        b = self.psl[self.psi % len(self.psl)]
        self.psi += 1
        return b

    def vbuf(self, name):
        b = Buf(name)
        self.bufs.append(b)
        return b

    def _wait(self, e, key, val):
        if val <= 0:
            return
        if self.waited[e].get(key, 0) >= val:
            return
        sem = self.sem[key] if isinstance(key, str) else key
        self.eng[e].wait_ge(sem, val)
        self.waited[e][key] = val

    def _deps(self, e, ins, outs, nosame=False):
        deps = []
        for b in ins:
            if b.w is not None:
                deps.append(b.w)
        for b in outs:
            if b.w is not None:
                deps.append(b.w)
            deps.extend(b.r)
        for (k, c) in deps:
            if isinstance(k, str):
                if k == e and (nosame or not self.same):
                    continue
                self._wait(e, k, c)
            else:
                sem, cnt = k
                self._wait(e, sem, cnt[0])

    def op(self, e, fn, outs=(), ins=(), nosame=False):
        self._deps(e, ins, outs, nosame or e == "pe")
        ins_ = fn(self.eng[e])
        self.cnt[e] += 1
        c = self.cnt[e]
        ins_.then_inc(self.sem[e], 1)
        self.ninstr += 1
        for b in ins:
            b.r.append((e, c))
            if len(b.r) > 48:
                b.r = b.r[-48:]
        for b in outs:
            b.w = (e, c)
            b.r = []
        return ins_

    def dma(self, q, out_ap, in_ap, outs=(), ins=(), sb=None, indirect=None, **kw):
        if sb.dsem is None:
            if self.free_dsems:
                sb.dsem = self.free_dsems.pop()
            else:
                sem = self.es.enter_context(self.nc.semaphore("d_" + sb.name))
                sb.dsem = (sem, [0])
                self.dsems.append(sb.dsem)
        self._deps(q, ins, outs)
        if indirect is None:
            ins_ = self.eng[q].dma_start(out=out_ap, in_=in_ap, **kw)
        else:
            ins_ = self.eng[q].indirect_dma_start(out=out_ap, in_=in_ap, **indirect)
        sem, cnt = sb.dsem
        cnt[0] += 16
        ins_.then_inc(sem, 16)
        self.ninstr += 1
        key = (sb.dsem, cnt[0])
        for b in ins:
            b.r.append(key)
            if len(b.r) > 48:
                b.r = b.r[-48:]
        for b in outs:
            b.w = key
            b.r = []
        return ins_

    def barrier(self):
        for e in self.eng:
            for k in self.eng:
                if k != e:
                    self._wait(e, k, self.cnt[k])
            for (sem, cnt) in self.dsems:
                self._wait(e, sem, cnt[0])
        for b in self.bufs:
            b.w = None
            b.r = []
            if b.dsem is not None:
                self.free_dsems.append(b.dsem)
                b.dsem = None


def _dft_consts(L):
    n = 2 * L
    k = np.arange(L, dtype=np.float64)
    w = 2.0 * np.pi * (k + 0.5) / n
    t = np.arange(L, dtype=np.float64)
    ang = np.outer(t, w)
    cs, sn = np.cos(ang), np.sin(ang)
    G = np.concatenate([cs, -sn], axis=1)
    Gb = np.concatenate([cs, sn], axis=1)
    Gb[0, :] = 0.0
    Gi = np.concatenate([cs.T, -sn.T], axis=0) / L
    bf = ml_dtypes.bfloat16
    RW = min(512, L); KH = min(L // 128, 16); RH = min(2 * L // 128, 16)
    Gt = G.astype(np.float32).reshape((L // 128) // KH, KH, 128, 2 * L // RW, RW).transpose(3, 0, 2, 1, 4)
    Git = Gi.astype(np.float32).reshape((2 * L // 128) // RH, RH, 128, L // RW, RW).transpose(3, 0, 2, 1, 4)
    return np.ascontiguousarray(Gt).astype(bf), None, np.ascontiguousarray(Git).astype(bf)


def _emb_const(L):
    pos = np.arange(L, dtype=np.float32)
    t = (pos / np.float32(L))[:, None]
    bands = np.linspace(1e-4, 15, 16, dtype=np.float32)
    ang = (np.float32(2.0 * math.pi / L)) * pos[:, None] * bands[None, :]
    emb = np.concatenate([t, np.cos(ang), -np.sin(ang)], axis=-1).astype(np.float32)
    return np.ascontiguousarray(emb.T)


_CONST_CACHE = {}


def _consts():
    if not _CONST_CACHE:
        for L in (256, 4096):
            G, Gb, Gi = _dft_consts(L)
            _CONST_CACHE[f"G{L}"] = G
            _CONST_CACHE[f"Gi{L}"] = Gi
            _CONST_CACHE[f"emb{L}"] = _emb_const(L)
    return _CONST_CACHE


WEIGHT_SPECS = {
    "norm1_g": [2, 1024], "norm2_g": [2, 1024], "w_mod": [2, 1024, 6144], "b_mod": [2, 6144],
    "w_in_ab": [1, 1024, 1536], "s5_a_re": [1, 2, 32, 64], "s5_a_im": [1, 2, 32, 64], "s5_log_dt": [1, 2, 32],
    "s5_b_re": [1, 2, 32, 64, 16], "s5_b_im": [1, 2, 32, 64, 16], "s5_c_re": [1, 2, 32, 16, 64],
    "s5_c_im": [1, 2, 32, 16, 64], "s5_d": [1, 512], "s5_w_glu": [1, 512, 512], "s5_b_glu": [1, 512],
    "lru_conv_w": [1, 4, 512], "lru_conv_b": [1, 512], "lru_w_a": [1, 2, 8, 64, 64], "lru_b_a": [1, 2, 512],
    "lru_w_x": [1, 2, 8, 64, 64], "lru_b_x": [1, 2, 512], "lru_lambda": [1, 2, 512], "w_out_ab": [1, 1024, 1024],
    "w_in_c": [1, 1024, 3072], "b_in_c": [1, 3072], "hy_conv_w": [1, 3, 3072], "hy_conv_b": [1, 3072],
    "hy_w1": [1, 33, 64], "hy_b1": [1, 64], "hy_freq1": [1, 64], "hy_w2": [1, 64, 64], "hy_b2": [1, 64],
    "hy_freq2": [1, 64], "hy_w3": [1, 64, 2048], "hy_decay": [1, 2, 1024], "hy_bias": [1, 1024],
    "w_out_c": [1, 1024, 1024], "b_out_c": [1, 1024], "peer_wq": [2, 1024, 2048],
    "peer_keys": [2, 8, 2, 128, 128], "final_g": [1024],
}


def build(NP=4, do_P=True, do_S=True, stages=None, dbg=()):
    nc = bass.Bass("TRN2", target_bir_lowering=False)
    c = Ctx(nc)
    TP = NP * 256
    TS = 4096
    W = {k: nc.dram_tensor(k, v, F32, kind="ExternalInput") for k, v in WEIGHT_SPECS.items()}
    I = {}
    I["xp"] = nc.dram_tensor("xp", [TP, D], F32, kind="ExternalInput")
    I["xs"] = nc.dram_tensor("xs", [TS, D], F32, kind="ExternalInput")
    I["cond"] = nc.dram_tensor("cond", [2, D], F32, kind="ExternalInput")
    I["h0re"] = nc.dram_tensor("h0re", [2, 32, 64], F32, kind="ExternalInput")
    I["h0im"] = nc.dram_tensor("h0im", [2, 32, 64], F32, kind="ExternalInput")
    I["h0lru"] = nc.dram_tensor("h0lru", [2, 512], F32, kind="ExternalInput")
    for L in (256, 4096):
        _RW = min(512, L); _KH = min(L // 128, 16); _RH = min(2 * L // 128, 16)
        I[f"G{L}"] = nc.dram_tensor(f"G{L}", [2 * L // _RW, (L // 128) // _KH, 128, _KH, _RW], BF16, kind="ExternalInput")
        I[f"Gi{L}"] = nc.dram_tensor(f"Gi{L}", [L // _RW, (2 * L // 128) // _RH, 128, _RH, _RW], BF16, kind="ExternalInput")
        I[f"emb{L}"] = nc.dram_tensor(f"emb{L}", [33, L], F32, kind="ExternalInput")
    O = {}
    O["yp"] = nc.dram_tensor("yp", [TP, D], F32, kind="ExternalOutput")
    O["ysq"] = nc.dram_tensor("ysq", [1024, D], F32, kind="ExternalOutput")
    I["qrows"] = nc.dram_tensor("qrows", [128, 8], I32, kind="ExternalInput")
    I["peer_uv"] = nc.dram_tensor("peer_uv", [2 * 16384, 2 * D], F32, kind="ExternalInput")
    O["s5re"] = nc.dram_tensor("s5re", [NP, 2, 32, 64], F32, kind="ExternalOutput")
    O["s5im"] = nc.dram_tensor("s5im", [NP, 2, 32, 64], F32, kind="ExternalOutput")
    O["lru"] = nc.dram_tensor("lru", [NP, 2, 512], F32, kind="ExternalOutput")

    def scratch(name, shape, dt=F32):
        kind = "ExternalOutput" if name in dbg else "Internal"
        return nc.dram_tensor(name, shape, dt, kind=kind)

    mod_d = scratch("mod_d", [2, 2, 6144])
    secs = []
    if do_P:
        secs.append(dict(n="P", T=TP, nseq=NP, L=256, row=256, x=I["xp"], ci=0, h0=False, y=O["yp"]))
    if do_S:
        secs.append(dict(n="S", T=TS, nseq=1, L=4096, row=64, x=I["xs"], ci=1, h0=True, y=None))
    for s in secs:
        n, T = s["n"], s["T"]
        s["zT"] = scratch(f"zT_{n}", [3072, T])
        s["mixT"] = scratch(f"mixT_{n}", [1024, T], BF16)
        s["ys5T"] = scratch(f"ys5T_{n}", [512, T])
        s["xa"] = scratch(f"xa_{n}", [T, D])
        s["xb"] = scratch(f"xb_{n}", [T, D])
        s["filt"] = scratch(f"filt_{n}", [s["L"], 2048], BF16)
        s["xq"] = scratch(f"xq_{n}", [1024, D])

    def want(st):
        return stages is None or st in stages

    for i in range(7):
        c.psl.append(c.psum([128, 512], F32, name=f"pb{i}"))
    psb = c.psum([128, 1024], BF16, name="psb")
    ident = c.sbuf([128, 128], F32, name="ident")
    identb = c.sbuf([128, 128], BF16, name="identb")
    ones = c.sbuf([128, 2], F32, name="ones")
    iota_f = c.sbuf([128, 128], F32, name="iota_f")
    tcol = c.sbuf([128, 32], F32, name="tcol")
    c.op("pool", lambda e: e.iota(iota_f[:], pattern=[[1, 128]], base=0, channel_multiplier=-1,
                                   allow_small_or_imprecise_dtypes=True), outs=[iota_f])
    c.op("dve", lambda e: e.tensor_single_scalar(ident[:], iota_f[:], 0.0, ALU.is_equal), outs=[ident], ins=[iota_f])
    c.op("dve", lambda e: e.tensor_copy(identb[:], ident[:]), outs=[identb], ins=[ident])
    c.op("dve", lambda e: e.memset(ones[:], 1.0), outs=[ones])
    c.op("pool", lambda e: e.iota(iota_f[:], pattern=[[1, 128]], base=0, channel_multiplier=0,
                                   allow_small_or_imprecise_dtypes=True), outs=[iota_f])
    c.op("pool", lambda e: e.iota(tcol[:], pattern=[[128, 32]], base=0, channel_multiplier=1,
                                   allow_small_or_imprecise_dtypes=True), outs=[tcol])

    def gelu(dst, dst_b, src, src_b, tmp, tmp_b):
        c.op("dve", lambda e: e.tensor_tensor(out=tmp, in0=src, in1=src, op=ALU.mult), outs=[tmp_b], ins=[src_b])
        c.op("dve", lambda e: e.tensor_scalar(out=tmp, in0=tmp, scalar1=0.044715, scalar2=1.0, op0=ALU.mult, op1=ALU.add),
             outs=[tmp_b], ins=[tmp_b])
        c.op("dve", lambda e: e.tensor_tensor(out=tmp, in0=tmp, in1=src, op=ALU.mult), outs=[tmp_b], ins=[tmp_b, src_b])
        c.op("act", lambda e: e.activation(tmp, tmp, AF.Sigmoid, scale=1.5957691216057308), outs=[tmp_b], ins=[tmp_b])
        c.op("dve", lambda e: e.tensor_tensor(out=dst, in0=src, in1=tmp, op=ALU.mult), outs=[dst_b], ins=[src_b, tmp_b])

    def fm_load(q, dst_b, dst_ap, src_1d):
        c.dma(q, dst_ap, src_1d.rearrange("(k p) -> p k", p=128), outs=[dst_b], sb=dst_b,
              allow_slow_non_contiguous=True)

    def stage_mod():
        with contextlib.ExitStack() as st:
            condT = c.sbuf([128, 8, 2], F32, "condT", st)
            for ci in range(2):
                c.dma("sp", condT[:, :, ci], I["cond"][ci, :].rearrange("(k p) -> p k", p=128), outs=[condT], sb=condT,
                      allow_slow_non_contiguous=True)
            c.op("act", lambda e: e.activation(condT[:], condT[:], AF.Silu), outs=[condT], ins=[condT])
            wts = [c.sbuf([128, 8, 512], F32, f"wmod{i}", st) for i in range(2)]
            brow = c.sbuf([2, 6144], F32, "brow", st)
            mrow = c.sbuf([2, 6144], F32, "mrow", st)
            for l in range(2):
                c.dma("sp", brow[:], W["b_mod"][l, :].partition_broadcast(2), outs=[brow], sb=brow)
                for n in range(12):
                    wt = wts[n % 2]
                    c.dma("sp", wt[:], W["w_mod"][l, :, n * 512:(n + 1) * 512].rearrange("(k p) n -> p k n", p=128),
                          outs=[wt], sb=wt)
                    ps = c.next_ps()
                    for k in range(8):
                        c.op("pe", lambda e, k=k: e.matmul(ps[0:2, :], condT[:, k, :], wt[:, k, :], start=(k == 0), stop=(k == 7)),
                             outs=[ps], ins=[condT, wt])
                    c.op("dve", lambda e: e.tensor_tensor(out=mrow[:, n * 512:(n + 1) * 512], in0=ps[0:2, :],
                                                          in1=brow[:, n * 512:(n + 1) * 512], op=ALU.add),
                         outs=[mrow], ins=[ps, brow])
                c.dma("sp", mod_d[l, :, :], mrow[:], ins=[mrow], sb=mrow)
            c.barrier()

    def stage_inproj(s, x_src, l, Wd, bias_d, N, which):
        T = s["T"]
        nm = N // 128
        with contextlib.ExitStack() as st:
            Wsb = c.sbuf([128, 8, N], BF16, "Wsb", st)
            wstg = [c.sbuf([128, N], F32, f"wstg{i}", st) for i in range(2)]
            for k in range(8):
                ws_ = wstg[k % 2]
                c.dma("sp", ws_[:], Wd[k * 128:(k + 1) * 128, :], outs=[ws_], sb=ws_)
                c.op("act" if k % 2 == 0 else "dve",
                     (lambda e, k=k, ws_=ws_: e.copy(Wsb[:, k, :], ws_[:])) if k % 2 == 0 else (lambda e, k=k, ws_=ws_: e.tensor_copy(Wsb[:, k, :], ws_[:])),
                     outs=[Wsb], ins=[ws_])
            gv = c.sbuf([128, 8], F32, "gv", st)
            scv = c.sbuf([128, 8], F32, "scv", st)
            shv = c.sbuf([128, 8], F32, "shv", st)
            fm_load("sp", gv, gv[:], W["norm1_g" if which == 1 else "norm2_g"][l, :])
            base = 0 if which == 1 else 3 * D
            fm_load("sp", shv, shv[:], mod_d[l, s["ci"], base:base + D])
            fm_load("sp", scv, scv[:], mod_d[l, s["ci"], base + D:base + 2 * D])
            c.op("dve", lambda e: e.scalar_tensor_tensor(out=scv[:], in0=scv[:], scalar=1.0, in1=gv[:], op0=ALU.add, op1=ALU.mult),
                 outs=[scv], ins=[scv, gv])
            bv = None
            if bias_d is not None:
                bv = c.sbuf([128, nm], F32, "bv", st)
                fm_load("sp", bv, bv[:], bias_d)
            xts = [c.sbuf([128, 4, D], F32, f"xt{i}", st) for i in range(2)]
            sq = c.sbuf([128, 4, D], F32, "sq", st)
            ss = c.sbuf([128, 4], F32, "ss", st)
            hT = c.sbuf([128, 8, 512], BF16, "hT", st)
            zos = [c.sbuf([128, 4, 512], F32, f"zo{i}", st) for i in range(2)]
            nt = T // 512
            zi = 0
            for t in range(nt):
                xt = xts[t % 2]
                c.dma("sp", xt[:], x_src[t * 512:(t + 1) * 512, :].rearrange("(j p) d -> p j d", p=128), outs=[xt], sb=xt)
                c.op("dve", lambda e: e.tensor_tensor(out=sq[:], in0=xt[:], in1=xt[:], op=ALU.mult), outs=[sq], ins=[xt])
                c.op("dve", lambda e: e.tensor_reduce(out=ss[:], in_=sq[:], axis=AX.X, op=ALU.add), outs=[ss], ins=[sq])
                c.op("dve", lambda e: e.tensor_scalar(out=ss[:], in0=ss[:], scalar1=1.0 / D, scalar2=1e-6, op0=ALU.mult, op1=ALU.add),
                     outs=[ss], ins=[ss])
                c.op("act", lambda e: e.activation(ss[:], ss[:], AF.Sqrt), outs=[ss], ins=[ss])
                c.op("dve", lambda e: e.reciprocal(ss[:], ss[:]), outs=[ss], ins=[ss])
                for j in range(4):
                    c.op("act", lambda e, j=j: e.activation(sq[:, j, :], xt[:, j, :], AF.Copy, scale=ss[:, j:j + 1]),
                         outs=[sq], ins=[xt, ss])
                for k in range(8):
                    ps = c.next_ps()
                    for j in range(4):
                        c.op("pe", lambda e, j=j, k=k: e.transpose(ps[:, j * 128:(j + 1) * 128], sq[:, j, k * 128:(k + 1) * 128], ident[:]),
                             outs=[ps], ins=[sq, ident])
                    c.op("act", lambda e, k=k: e.activation(hT[:, k, :], ps[:], AF.Identity, scale=scv[:, k:k + 1], bias=shv[:, k:k + 1]),
                         outs=[hT], ins=[ps, scv, shv], nosame=True)
                for m0 in range(0, nm, 4):
                    zo = zos[zi % 2]
                    zi += 1
                    for mm in range(4):
                        m = m0 + mm
                        ps = c.next_ps()
                        for k in range(8):
                            c.op("pe", lambda e, k=k, m=m: e.matmul(ps[:], Wsb[:, k, m * 128:(m + 1) * 128], hT[:, k, :], start=(k == 0), stop=(k == 7)),
                                 outs=[ps], ins=[Wsb, hT])
                        if bv is not None:
                            c.op("act", lambda e, m=m, mm=mm: e.activation(zo[:, mm, :], ps[:], AF.Identity, bias=bv[:, m:m + 1]),
                                 outs=[zo], ins=[ps, bv])
                        elif mm % 2 == 0:
                            c.op("act", lambda e, mm=mm: e.copy(zo[:, mm, :], ps[:]), outs=[zo], ins=[ps])
                        else:
                            c.op("dve", lambda e, mm=mm: e.tensor_copy(zo[:, mm, :], ps[:]), outs=[zo], ins=[ps])
                    c.dma("sp", s["zT"][m0 * 128:(m0 + 4) * 128, t * 512:(t + 1) * 512].rearrange("(m p) t -> p m t", p=128),
                          zo[:], ins=[zo], sb=zo)
            c.barrier()

    def stage_outproj(s, x_src, x_dst, l, Wd, bias_d):
        T = s["T"]
        with contextlib.ExitStack() as st:
            Wsb = c.sbuf([128, 8, D], BF16, "Wo", st)
            wstg = [c.sbuf([128, D], F32, f"wostg{i}", st) for i in range(2)]
            for k in range(8):
                ws_ = wstg[k % 2]
                c.dma("sp", ws_[:], Wd[k * 128:(k + 1) * 128, :], outs=[ws_], sb=ws_)
                c.op("act" if k % 2 == 0 else "dve",
                     (lambda e, k=k, ws_=ws_: e.copy(Wsb[:, k, :], ws_[:])) if k % 2 == 0 else (lambda e, k=k, ws_=ws_: e.tensor_copy(Wsb[:, k, :], ws_[:])),
                     outs=[Wsb], ins=[ws_])
            g1 = c.sbuf([128, 8], F32, "g1", st)
            fm_load("sp", g1, g1[:], mod_d[l, s["ci"], 2 * D:3 * D])
            gb = None
            if bias_d is not None:
                gb = c.sbuf([128, 8], F32, "gb", st)
                fm_load("sp", gb, gb[:], bias_d)
                c.op("dve", lambda e: e.tensor_tensor(out=gb[:], in0=gb[:], in1=g1[:], op=ALU.mult), outs=[gb], ins=[gb, g1])
            mts = [c.sbuf([128, 8, 512], BF16, f"mt{i}", st) for i in range(2)]
            xts = [c.sbuf([128, 4, D], F32, f"xo{i}", st) for i in range(2)]
            oT = c.sbuf([128, 8, 512], F32, "oT", st)
            for t in range(T // 512):
                mt, xt = mts[t % 2], xts[t % 2]
                c.dma("sp", mt[:], s["mixT"][:, t * 512:(t + 1) * 512].rearrange("(k p) t -> p k t", p=128), outs=[mt], sb=mt)
                c.dma("sp", xt[:], x_src[t * 512:(t + 1) * 512, :].rearrange("(j p) d -> p j d", p=128), outs=[xt], sb=xt)
                for m in range(8):
                    ps = c.next_ps()
                    for k in range(8):
                        c.op("pe", lambda e, k=k, m=m: e.matmul(ps[:], Wsb[:, k, m * 128:(m + 1) * 128], mt[:, k, :], start=(k == 0), stop=(k == 7)),
                             outs=[ps], ins=[Wsb, mt])
                    if gb is not None:
                        c.op("act", lambda e, m=m: e.activation(oT[:, m, :], ps[:], AF.Identity, scale=g1[:, m:m + 1], bias=gb[:, m:m + 1]),
                             outs=[oT], ins=[ps, g1, gb], nosame=True)
                    else:
                        c.op("act", lambda e, m=m: e.activation(oT[:, m, :], ps[:], AF.Copy, scale=g1[:, m:m + 1]),
                             outs=[oT], ins=[ps, g1], nosame=True)
                for j in range(4):
                    for half in range(2):
                        ps = c.next_ps()
                        for mm in range(4):
                            m = half * 4 + mm
                            c.op("pe", lambda e, m=m, mm=mm, j=j: e.transpose(ps[:, mm * 128:(mm + 1) * 128], oT[:, m, j * 128:(j + 1) * 128], ident[:]),
                                 outs=[ps], ins=[oT, ident])
                        c.op("dve", lambda e, j=j, half=half: e.tensor_tensor(out=xt[:, j, half * 512:(half + 1) * 512], in0=xt[:, j, half * 512:(half + 1) * 512],
                                                                              in1=ps[:], op=ALU.add), outs=[xt], ins=[xt, ps])
                c.dma("sp", x_dst[t * 512:(t + 1) * 512, :].rearrange("(j p) d -> p j d", p=128), xt[:], ins=[xt], sb=xt)
            c.barrier()

    def stage_s5(s):
        T, nseq, L = s["T"], s["nseq"], s["L"]
        K = int(math.log2(L))
        with contextlib.ExitStack() as st:
            mt_ = c.sbuf([128, 1], F32, "mt_", st)
            mb_ = c.sbuf([128, 1], F32, "mb_", st)
            C1 = c.sbuf([128, K, 64], F32, "C1", st)
            C2 = c.sbuf([128, K, 64], F32, "C2", st)
            J = c.sbuf([128, 64], F32, "J", st)
            BT = c.sbuf([16, 64, 128], BF16, "BT", st)
            CT = c.sbuf([128, 64, 16], BF16, "CT", st)
            Dv = c.sbuf([16, 32], F32, "Dv", st)
            hh = c.sbuf([128, 64], F32, "hh", st) if s["h0"] else None
            stS = c.sbuf([128, nseq, 64], F32, "stS", st) if not s["h0"] else None
            stp = contextlib.ExitStack()
            are = c.sbuf([128, 64], F32, "are", stp)
            aim = c.sbuf([128, 64], F32, "aim", stp)
            dtt = c.sbuf([128, 64], F32, "dtt", stp)
            for h in range(2):
                c.dma("sp", are[h * 64:(h + 1) * 64, :], W["s5_a_re"][0].rearrange("d g p -> p (d g)"), outs=[are], sb=are,
                      allow_slow_non_contiguous=True)
                c.dma("sp", aim[h * 64:(h + 1) * 64, :], W["s5_a_im"][0].rearrange("d g p -> p (d g)"), outs=[aim], sb=aim,
                      allow_slow_non_contiguous=True)
            c.dma("sp", dtt[:], W["s5_log_dt"][0].rearrange("d g -> (d g)").partition_broadcast(128), outs=[dtt], sb=dtt)
            c.op("act", lambda e: e.activation(dtt[:], dtt[:], AF.Exp), outs=[dtt], ins=[dtt])
            lr = c.sbuf([128, 64], F32, "lr", stp)
            li = c.sbuf([128, 64], F32, "li", stp)
            c.op("dve", lambda e: e.tensor_tensor(out=lr[:], in0=are[:], in1=dtt[:], op=ALU.mult), outs=[lr], ins=[are, dtt])
            c.op("dve", lambda e: e.tensor_tensor(out=li[:], in0=aim[:], in1=dtt[:], op=ALU.mult), outs=[li], ins=[aim, dtt])
            mag = c.sbuf([128, 64], F32, "mag", stp)
            c.op("act", lambda e: e.activation(mag[:], lr[:], AF.Exp), outs=[mag], ins=[lr])
            kf = c.sbuf([128, 64], F32, "kf", stp)
            ki = c.sbuf([128, 64], I32, "ki", stp)
            sn = c.sbuf([128, 64], F32, "sn", stp)
            cs = c.sbuf([128, 64], F32, "cs", stp)

            def sincos(arg_b):
                for (dst, shift) in ((sn, 0.0), (cs, math.pi / 2)):
                    c.op("dve", lambda e: e.tensor_scalar(out=kf[:], in0=arg_b[:], scalar1=shift, scalar2=1.0 / TWO_PI, op0=ALU.add, op1=ALU.mult),
                         outs=[kf], ins=[arg_b])
                    c.op("dve", lambda e: e.tensor_copy(ki[:], kf[:]), outs=[ki], ins=[kf])
                    c.op("dve", lambda e: e.tensor_copy(kf[:], ki[:]), outs=[kf], ins=[ki])
                    c.op("dve", lambda e: e.scalar_tensor_tensor(out=kf[:], in0=kf[:], scalar=-TWO_PI, in1=arg_b[:], op0=ALU.mult, op1=ALU.add),
                         outs=[kf], ins=[kf, arg_b])
                    c.op("dve", lambda e: e.tensor_scalar(out=kf[:], in0=kf[:], scalar1=shift, scalar2=None, op0=ALU.add), outs=[kf], ins=[kf])
                    c.op("dve", lambda e: e.tensor_scalar(out=kf[:], in0=kf[:], scalar1=math.pi, scalar2=-math.pi, op0=ALU.min, op1=ALU.max),
                         outs=[kf], ins=[kf])
                    c.op("act", lambda e: e.activation(dst[:], kf[:], AF.Sin), outs=[dst], ins=[kf])

            sincos(li)
            ar = c.sbuf([128, 64], F32, "ar", stp)
            ai = c.sbuf([128, 64], F32, "ai", stp)
            c.op("dve", lambda e: e.tensor_tensor(out=ar[:], in0=mag[:], in1=cs[:], op=ALU.mult), outs=[ar], ins=[mag, cs])
            c.op("dve", lambda e: e.tensor_tensor(out=ai[:], in0=mag[:], in1=sn[:], op=ALU.mult), outs=[ai], ins=[mag, sn])
            den = c.sbuf([128, 64], F32, "den", stp)
            t1 = c.sbuf([128, 64], F32, "t1", stp)
            t2 = c.sbuf([128, 64], F32, "t2", stp)
            cr = c.sbuf([128, 64], F32, "cr", stp)
            cim = c.sbuf([128, 64], F32, "cim", stp)
            nr = c.sbuf([128, 64], F32, "nr", stp)
            c.op("dve", lambda e: e.tensor_tensor(out=den[:], in0=are[:], in1=are[:], op=ALU.mult), outs=[den], ins=[are])
            c.op("dve", lambda e: e.tensor_tensor(out=t1[:], in0=aim[:], in1=aim[:], op=ALU.mult), outs=[t1], ins=[aim])
            c.op("dve", lambda e: e.tensor_tensor(out=den[:], in0=den[:], in1=t1[:], op=ALU.add), outs=[den], ins=[den, t1])
            c.op("dve", lambda e: e.reciprocal(den[:], den[:]), outs=[den], ins=[den])
            c.op("dve", lambda e: e.tensor_scalar(out=nr[:], in0=ar[:], scalar1=-1.0, scalar2=None, op0=ALU.add), outs=[nr], ins=[ar])
            c.op("dve", lambda e: e.tensor_tensor(out=t1[:], in0=nr[:], in1=are[:], op=ALU.mult), outs=[t1], ins=[nr, are])
            c.op("dve", lambda e: e.tensor_tensor(out=t2[:], in0=ai[:], in1=aim[:], op=ALU.mult), outs=[t2], ins=[ai, aim])
            c.op("dve", lambda e: e.tensor_tensor(out=t1[:], in0=t1[:], in1=t2[:], op=ALU.add), outs=[t1], ins=[t1, t2])
            c.op("dve", lambda e: e.tensor_tensor(out=cr[:], in0=t1[:], in1=den[:], op=ALU.mult), outs=[cr], ins=[t1, den])
            c.op("dve", lambda e: e.tensor_tensor(out=t1[:], in0=ai[:], in1=are[:], op=ALU.mult), outs=[t1], ins=[ai, are])
            c.op("dve", lambda e: e.tensor_tensor(out=t2[:], in0=nr[:], in1=aim[:], op=ALU.mult), outs=[t2], ins=[nr, aim])
            c.op("dve", lambda e: e.tensor_tensor(out=t1[:], in0=t1[:], in1=t2[:], op=ALU.subtract), outs=[t1], ins=[t1, t2])
            c.op("dve", lambda e: e.tensor_tensor(out=cim[:], in0=t1[:], in1=den[:], op=ALU.mult), outs=[cim], ins=[t1, den])
            c.op("dve", lambda e: e.memset(mt_[:], 0.0), outs=[mt_])
            c.op("dve", lambda e: e.memset(mt_[0:64, :], 1.0), outs=[mt_])
            c.op("dve", lambda e: e.memset(mb_[:], 1.0), outs=[mb_])
            c.op("dve", lambda e: e.memset(mb_[0:64, :], 0.0), outs=[mb_])
            pr = c.sbuf([128, 64], F32, "pr", stp)
            pi_ = c.sbuf([128, 64], F32, "pi_", stp)
            c.op("dve", lambda e: e.tensor_copy(pr[:], ar[:]), outs=[pr], ins=[ar])
            c.op("dve", lambda e: e.tensor_copy(pi_[:], ai[:]), outs=[pi_], ins=[ai])
            for k in range(K):
                c.op("dve", lambda e, k=k: e.tensor_scalar(out=t1[:], in0=pi_[:], scalar1=mb_[:, 0:1], scalar2=-1.0, op0=ALU.mult, op1=ALU.mult),
                     outs=[t1], ins=[pi_, mb_])
                c.op("dve", lambda e, k=k: e.scalar_tensor_tensor(out=C1[:, k, :], in0=pr[:], scalar=mt_[:, 0:1], in1=t1[:], op0=ALU.mult, op1=ALU.add),
                     outs=[C1], ins=[pr, mt_, t1])
                c.op("dve", lambda e, k=k: e.tensor_scalar(out=t1[:], in0=pr[:], scalar1=mb_[:, 0:1], scalar2=None, op0=ALU.mult),
                     outs=[t1], ins=[pr, mb_])
                c.op("dve", lambda e, k=k: e.scalar_tensor_tensor(out=C2[:, k, :], in0=pi_[:], scalar=mt_[:, 0:1], in1=t1[:], op0=ALU.mult, op1=ALU.add),
                     outs=[C2], ins=[pi_, mt_, t1])
                if k < K - 1:
                    c.op("dve", lambda e: e.tensor_tensor(out=t1[:], in0=pr[:], in1=pr[:], op=ALU.mult), outs=[t1], ins=[pr])
                    c.op("dve", lambda e: e.tensor_tensor(out=t2[:], in0=pi_[:], in1=pi_[:], op=ALU.mult), outs=[t2], ins=[pi_])
                    c.op("dve", lambda e: e.scalar_tensor_tensor(out=pi_[:], in0=pr[:], scalar=2.0, in1=pi_[:], op0=ALU.mult, op1=ALU.mult),
                         outs=[pi_], ins=[pr, pi_])
                    c.op("dve", lambda e: e.tensor_tensor(out=pr[:], in0=t1[:], in1=t2[:], op=ALU.subtract), outs=[pr], ins=[t1, t2])
            c.op("dve", lambda e: e.tensor_tensor(out=J[:], in0=ident[:, 0:64], in1=ident[:, 64:128], op=ALU.add), outs=[J], ins=[ident])
            bre = c.sbuf([64, 64, 16], F32, "bre", stp)
            bim = c.sbuf([64, 64, 16], F32, "bim", stp)
            for d in range(2):
                c.dma("sp", bre[:, d * 32:(d + 1) * 32, :], W["s5_b_re"][0, d].rearrange("g p c -> p g c"), outs=[bre], sb=bre)
                c.dma("sp", bim[:, d * 32:(d + 1) * 32, :], W["s5_b_im"][0, d].rearrange("g p c -> p g c"), outs=[bim], sb=bim)
            bbr = c.sbuf([64, 64, 16], F32, "bbr", stp)
            bbi = c.sbuf([64, 64, 16], F32, "bbi", stp)
            tb = c.sbuf([64, 64, 16], F32, "tb", stp)
            crb = cr[0:64, :].unsqueeze(2).broadcast_to([64, 64, 16])
            cib = cim[0:64, :].unsqueeze(2).broadcast_to([64, 64, 16])
            c.op("dve", lambda e: e.tensor_tensor(out=bbr[:], in0=bre[:], in1=crb, op=ALU.mult), outs=[bbr], ins=[bre, cr])
            c.op("dve", lambda e: e.tensor_tensor(out=tb[:], in0=bim[:], in1=cib, op=ALU.mult), outs=[tb], ins=[bim, cim])
            c.op("dve", lambda e: e.tensor_tensor(out=bbr[:], in0=bbr[:], in1=tb[:], op=ALU.subtract), outs=[bbr], ins=[bbr, tb])
            c.op("dve", lambda e: e.tensor_tensor(out=bbi[:], in0=bim[:], in1=crb, op=ALU.mult), outs=[bbi], ins=[bim, cr])
            c.op("dve", lambda e: e.tensor_tensor(out=tb[:], in0=bre[:], in1=cib, op=ALU.mult), outs=[tb], ins=[bre, cim])
            c.op("dve", lambda e: e.tensor_tensor(out=bbi[:], in0=bbi[:], in1=tb[:], op=ALU.add), outs=[bbi], ins=[bbi, tb])
            for dg0 in range(0, 64, 4):
                ps = c.next_ps()
                for q in range(4):
                    dg = dg0 + q
                    c.op("pe", lambda e, dg=dg, q=q: e.transpose(ps[0:16, q * 128:q * 128 + 64], bbr[:, dg, :], ident[0:64, 0:64]),
                         outs=[ps], ins=[bbr, ident])
                    c.op("pe", lambda e, dg=dg, q=q: e.transpose(ps[0:16, q * 128 + 64:q * 128 + 128], bbi[:, dg, :], ident[0:64, 0:64]),
                         outs=[ps], ins=[bbi, ident])
                c.op("act", lambda e, dg0=dg0: e.copy(BT[:, dg0:dg0 + 4, :], ps[0:16, :].rearrange("p (q s) -> p q s", q=4)), outs=[BT], ins=[ps])
            Cn = c.sbuf([16, 64, 128], F32, "Cn", stp)
            for d in range(2):
                c.dma("sp", Cn[:, d * 32:(d + 1) * 32, 0:64], W["s5_c_re"][0, d].rearrange("g c p -> c g p"), outs=[Cn], sb=Cn)
                c.dma("sp", Cn[:, d * 32:(d + 1) * 32, 64:128], W["s5_c_im"][0, d].rearrange("g c p -> c g p"), outs=[Cn], sb=Cn)
            c.op("act", lambda e: e.mul(Cn[:, :, 64:128], Cn[:, :, 64:128], -1.0), outs=[Cn], ins=[Cn])
            for dg0 in range(0, 64, 32):
                ps = c.next_ps()
                for q in range(32):
                    c.op("pe", lambda e, q=q, dg0=dg0: e.transpose(ps[:, q * 16:(q + 1) * 16], Cn[:, dg0 + q, :], ident[0:16, 0:16]),
                         outs=[ps], ins=[Cn, ident])
                c.op("act", lambda e, dg0=dg0: e.copy(CT[:, dg0:dg0 + 32, :], ps[:].rearrange("p (q s) -> p q s", q=32)), outs=[CT], ins=[ps])
            c.dma("sp", Dv[:], W["s5_d"][0, :].rearrange("(g c) -> c g", c=16), outs=[Dv], sb=Dv, allow_slow_non_contiguous=True)
            if s["h0"]:
                h0r = c.sbuf([128, 64], F32, "h0r", stp)
                h0s = c.sbuf([128, 64], F32, "h0s", stp)
                c.dma("sp", h0r[0:64, :], I["h0re"].ap().rearrange("d g p -> p (d g)"), outs=[h0r], sb=h0r, allow_slow_non_contiguous=True)
                c.dma("sp", h0r[64:128, :], I["h0im"].ap().rearrange("d g p -> p (d g)"), outs=[h0r], sb=h0r, allow_slow_non_contiguous=True)
                c.dma("sp", h0s[0:64, :], I["h0im"].ap().rearrange("d g p -> p (d g)"), outs=[h0s], sb=h0s, allow_slow_non_contiguous=True)
                c.dma("sp", h0s[64:128, :], I["h0re"].ap().rearrange("d g p -> p (d g)"), outs=[h0s], sb=h0s, allow_slow_non_contiguous=True)
                c.op("dve", lambda e: e.tensor_tensor(out=hh[:], in0=ar[:], in1=h0r[:], op=ALU.mult), outs=[hh], ins=[ar, h0r])
                c.op("dve", lambda e: e.tensor_tensor(out=t1[:], in0=ai[:], in1=h0s[:], op=ALU.mult), outs=[t1], ins=[ai, h0s])
                c.op("dve", lambda e: e.tensor_scalar(out=t2[:], in0=t1[:], scalar1=mb_[:, 0:1], scalar2=None, op0=ALU.mult), outs=[t2], ins=[t1, mb_])
                c.op("dve", lambda e: e.tensor_tensor(out=hh[:], in0=hh[:], in1=t2[:], op=ALU.add), outs=[hh], ins=[hh, t2])
                c.op("dve", lambda e: e.tensor_scalar(out=t2[:], in0=t1[:], scalar1=mt_[:, 0:1], scalar2=None, op0=ALU.mult), outs=[t2], ins=[t1, mt_])
                c.op("dve", lambda e: e.tensor_tensor(out=hh[:], in0=hh[:], in1=t2[:], op=ALU.subtract), outs=[hh], ins=[hh, t2])
            if not s["h0"]:
                pass
            c.barrier()
            stp.close()
            ug = [c.sbuf([16, T], F32, f"ug{i}", st) for i in range(2)]
            ugb = c.sbuf([16, T], BF16, "ugb", st)
            sst = [c.sbuf([128, T], F32, f"sst{d}", st) for d in range(2)]
            sbb = [[c.sbuf([128, T], BF16, f"sbb{d}{i}", st) for i in range(2)] for d in range(2)]
            hfin = [c.sbuf([128, T], BF16, f"hfin{i}", st) for i in range(2)]
            Ms = [[c.sbuf([128, K, 128], BF16, f"Ms{d}{i}", st) for i in range(2)] for d in range(2)]
            yg = [c.sbuf([16, T], F32, f"yg{i}", st) for i in range(2)]
            CW = 512 if L >= 512 else L
            PS_STATE = False
            for g in (range(32) if PS_STATE else ()):
                if g == 0:
                    saved_psl5 = c.psl
                    c.psl = saved_psl5[:3]
                    SB = [[saved_psl5[3], saved_psl5[4]], [saved_psl5[5], saved_psl5[6]]]
                u = ug[g % 2]
                c.dma("sp", u[:], s["zT"][g * 16:(g + 1) * 16, :], outs=[u], sb=u)
                c.op("act", lambda e: e.copy(ugb[:], u[:]), outs=[ugb], ins=[u])
                Md = [Ms[d][g % 2] for d in range(2)]
                cur = [sbb[d][0] for d in range(2)]
                for d in range(2):
                    dg = d * 32 + g
                    M = Md[d]
                    for k in range(K):
                        c.op("pool", lambda e, k=k, dg=dg, M=M: e.tensor_scalar(out=M[:, k, 0:64], in0=J[:], scalar1=C1[:, k, dg:dg + 1], scalar2=None, op0=ALU.mult),
                             outs=[M], ins=[J, C1])
                        c.op("pool", lambda e, k=k, dg=dg, M=M: e.tensor_scalar(out=M[:, k, 64:128], in0=J[:], scalar1=C2[:, k, dg:dg + 1], scalar2=None, op0=ALU.mult),
                             outs=[M], ins=[J, C2])
                for d in range(2):
                    dg = d * 32 + g
                    for bk in range(2):
                        c.op("pe", lambda e, bk=bk, dg=dg, d=d: e.matmul(SB[d][bk][:], BT[:, dg, :], ugb[:, bk * 512:(bk + 1) * 512], start=True, stop=True),
                             outs=[SB[d][bk]], ins=[BT, ugb])
                        c.op("act", lambda e, bk=bk, d=d: e.copy(cur[d][:, bk * 512:(bk + 1) * 512], SB[d][bk][:]), outs=[cur[d]], ins=[SB[d][bk]], nosame=True)
                for k in range(K):
                    sh = 1 << k
                    last = (k == K - 1)
                    for d in range(2):
                        M = Md[d]
                        lo, hi = (sh, L) if d == 0 else (0, L - sh)
                        off = -sh if d == 0 else sh
                        for sq_ in range(nseq):
                            base = sq_ * L
                            pb_ = SB[d][sq_ // 2]
                            o0 = (sq_ % 2) * 256
                            c.op("pe", lambda e, k=k, M=M, cu=cur[d], pb_=pb_, o0=o0, base=base, lo=lo, hi=hi, off=off: e.matmul(
                                pb_[:, o0 + lo:o0 + hi], M[:, k, :], cu[:, base + lo + off:base + hi + off], start=False, stop=True, skip_group_check=True),
                                outs=[pb_], ins=[M, cur[d]], nosame=True)
                        nxt = hfin[d] if last else sbb[d][(k + 1) % 2]
                        for bk in range(2):
                            c.op("act", lambda e, bk=bk, d=d, nxt=nxt: e.copy(nxt[:, bk * 512:(bk + 1) * 512], SB[d][bk][:]), outs=[nxt], ins=[SB[d][bk]], nosame=True)
                        cur[d] = nxt
                if stS is not None:
                    for d in range(2):
                        dg = d * 32 + g
                        for sq_ in range(nseq):
                            col = (sq_ % 2) * 256 + (L - 1 if d == 0 else 0)
                            c.op("dve", lambda e, sq_=sq_, col=col, dg=dg, d=d: e.tensor_copy(stS[:, sq_, dg:dg + 1], SB[d][sq_ // 2][:, col:col + 1]),
                                 outs=[stS], ins=[SB[d][sq_ // 2]], nosame=True)
                y = yg[g % 2]
                for t0 in range(0, T, 512):
                    ps = c.next_ps()
                    c.op("pe", lambda e, t0=t0, g=g: e.matmul(ps[0:16, :], CT[:, g, :], hfin[0][:, t0:t0 + 512], start=True, stop=False),
                         outs=[ps], ins=[CT, hfin[0]])
                    c.op("pe", lambda e, t0=t0, g=g: e.matmul(ps[0:16, :], CT[:, 32 + g, :], hfin[1][:, t0:t0 + 512], start=False, stop=True),
                         outs=[ps], ins=[CT, hfin[1]])
                    c.op("dve", lambda e, t0=t0, g=g: e.scalar_tensor_tensor(out=y[:, t0:t0 + 512], in0=u[:, t0:t0 + 512], scalar=Dv[:, g:g + 1],
                                                                             in1=ps[0:16, :], op0=ALU.mult, op1=ALU.add),
                         outs=[y], ins=[u, Dv, ps])
                c.dma("sp", s["ys5T"][g * 16:(g + 1) * 16, :], y[:], ins=[y], sb=y)
                if g == 31:
                    c.barrier()
                    c.psl = saved_psl5
            for g in (() if PS_STATE else range(32)):
                u = ug[g % 2]
                c.dma("sp", u[:], s["zT"][g * 16:(g + 1) * 16, :], outs=[u], sb=u)
                c.op("act", lambda e: e.copy(ugb[:], u[:]), outs=[ugb], ins=[u])
                Md = [Ms[d][g % 2] for d in range(2)]
                cur = [sbb[d][0] for d in range(2)]
                for d in range(2):
                    dg = d * 32 + g
                    M = Md[d]
                    for k in range(K):
                        eng = "pool"
                        c.op(eng, lambda e, k=k, dg=dg, M=M: e.tensor_scalar(out=M[:, k, 0:64], in0=J[:], scalar1=C1[:, k, dg:dg + 1], scalar2=None, op0=ALU.mult),
                             outs=[M], ins=[J, C1])
                        c.op(eng, lambda e, k=k, dg=dg, M=M: e.tensor_scalar(out=M[:, k, 64:128], in0=J[:], scalar1=C2[:, k, dg:dg + 1], scalar2=None, op0=ALU.mult),
                             outs=[M], ins=[J, C2])
                for d in range(2):
                    dg = d * 32 + g
                    for t0 in range(0, T, 512):
                        ps = c.next_ps()
                        c.op("pe", lambda e, t0=t0, dg=dg: e.matmul(ps[:], BT[:, dg, :], ugb[:, t0:t0 + 512], start=True, stop=True),
                             outs=[ps], ins=[BT, ugb])
                        c.op("dve", lambda e, t0=t0, d=d: e.tensor_copy(sst[d][:, t0:t0 + 512], ps[:]), outs=[sst[d]], ins=[ps], nosame=True)
                    if hh is not None:
                        col = 0 if d == 0 else T - 1
                        c.op("dve", lambda e, col=col, dg=dg, d=d: e.tensor_tensor(out=sst[d][:, col:col + 1], in0=sst[d][:, col:col + 1], in1=hh[:, dg:dg + 1], op=ALU.add),
                             outs=[sst[d]], ins=[sst[d], hh])
                    c.op("act", lambda e, d=d: e.copy(cur[d][:], sst[d][:]), outs=[cur[d]], ins=[sst[d]])
                for k in range(K):
                    sh = 1 << k
                    last = (k == K - 1)
                    for d in range(2):
                        M = Md[d]
                        if nseq > 1 and nseq % 2 == 0 and 2 * L <= 512:
                            lo, hi = (sh, L) if d == 0 else (0, L - sh)
                            off = -sh if d == 0 else sh
                            w_ = hi - lo
                            cu3 = cur[d][:].rearrange("p (s l) -> p s l", l=L)
                            ss3 = sst[d][:].rearrange("p (s l) -> p s l", l=L)
                            for sq0 in range(0, nseq, 2):
                                ps = c.next_ps()
                                po = ps[:, 0:2 * w_].rearrange("p (s l) -> p s l", s=2)
                                c.op("pe", lambda e, k=k, M=M, sq0=sq0, po=po, cu3=cu3, lo=lo, hi=hi, off=off: e.matmul(
                                    po, M[:, k, :], cu3[:, sq0:sq0 + 2, lo + off:hi + off], start=True, stop=True),
                                    outs=[ps], ins=[M, cur[d]])
                                c.op("dve", lambda e, sq0=sq0, po=po, ss3=ss3, lo=lo, hi=hi, d=d: e.tensor_tensor(
                                    out=ss3[:, sq0:sq0 + 2, lo:hi], in0=ss3[:, sq0:sq0 + 2, lo:hi], in1=po, op=ALU.add),
                                    outs=[sst[d]], ins=[sst[d], ps], nosame=True)
                            nxt = hfin[d] if last else sbb[d][(k + 1) % 2]
                            c.op("act", lambda e, nxt=nxt, d=d: e.copy(nxt[:], sst[d][:]), outs=[nxt], ins=[sst[d]])
                            cur[d] = nxt
                            continue
                        for sq_ in range(nseq):
                            base = sq_ * L
                            lo, hi = (sh, L) if d == 0 else (0, L - sh)
                            off = -sh if d == 0 else sh
                            for a0 in range(lo, hi, CW):
                                a1 = min(a0 + CW, hi)
                                ps = c.next_ps()
                                c.op("pe", lambda e, a0=a0, a1=a1, base=base, off=off, k=k, M=M, cu=cur[d]: e.matmul(
                                    ps[:, 0:a1 - a0], M[:, k, :], cu[:, base + a0 + off:base + a1 + off], start=True, stop=True),
                                    outs=[ps], ins=[M, cur[d]])
                                c.op("dve", lambda e, a0=a0, a1=a1, base=base, d=d: e.tensor_tensor(
                                    out=sst[d][:, base + a0:base + a1], in0=sst[d][:, base + a0:base + a1], in1=ps[:, 0:a1 - a0], op=ALU.add),
                                    outs=[sst[d]], ins=[sst[d], ps], nosame=True)
                        nxt = hfin[d] if last else sbb[d][(k + 1) % 2]
                        c.op("act", lambda e, nxt=nxt, d=d: e.copy(nxt[:], sst[d][:]), outs=[nxt], ins=[sst[d]])
                        cur[d] = nxt
                if stS is not None:
                    for d in range(2):
                        dg = d * 32 + g
                        for sq_ in range(nseq):
                            col = sq_ * L + (L - 1 if d == 0 else 0)
                            c.op("pool", lambda e, sq_=sq_, col=col, dg=dg, d=d: e.tensor_copy(stS[:, sq_, dg:dg + 1], sst[d][:, col:col + 1]),
                                 outs=[stS], ins=[sst[d]])
                y = yg[g % 2]
                for t0 in range(0, T, 512):
                    ps = c.next_ps()
                    c.op("pe", lambda e, t0=t0, g=g: e.matmul(ps[0:16, :], CT[:, g, :], hfin[0][:, t0:t0 + 512], start=True, stop=False),
                         outs=[ps], ins=[CT, hfin[0]])
                    c.op("pe", lambda e, t0=t0, g=g: e.matmul(ps[0:16, :], CT[:, 32 + g, :], hfin[1][:, t0:t0 + 512], start=False, stop=True),
                         outs=[ps], ins=[CT, hfin[1]])
                    c.op("dve", lambda e, t0=t0, g=g: e.scalar_tensor_tensor(out=y[:, t0:t0 + 512], in0=u[:, t0:t0 + 512], scalar=Dv[:, g:g + 1],
                                                                             in1=ps[0:16, :], op0=ALU.mult, op1=ALU.add),
                         outs=[y], ins=[u, Dv, ps])
                c.dma("sp", s["ys5T"][g * 16:(g + 1) * 16, :], y[:], ins=[y], sb=y)
            if stS is not None:
                for sq_ in range(nseq):
                    c.dma("sp", O["s5re"][sq_].rearrange("d g p -> p (d g)"), stS[0:64, sq_, :], ins=[stS], sb=stS, allow_slow_non_contiguous=True)
                    c.dma("sp", O["s5im"][sq_].rearrange("d g p -> p (d g)"), stS[64:128, sq_, :], ins=[stS], sb=stS, allow_slow_non_contiguous=True)
            c.barrier()

    def stage_glu(s):
        T = s["T"]
        with contextlib.ExitStack() as st:
            Wg = c.sbuf([128, 4, 512], F32, "Wg", st)
            for k in range(4):
                c.dma("sp", Wg[:, k, :], W["s5_w_glu"][0, k * 128:(k + 1) * 128, :], outs=[Wg], sb=Wg)
            bg = c.sbuf([128, 4], F32, "bg", st)
            fm_load("sp", bg, bg[:], W["s5_b_glu"][0, :])
            yts = [c.sbuf([128, 4, 512], F32, f"yt{i}", st) for i in range(2)]
            zs = c.sbuf([128, 4, 512], F32, "zs", st)
            tmp = c.sbuf([128, 4, 512], F32, "tmpg", st)
            outs_ = [c.sbuf([128, 4, 512], BF16, f"og{i}", st) for i in range(2)]
            for t in range(T // 512):
                yt, ot = yts[t % 2], outs_[t % 2]
                c.dma("sp", yt[:], s["ys5T"][:, t * 512:(t + 1) * 512].rearrange("(k p) t -> p k t", p=128), outs=[yt], sb=yt)
                gelu(zs[:], zs, yt[:], yt, tmp[:], tmp)
                for m in range(4):
                    ps = c.next_ps()
                    for k in range(4):
                        c.op("pe", lambda e, k=k, m=m: e.matmul(ps[:], Wg[:, k, m * 128:(m + 1) * 128], zs[:, k, :], start=(k == 0), stop=(k == 3)),
                             outs=[ps], ins=[Wg, zs])
                    c.op("act", lambda e, m=m: e.activation(tmp[:, m, :], ps[:], AF.Sigmoid, bias=bg[:, m:m + 1]), outs=[tmp], ins=[ps, bg])
                c.op("dve", lambda e: e.tensor_tensor(out=ot[:], in0=zs[:], in1=tmp[:], op=ALU.mult), outs=[ot], ins=[zs, tmp])
                c.dma("sp", s["mixT"][0:512, t * 512:(t + 1) * 512].rearrange("(k p) t -> p k t", p=128), ot[:], ins=[ot], sb=ot)
            c.barrier()

    def stage_lru(s):
        T, nseq, L, row = s["T"], s["nseq"], s["L"], s["row"]
        nrow = T // row
        with contextlib.ExitStack() as st:
            cw = c.sbuf([128, 4, 4], F32, "cw", st)
            for k in range(4):
                c.dma("sp", cw[:, :, k], W["lru_conv_w"][0, k, :].rearrange("(c p) -> p c", p=128), outs=[cw], sb=cw, allow_slow_non_contiguous=True)
            cb = c.sbuf([128, 4], F32, "cb", st)
            fm_load("sp", cb, cb[:], W["lru_conv_b"][0, :])
            ba = c.sbuf([128, 2, 4], F32, "ba", st)
            bx = c.sbuf([128, 2, 4], F32, "bx", st)
            lam = c.sbuf([128, 2, 4], F32, "lam", st)
            for d in range(2):
                fm_load("sp", ba, ba[:, d, :], W["lru_b_a"][0, d, :])
                fm_load("sp", bx, bx[:, d, :], W["lru_b_x"][0, d, :])
                fm_load("sp", lam, lam[:, d, :], W["lru_lambda"][0, d, :])
            c.op("act", lambda e: e.activation(lam[:], lam[:], AF.Exp, scale=-1.0), outs=[lam], ins=[lam])
            c.op("act", lambda e: e.activation(lam[:], lam[:], AF.Ln, bias=1.0), outs=[lam], ins=[lam])
            c.op("act", lambda e: e.mul(lam[:], lam[:], -8.0), outs=[lam], ins=[lam])
            h0l = None
            if s["h0"]:
                h0l = c.sbuf([128, 2, 4], F32, "h0l", st)
                for d in range(2):
                    fm_load("sp", h0l, h0l[:, d, :], I["h0lru"][d, :])
            stL = None
            if not s["h0"]:
                stL = c.sbuf([128, nseq, 2, 4], F32, "stL", st)
            Wa = c.sbuf([128, 2, 4, 128], F32, "Wa", st)
            Wx = c.sbuf([128, 2, 4, 128], F32, "Wx", st)
            c.op("dve", lambda e: e.memset(Wa[:], 0.0), outs=[Wa])
            c.op("dve", lambda e: e.memset(Wx[:], 0.0), outs=[Wx])
            for d in range(2):
                for cc in range(4):
                    for hh_ in range(2):
                        c.dma("sp", Wa[hh_ * 64:(hh_ + 1) * 64, d, cc, hh_ * 64:(hh_ + 1) * 64], W["lru_w_a"][0, d, 2 * cc + hh_], outs=[Wa], sb=Wa)
                        c.dma("sp", Wx[hh_ * 64:(hh_ + 1) * 64, d, cc, hh_ * 64:(hh_ + 1) * 64], W["lru_w_x"][0, d, 2 * cc + hh_], outs=[Wx], sb=Wx)
            xr = c.sbuf([128, T], F32, "xr", st)
            xg = c.sbuf([128, T], F32, "xg", st)
            xb = c.sbuf([128, T], F32, "xb", st)
            ra = c.sbuf([128, T], F32, "ra", st)
            ib = c.sbuf([128, T], F32, "ib", st)
            hf = c.sbuf([128, T], F32, "hf", st)
            hb = c.sbuf([128, T], F32, "hb", st)
            lob = c.sbuf([128, T], BF16, "lob", st)
            for cc in range(4):
                c.dma("sp", xr[:], s["zT"][512 + cc * 128:512 + (cc + 1) * 128, :], outs=[xr], sb=xr)
                c.dma("sp", xg[:], s["zT"][1024 + cc * 128:1024 + (cc + 1) * 128, :], outs=[xg], sb=xg)
                c.op("dve", lambda e, cc=cc: e.tensor_scalar(out=xb[:], in0=xr[:], scalar1=cw[:, cc, 2:3], scalar2=cb[:, cc:cc + 1], op0=ALU.mult, op1=ALU.add),
                     outs=[xb], ins=[xr, cw, cb])
                xr3 = xr[:].rearrange("p (r w) -> p r w", w=row)
                xb3 = xb[:].rearrange("p (r w) -> p r w", w=row)
                for k in (0, 1, 3):
                    o = k - 2
                    dlo, dhi = max(0, -o), row - max(0, o)
                    c.op("dve", lambda e, cc=cc, k=k, o=o, dlo=dlo, dhi=dhi: e.scalar_tensor_tensor(
                        out=xb3[:, :, dlo:dhi], in0=xr3[:, :, dlo + o:dhi + o], scalar=cw[:, cc, k:k + 1], in1=xb3[:, :, dlo:dhi],
                        op0=ALU.mult, op1=ALU.add), outs=[xb], ins=[xr, cw, xb])
                for d in range(2):
                    h = hf if d == 0 else hb
                    for t0 in range(0, T, 512):
                        ps = c.next_ps()
                        c.op("pe", lambda e, t0=t0, d=d, cc=cc: e.matmul(ps[:], Wa[:, d, cc, :], xb[:, t0:t0 + 512], start=True, stop=True),
                             outs=[ps], ins=[Wa, xb])
                        c.op("act", lambda e, t0=t0, d=d, cc=cc: e.activation(ra[:, t0:t0 + 512], ps[:], AF.Sigmoid, bias=ba[:, d, cc:cc + 1]),
                             outs=[ra], ins=[ps, ba], nosame=True)
                        ps2 = c.next_ps()
                        c.op("pe", lambda e, t0=t0, d=d, cc=cc: e.matmul(ps2[:], Wx[:, d, cc, :], xb[:, t0:t0 + 512], start=True, stop=True),
                             outs=[ps2], ins=[Wx, xb])
                        c.op("act", lambda e, t0=t0, d=d, cc=cc: e.activation(ib[:, t0:t0 + 512], ps2[:], AF.Sigmoid, bias=bx[:, d, cc:cc + 1]),
                             outs=[ib], ins=[ps2, bx], nosame=True)
                    c.op("act", lambda e, d=d, cc=cc: e.activation(ra[:], ra[:], AF.Exp, scale=lam[:, d, cc:cc + 1]), outs=[ra], ins=[ra, lam])
                    c.op("dve", lambda e: e.tensor_tensor(out=h[:], in0=ra[:], in1=ra[:], op=ALU.mult), outs=[h], ins=[ra])
                    c.op("dve", lambda e: e.tensor_scalar(out=h[:], in0=h[:], scalar1=-1.0, scalar2=1.0, op0=ALU.mult, op1=ALU.add), outs=[h], ins=[h])
                    c.op("dve", lambda e: e.tensor_scalar(out=h[:], in0=h[:], scalar1=0.0, scalar2=None, op0=ALU.max), outs=[h], ins=[h])
                    c.op("act", lambda e: e.activation(h[:], h[:], AF.Sqrt), outs=[h], ins=[h])
                    c.op("dve", lambda e: e.tensor_tensor(out=ib[:], in0=ib[:], in1=h[:], op=ALU.mult), outs=[ib], ins=[ib, h])
                    c.op("dve", lambda e: e.tensor_tensor(out=ib[:], in0=ib[:], in1=xb[:], op=ALU.mult), outs=[ib], ins=[ib, xb])
                    if h0l is not None:
                        col = 0 if d == 0 else T - 1
                        c.op("dve", lambda e, col=col, d=d, cc=cc: e.scalar_tensor_tensor(
                            out=ib[:, col:col + 1], in0=ra[:, col:col + 1], scalar=h0l[:, d, cc:cc + 1], in1=ib[:, col:col + 1],
                            op0=ALU.mult, op1=ALU.add), outs=[ib], ins=[ra, h0l, ib])
                    for sq_ in range(nseq):
                        sl = slice(sq_ * L, (sq_ + 1) * L)
                        if d == 0:
                            c.op("dve", lambda e, sl=sl: e.tensor_tensor_scan(h[:, sl], ra[:, sl], ib[:, sl], 0.0, ALU.mult, ALU.add),
                                 outs=[h], ins=[ra, ib])
                        else:
                            c.op("dve", lambda e, sl=sl: e.tensor_tensor_scan(h[:, sl][:, ::-1], ra[:, sl][:, ::-1], ib[:, sl][:, ::-1], 0.0, ALU.mult, ALU.add),
                                 outs=[h], ins=[ra, ib])
                        if stL is not None:
                            col = sq_ * L + (L - 1 if d == 0 else 0)
                            c.op("dve", lambda e, sq_=sq_, d=d, cc=cc, col=col: e.tensor_copy(stL[:, sq_, d, cc:cc + 1], h[:, col:col + 1]),
                                 outs=[stL], ins=[h])
                c.op("dve", lambda e: e.tensor_tensor(out=hf[:], in0=hf[:], in1=hb[:], op=ALU.add), outs=[hf], ins=[hf, hb])
                gelu(ib[:], ib, xg[:], xg, ra[:], ra)
                c.op("dve", lambda e: e.tensor_tensor(out=lob[:], in0=hf[:], in1=ib[:], op=ALU.mult), outs=[lob], ins=[hf, ib])
                c.dma("sp", s["mixT"][512 + cc * 128:512 + (cc + 1) * 128, :], lob[:], ins=[lob], sb=lob)
            if stL is not None:
                for sq_ in range(nseq):
                    for d in range(2):
                        c.dma("sp", O["lru"][sq_, d, :].rearrange("(c p) -> p c", p=128), stL[:, sq_, d, :], ins=[stL], sb=stL,
                              allow_slow_non_contiguous=True)
            c.barrier()

    def wrap_pi(buf):
        with contextlib.ExitStack() as st2:
            m = c.sbuf(list(buf.t.shape), F32, "wrapm", st2)
            c.op("dve", lambda e: e.tensor_single_scalar(m[:], buf[:], math.pi, ALU.is_gt), outs=[m], ins=[buf])
            c.op("dve", lambda e: e.scalar_tensor_tensor(out=buf[:], in0=m[:], scalar=-TWO_PI, in1=buf[:], op0=ALU.mult, op1=ALU.add),
                 outs=[buf], ins=[m, buf])
            c.op("dve", lambda e: e.tensor_single_scalar(m[:], buf[:], -math.pi, ALU.is_lt), outs=[m], ins=[buf])
            c.op("dve", lambda e: e.scalar_tensor_tensor(out=buf[:], in0=m[:], scalar=TWO_PI, in1=buf[:], op0=ALU.mult, op1=ALU.add),
                 outs=[buf], ins=[m, buf])
            c.op("dve", lambda e: e.tensor_scalar(out=buf[:], in0=buf[:], scalar1=math.pi, scalar2=-math.pi, op0=ALU.min, op1=ALU.max),
                 outs=[buf], ins=[buf])
            c.barrier()

    def stage_hyfilt(s):
        L = s["L"]
        nj = L // 128
        rn_all = c.sbuf([128, 8], F32, "rn_all")
        s["rn"] = rn_all
        with contextlib.ExitStack() as st:
            emb = c.sbuf([33, L], F32, "emb", st)
            c.dma("sp", emb[:], I[f"emb{L}"].ap(), outs=[emb], sb=emb)
            w1 = c.sbuf([33, 64], F32, "w1", st)
            w2 = c.sbuf([64, 64], F32, "w2", st)
            w3 = c.sbuf([64, 2048], F32, "w3", st)
            c.dma("sp", w1[:], W["hy_w1"][0], outs=[w1], sb=w1)
            c.dma("sp", w2[:], W["hy_w2"][0], outs=[w2], sb=w2)
            c.dma("sp", w3[:], W["hy_w3"][0], outs=[w3], sb=w3)
            vec = c.sbuf([64, 4], F32, "hvec", st)
            for i, nm in enumerate(["hy_b1", "hy_freq1", "hy_b2", "hy_freq2"]):
                c.dma("sp", vec[:, i:i + 1], W[nm][0, :].rearrange("(p o) -> p o", o=1), outs=[vec], sb=vec, allow_slow_non_contiguous=True)
            fb = c.sbuf([64, 2], F32, "fb", st)
            c.op("dve", lambda e: e.tensor_tensor(out=fb[:, 0:1], in0=vec[:, 0:1], in1=vec[:, 1:2], op=ALU.mult), outs=[fb], ins=[vec])
            c.op("dve", lambda e: e.tensor_tensor(out=fb[:, 1:2], in0=vec[:, 2:3], in1=vec[:, 3:4], op=ALU.mult), outs=[fb], ins=[vec])
            z1 = c.sbuf([64, L], F32, "z1", st)
            z2 = c.sbuf([64, L], F32, "z2", st)
            CWL = min(512, L)
            for t0 in range(0, L, CWL):
                ps = c.next_ps()
                c.op("pe", lambda e, t0=t0: e.matmul(ps[0:64, 0:CWL], w1[:], emb[:, t0:t0 + CWL], start=True, stop=True), outs=[ps], ins=[w1, emb])
                c.op("dve", lambda e, t0=t0: e.tensor_scalar(out=z1[:, t0:t0 + CWL], in0=ps[0:64, 0:CWL], scalar1=vec[:, 1:2], scalar2=fb[:, 0:1], op0=ALU.mult, op1=ALU.add),
                     outs=[z1], ins=[ps, vec, fb])
            wrap_pi(z1)
            c.op("act", lambda e: e.activation(z1[:], z1[:], AF.Sin), outs=[z1], ins=[z1])
            for t0 in range(0, L, CWL):
                ps = c.next_ps()
                c.op("pe", lambda e, t0=t0: e.matmul(ps[0:64, 0:CWL], w2[:], z1[:, t0:t0 + CWL], start=True, stop=True), outs=[ps], ins=[w2, z1])
                c.op("dve", lambda e, t0=t0: e.tensor_scalar(out=z2[:, t0:t0 + CWL], in0=ps[0:64, 0:CWL], scalar1=vec[:, 3:4], scalar2=fb[:, 1:2], op0=ALU.mult, op1=ALU.add),
                     outs=[z2], ins=[ps, vec, fb])
            wrap_pi(z2)
            c.op("act", lambda e: e.activation(z2[:], z2[:], AF.Sin), outs=[z2], ins=[z2])
            dec = c.sbuf([128, 2048], F32, "dec", st)
            c.dma("sp", dec[:], W["hy_decay"][0].rearrange("a c -> (a c)").partition_broadcast(128), outs=[dec], sb=dec)
            c.op("act", lambda e: e.activation(dec[:], dec[:], AF.Abs), outs=[dec], ins=[dec])
            nt = c.sbuf([128, 32], F32, "nt", st)
            c.op("dve", lambda e: e.tensor_scalar(out=nt[:], in0=tcol[:], scalar1=-1.0 / L, scalar2=None, op0=ALU.mult), outs=[nt], ins=[tcol])
            win = c.sbuf([128, 2048], F32, "win", st)
            fl = c.sbuf([128, 2048], F32, "fl", st)
            flb = [c.sbuf([128, 2048], BF16, f"flb{i}", st) for i in range(2)]
            sqt = c.sbuf([128, 2048], F32, "sqt", st)
            accs = c.sbuf([128, 8], F32, "accs", st)
            c.op("dve", lambda e: e.memset(accs[:], 0.0), outs=[accs])
            for j in range(nj):
                for q in range(4):
                    ps = c.next_ps()
                    c.op("pe", lambda e, j=j, q=q: e.matmul(ps[:], z2[:, j * 128:(j + 1) * 128], w3[:, q * 512:(q + 1) * 512], start=True, stop=True),
                         outs=[ps], ins=[z2, w3])
                    c.op("act", lambda e, q=q: e.copy(fl[:, q * 512:(q + 1) * 512], ps[:]), outs=[fl], ins=[ps])
                c.op("act", lambda e, j=j: e.activation(win[:], dec[:], AF.Exp, scale=nt[:, j:j + 1]), outs=[win], ins=[dec, nt])
                c.op("dve", lambda e: e.scalar_tensor_tensor(out=fl[:], in0=win[:], scalar=0.05, in1=fl[:], op0=ALU.add, op1=ALU.mult),
                     outs=[fl], ins=[win, fl])
                if j == 0:
                    c.op("dve", lambda e: e.memset(fl[0:1, 1024:2048], 0.0), outs=[fl])
                c.op("dve", lambda e: e.tensor_tensor(out=sqt[:], in0=fl[:], in1=fl[:], op=ALU.mult), outs=[sqt], ins=[fl])
                ps = c.next_ps()
                for cbk in range(8):
                    c.op("pe", lambda e, cbk=cbk: e.matmul(ps[:, cbk:cbk + 1], sqt[:, cbk * 128:(cbk + 1) * 128], ones[:, 0:1], start=True, stop=False),
                         outs=[ps], ins=[sqt, ones])
                    c.op("pe", lambda e, cbk=cbk: e.matmul(ps[:, cbk:cbk + 1], sqt[:, 1024 + cbk * 128:1024 + (cbk + 1) * 128], ones[:, 0:1], start=False, stop=True),
                         outs=[ps], ins=[sqt, ones])
                c.op("dve", lambda e: e.tensor_tensor(out=accs[:], in0=accs[:], in1=ps[:, 0:8], op=ALU.add), outs=[accs], ins=[accs, ps])
                fb_ = flb[j % 2]
                c.op("pool", lambda e: e.tensor_tensor(out=fb_[:, 0:1024], in0=fl[:, 0:1024], in1=fl[:, 1024:2048], op=ALU.add), outs=[fb_], ins=[fl])
                c.op("dve", lambda e: e.tensor_tensor(out=fb_[:, 1024:2048], in0=fl[:, 0:1024], in1=fl[:, 1024:2048], op=ALU.subtract), outs=[fb_], ins=[fl])
                c.dma("sp", s["filt"][j * 128:(j + 1) * 128, :], fb_[:], ins=[fb_], sb=fb_)
            c.op("dve", lambda e: e.tensor_scalar(out=accs[:], in0=accs[:], scalar1=1e-6, scalar2=None, op0=ALU.add), outs=[accs], ins=[accs])
            c.op("act", lambda e: e.activation(accs[:], accs[:], AF.Sqrt), outs=[accs], ins=[accs])
            c.op("dve", lambda e: e.reciprocal(rn_all[:], accs[:]), outs=[rn_all], ins=[accs])
            c.barrier()

    def stage_hyena(s):
        T, nseq, L, row = s["T"], s["nseq"], s["L"], s["row"]
        nj = L // 128
        R = 2 * L
        RW = min(512, L)
        nrt2 = L // RW
        NB4 = RW // 128
        KH = min(nj, 16)
        nkh = nj // KH
        rn_all = s["rn"]
        Gd, Gid = I[f"G{L}"], I[f"Gi{L}"]
        with contextlib.ExitStack() as st:
            hw = c.sbuf([128, 24, 3], F32, "hw", st)
            for k in range(3):
                c.dma("sp", hw[:, :, k], W["hy_conv_w"][0, k, :].rearrange("(c p) -> p c", p=128), outs=[hw], sb=hw, allow_slow_non_contiguous=True)
            hcb = c.sbuf([128, 24], F32, "hcb", st)
            fm_load("sp", hcb, hcb[:], W["hy_conv_b"][0, :])
            hbias = c.sbuf([128, 8], F32, "hbias", st)
            fm_load("sp", hbias, hbias[:], W["hy_bias"][0, :])
            raw = c.sbuf([128, T], F32, "raw", st)
            bA = c.sbuf([128, T], F32, "bA", st)
            bB = c.sbuf([128, T], F32, "bB", st)
            vvb = c.sbuf([128, T], BF16, "vvb", st)
            vtm = c.sbuf([128, T // 128, 128], BF16, "vtm", st)
            ftm = c.sbuf([128, nj, 2, 128], BF16, "ftm", st)
            Gt = [c.sbuf([128, max(KH, min(R // 128, 16)), 512], BF16, f"Gt{i}", st) for i in range(3)]
            khat = c.sbuf([128, 2, 512], F32, "khat", st)
            tmpa = c.sbuf([128, 512], F32, "tmpa", st)
            tmpb = c.sbuf([128, 512], F32, "tmpb", st)
            yhat = c.sbuf([128, 2, 512], BF16, "yhat", st)
            yrm = c.sbuf([128, nseq, R // 128, 128], BF16, "yrm", st)
            ot = [c.sbuf([128, 512], F32, f"hyo{i}", st) for i in range(2)]
            otb = [c.sbuf([128, 512], BF16, f"hyob{i}", st) for i in range(2)]
            gi = 0

            def conv3(dst, chunk):
                c.op("dve", lambda e: e.tensor_scalar(out=dst[:], in0=raw[:], scalar1=hw[:, chunk, 1:2], scalar2=hcb[:, chunk:chunk + 1], op0=ALU.mult, op1=ALU.add),
                     outs=[dst], ins=[raw, hw, hcb])
                r3 = raw[:].rearrange("p (r w) -> p r w", w=row)
                d3 = dst[:].rearrange("p (r w) -> p r w", w=row)
                for k in (0, 2):
                    o = k - 1
                    dlo, dhi = max(0, -o), row - max(0, o)
                    c.op("dve", lambda e, k=k, o=o, dlo=dlo, dhi=dhi: e.scalar_tensor_tensor(
                        out=d3[:, :, dlo:dhi], in0=r3[:, :, dlo + o:dhi + o], scalar=hw[:, chunk, k:k + 1], in1=d3[:, :, dlo:dhi],
                        op0=ALU.mult, op1=ALU.add), outs=[dst], ins=[raw, hw, dst])

            for cbk in range(8):
                c.dma("sp", raw[:], s["zT"][2048 + cbk * 128:2048 + (cbk + 1) * 128, :], outs=[raw], sb=raw)
                conv3(bA, 16 + cbk)
                c.dma("sp", raw[:], s["zT"][1024 + cbk * 128:1024 + (cbk + 1) * 128, :], outs=[raw], sb=raw)
                conv3(bB, 8 + cbk)
                c.op("dve", lambda e: e.tensor_tensor(out=bA[:], in0=bA[:], in1=bB[:], op=ALU.mult), outs=[bA], ins=[bA, bB])
                c.op("act", lambda e: e.copy(vvb[:], bA[:]), outs=[vvb], ins=[bA])
                c.dma("sp", raw[:], s["zT"][cbk * 128:(cbk + 1) * 128, :], outs=[raw], sb=raw)
                conv3(bB, cbk)
                JB = min(8, T // 128)
                for j0 in range(0, T // 128, JB):
                    for jj in range(JB):
                        c.op("pe", lambda e, j0=j0, jj=jj: e.transpose(psb[:, jj * 128:(jj + 1) * 128], vvb[:, (j0 + jj) * 128:(j0 + jj + 1) * 128], identb[:]),
                             outs=[psb], ins=[vvb, identb])
                    c.op("act", lambda e, j0=j0: e.copy(vtm[:, j0:j0 + JB, :], psb[:, 0:JB * 128].rearrange("p (j s) -> p j s", j=JB)), outs=[vtm], ins=[psb])
                for a_ in range(2):
                    c.dma("sp", ftm[:, :, a_, :], s["filt"][:, a_ * 1024 + cbk * 128:a_ * 1024 + (cbk + 1) * 128].rearrange("(j p) c -> p j c", p=128),
                          outs=[ftm], sb=ftm)
                for i in range(nrt2):
                    for sq_ in range(nseq):
                        psv = [c.next_ps(), c.next_ps()]
                        psk = [c.next_ps(), c.next_ps()] if sq_ == 0 else None
                        for part in range(2):
                            r0 = part * L + i * RW
                            for kh in range(nkh):
                                if sq_ == 0:
                                    G_ = Gt[gi % 3]
                                    gi += 1
                                    c.dma("sp", G_[:, 0:KH, 0:RW], Gd[r0 // RW, kh], outs=[G_], sb=G_)
                                    s.setdefault("_gcache", {})[(part, kh)] = G_
                                else:
                                    G_ = s["_gcache"][(part, kh)]
                                for k in range(KH):
                                    tc_ = sq_ * nj + kh * KH + k
                                    first = (kh == 0 and k == 0)
                                    lastm = (kh == nkh - 1 and k == KH - 1)
                                    c.op("pe", lambda e, part=part, tc_=tc_, k=k, G_=G_, first=first, lastm=lastm: e.matmul(
                                        psv[part][:, 0:RW], vtm[:, tc_, :], G_[:, k, 0:RW], start=first, stop=lastm), outs=[psv[part]], ins=[vtm, G_])
                                    if sq_ == 0:
                                        jc = kh * KH + k
                                        c.op("pe", lambda e, part=part, jc=jc, k=k, G_=G_, first=first, lastm=lastm: e.matmul(
                                            psk[part][:, 0:RW], ftm[:, jc, part, :], G_[:, k, 0:RW], start=first, stop=lastm), outs=[psk[part]], ins=[ftm, G_])
                        if sq_ == 0:
                            for part in range(2):
                                c.op("act", lambda e, part=part: e.activation(khat[:, part, 0:RW], psk[part][:, 0:RW], AF.Copy, scale=rn_all[:, cbk:cbk + 1]),
                                     outs=[khat], ins=[psk[part], rn_all])
                        c.op("dve", lambda e: e.tensor_tensor(out=tmpa[:, 0:RW], in0=psv[0][:, 0:RW], in1=khat[:, 0, 0:RW], op=ALU.mult), outs=[tmpa], ins=[psv[0], khat])
                        c.op("dve", lambda e: e.tensor_tensor(out=tmpb[:, 0:RW], in0=psv[1][:, 0:RW], in1=khat[:, 1, 0:RW], op=ALU.mult), outs=[tmpb], ins=[psv[1], khat])
                        c.op("dve", lambda e: e.tensor_tensor(out=yhat[:, 0, 0:RW], in0=tmpa[:, 0:RW], in1=tmpb[:, 0:RW], op=ALU.subtract), outs=[yhat], ins=[tmpa, tmpb])
                        c.op("dve", lambda e: e.tensor_tensor(out=tmpa[:, 0:RW], in0=psv[0][:, 0:RW], in1=khat[:, 1, 0:RW], op=ALU.mult), outs=[tmpa], ins=[psv[0], khat])
                        c.op("dve", lambda e: e.tensor_tensor(out=tmpb[:, 0:RW], in0=psv[1][:, 0:RW], in1=khat[:, 0, 0:RW], op=ALU.mult), outs=[tmpb], ins=[psv[1], khat])
                        c.op("dve", lambda e: e.tensor_tensor(out=yhat[:, 1, 0:RW], in0=tmpa[:, 0:RW], in1=tmpb[:, 0:RW], op=ALU.add), outs=[yhat], ins=[tmpa, tmpb])
                        for part in range(2):
                            for q in range(NB4):
                                c.op("pe", lambda e, part=part, q=q: e.transpose(psb[:, (part * 4 + q) * 128:(part * 4 + q + 1) * 128], yhat[:, part, q * 128:(q + 1) * 128], identb[:]),
                                     outs=[psb], ins=[yhat, identb])
                        for part in range(2):
                            rc0 = (part * L + i * RW) // 128
                            c.op("act", lambda e, part=part, rc0=rc0, sq_=sq_: e.copy(yrm[:, sq_, rc0:rc0 + NB4, :], psb[:, part * 512:part * 512 + NB4 * 128].rearrange("p (q s) -> p q s", q=NB4)),
                                 outs=[yrm], ins=[psb])
                nrc = R // 128
                RH = min(nrc, 16)
                CWL = min(512, L)
                for tt in range(L // CWL):
                    pss_ = [c.next_ps() for _ in range(nseq)]
                    for rh in range(nrc // RH):
                        G_ = Gt[gi % 3]
                        gi += 1
                        c.dma("sp", G_[:, 0:RH, 0:CWL], Gid[tt, rh], outs=[G_], sb=G_)
                        for sq_ in range(nseq):
                            for k in range(RH):
                                rc = rh * RH + k
                                c.op("pe", lambda e, sq_=sq_, rc=rc, k=k, G_=G_: e.matmul(pss_[sq_][:, 0:CWL], yrm[:, sq_, rc, :], G_[:, k, 0:CWL],
                                                                                      start=(rc == 0), stop=(rc == nrc - 1)),
                                     outs=[pss_[sq_]], ins=[yrm, G_])
                    for sq_ in range(nseq):
                        t0 = sq_ * L + tt * CWL
                        o_ = ot[(tt * nseq + sq_) % 2]
                        c.op("dve", lambda e, t0=t0, sq_=sq_, o_=o_: e.scalar_tensor_tensor(out=o_[:, 0:CWL], in0=bA[:, t0:t0 + CWL], scalar=hbias[:, cbk:cbk + 1],
                                                                                      in1=pss_[sq_][:, 0:CWL], op0=ALU.mult, op1=ALU.add),
                             outs=[o_], ins=[bA, hbias, pss_[sq_]])
                        ob_ = otb[(tt * nseq + sq_) % 2]
                        c.op("dve", lambda e, t0=t0, o_=o_, ob_=ob_: e.tensor_tensor(out=ob_[:, 0:CWL], in0=o_[:, 0:CWL], in1=bB[:, t0:t0 + CWL], op=ALU.mult),
                             outs=[ob_], ins=[o_, bB])
                        c.dma("sp", s["mixT"][cbk * 128:(cbk + 1) * 128, t0:t0 + CWL], ob_[:, 0:CWL], ins=[ob_], sb=ob_)
            c.barrier()

    def stage_peer(s, x_src, x_dst, l, rows=None, nb=None):
        T = s["T"]
        nb = nb or T // 128
        with contextlib.ExitStack() as st:
            wq = c.sbuf([128, 8, 2048], F32, "wq", st)
            for k in range(8):
                c.dma("sp", wq[:, k, :], W["peer_wq"][l, k * 128:(k + 1) * 128, :], outs=[wq], sb=wq)
            kn = c.sbuf([128, 16, 128], F32, "kn", st)
            c.dma("sp", kn[:], W["peer_keys"][l].rearrange("h p n k -> n (h p) k"), outs=[kn], sb=kn)
            kT = c.sbuf([128, 16, 128], F32, "kT", st)
            for hp0 in range(0, 16, 4):
                ps = c.next_ps()
                for q in range(4):
                    c.op("pe", lambda e, q=q, hp0=hp0: e.transpose(ps[:, q * 128:(q + 1) * 128], kn[:, hp0 + q, :], ident[:]), outs=[ps], ins=[kn, ident])
                c.op("act", lambda e, hp0=hp0: e.copy(kT[:, hp0:hp0 + 4, :], ps[:].rearrange("p (q s) -> p q s", q=4)), outs=[kT], ins=[ps])
            A2 = c.sbuf([128, D], F32, "A2", st)
            S2 = c.sbuf([128, D], F32, "S2", st)
            G2 = c.sbuf([128, D], F32, "G2", st)
            xts = [c.sbuf([128, D], F32, f"px{i}", st) for i in range(2)]
            htms = [c.sbuf([128, D], F32, f"htm{i}", st) for i in range(2)]
            gr = htms[1]
            ci = s["ci"]
            c.dma("sp", S2[:], mod_d[l, ci, 3 * D:4 * D].partition_broadcast(128), outs=[S2], sb=S2)
            c.dma("sp", A2[:], mod_d[l, ci, 4 * D:5 * D].partition_broadcast(128), outs=[A2], sb=A2)
            c.dma("sp", G2[:], mod_d[l, ci, 5 * D:6 * D].partition_broadcast(128), outs=[G2], sb=G2)
            c.dma("sp", gr[:], W["norm2_g"][l, :].partition_broadcast(128), outs=[gr], sb=gr)
            c.op("dve", lambda e: e.scalar_tensor_tensor(out=A2[:], in0=A2[:], scalar=1.0, in1=gr[:], op0=ALU.add, op1=ALU.mult), outs=[A2], ins=[A2, gr])
            hT = c.sbuf([128, 8, 128], F32, "phT", st)
            qT = c.sbuf([128, 16, 128], F32, "qT", st)
            sc = kn
            sc2 = c.sbuf([128, 128], F32, "sc2", st)
            sv = c.sbuf([128, 16, 16], F32, "sv", st)
            si = c.sbuf([128, 16, 16], U32, "si", st)
            sif = c.sbuf([128, 16, 16], F32, "sif", st)
            cand2 = c.sbuf([128, 256], F32, "cand2", st)
            fv = c.sbuf([128, 8, 16], F32, "fv", st)
            fp_ = c.sbuf([128, 8, 16], U32, "fp_", st)
            j1 = c.sbuf([128, 8, 16], I32, "j1", st)
            j1f = c.sbuf([128, 8, 16], F32, "j1f", st)
            j2f = c.sbuf([128, 8, 16], F32, "j2f", st)
            ai_ = c.sbuf([128, 8, 16], F32, "ai_", st)
            bi_ = c.sbuf([128, 8, 16], F32, "bi_", st)
            eidxs = [c.sbuf([128, 128], I32, f"eidx{i}", st) for i in range(2)]
            gws = [c.sbuf([128, 8, 16], F32, f"gw{i}", st) for i in range(2)]
            zs_ = c.sbuf([128, 8], F32, "zs_", st)
            NV = 6
            vb16 = [c.sbuf([128, D], BF16, f"vb16_{i}", st) for i in range(NV)]
            dgs = [c.sbuf([128, 128], BF16, f"dgs{i}", st) for i in range(4)]
            saved_psl = c.psl
            c.psl = saved_psl[:5]
            psA, psB = saved_psl[5], saved_psl[6]
            junks = [c.sbuf([128, D], BF16, f"junkb{i}", st) for i in range(2)]
            GS = 2
            NBUF = 6
            uvs = [c.sbuf([128, 2 * D], F32, f"uvp{i}", st) for i in range(NBUF)]
            prods = [c.sbuf([128, D], F32, f"prod{i}", st) for i in range(2)] if _EXP.get("split", False) else None
            actc = [c.sbuf([128, GS], F32, f"actc{i}", st) for i in range(2)]
            gtc = [c.sbuf([128, GS], F32, f"gtc{i}", st) for i in range(2)]
            gac = [c.sbuf([128, GS], F32, f"gac{i}", st) for i in range(2)]
            uv_d = I["peer_uv"].ap()
            qr = None
            if rows is not None:
                qr = c.sbuf([128, nb], I32, "qr", st)
                c.dma("sp", qr[:], rows.ap(), outs=[qr], sb=qr)
            ss = c.sbuf([128, 1], F32, "pss", st)
            ss2 = c.sbuf([128, 1], F32, "pss2", st)
            iota16 = iota_f[:, 0:16]
            eqb, candb = sc, qT
            eq4 = sc[:].rearrange("p (h a) (b c) -> p h (a b) c", a=2, c=16)
            cand3 = qT[:].rearrange("p (h a) n -> p h (a n)", a=2)
            ac_tr = [[c.vbuf(f"actr{i}{j}") for j in range(GS)] for i in range(2)]

            def pre_ops(b):
                xt, htm, eidx, gw = xts[b % 2], htms[b % 2], eidxs[b % 2], gws[b % 2]
                if qr is None:
                    c.dma("sp", xt[:], x_src[b * 128:(b + 1) * 128, :], outs=[xt], sb=xt)
                else:
                    c.dma("pool", xt[:], x_src.ap(), outs=[xt], ins=[qr], sb=xt,
                          indirect=dict(out_offset=None, in_offset=bass.IndirectOffsetOnAxis(ap=qr[:, b:b + 1].bitcast(U32), axis=0)))
                yield
                yield
                yield
                jk = junks[0]
                c.op("dve", lambda e: e.scalar_tensor_tensor(out=jk[:], in0=xt[:], scalar=1.0, in1=xt[:], op0=ALU.mult, op1=ALU.mult, accum_out=ss[:]),
                     outs=[jk, ss], ins=[xt])
                yield
                c.op("dve", lambda e: e.tensor_scalar(out=ss2[:], in0=ss[:], scalar1=1.0 / D, scalar2=1e-6, op0=ALU.mult, op1=ALU.add), outs=[ss2], ins=[ss])
                c.op("act", lambda e: e.activation(ss2[:], ss2[:], AF.Sqrt), outs=[ss2], ins=[ss2])
                yield
                c.op("dve", lambda e: e.reciprocal(ss2[:], ss2[:]), outs=[ss2], ins=[ss2])
                yield
                c.op("dve", lambda e: e.scalar_tensor_tensor(out=htm[:], in0=xt[:], scalar=ss2[:, 0:1], in1=A2[:], op0=ALU.mult, op1=ALU.mult),
                     outs=[htm], ins=[xt, ss2, A2])
                yield
                c.op("dve", lambda e: e.tensor_tensor(out=htm[:], in0=htm[:], in1=S2[:], op=ALU.add), outs=[htm], ins=[htm, S2])
                yield
                for half in range(2):
                    ps = c.next_ps()
                    for kk in range(4):
                        k = half * 4 + kk
                        c.op("pe", lambda e, k=k, kk=kk: e.transpose(ps[:, kk * 128:(kk + 1) * 128], htm[:, k * 128:(k + 1) * 128], ident[:]), outs=[ps], ins=[htm, ident])
                    c.op("act", lambda e, half=half: e.copy(hT[:, half * 4:half * 4 + 4, :], ps[:].rearrange("p (q s) -> p q s", q=4)), outs=[hT], ins=[ps])
                    yield
                for m0 in range(0, 16, 4):
                    ps = c.next_ps()
                    for mm in range(4):
                        m = m0 + mm
                        for k in range(8):
                            c.op("pe", lambda e, m=m, mm=mm, k=k: e.matmul(ps[:, mm * 128:(mm + 1) * 128], wq[:, k, m * 128:(m + 1) * 128], hT[:, k, :], start=(k == 0), stop=(k == 7)),
                                 outs=[ps], ins=[wq, hT], nosame=True)
                        yield
                    c.op("act", lambda e, m0=m0: e.copy(qT[:, m0:m0 + 4, :], ps[:].rearrange("p (q s) -> p q s", q=4)), outs=[qT], ins=[ps])
                    yield
                for m0 in range(0, 16, 4):
                    ps = c.next_ps()
                    for mm in range(4):
                        m = m0 + mm
                        c.op("pe", lambda e, m=m, mm=mm: e.matmul(ps[:, mm * 128:(mm + 1) * 128], qT[:, m, :], kT[:, m, :], start=True, stop=True), outs=[ps], ins=[qT, kT])
                    c.op("act", lambda e, m0=m0: e.copy(sc[:, m0:m0 + 4, :], ps[:].rearrange("p (q s) -> p q s", q=4)), outs=[sc], ins=[ps])
                    yield
                for _ in range(24):
                    yield
                for m in range(16):
                    c.op("dve", lambda e, m=m: e.max(sv[:, m, 0:8], sc[:, m, :]), outs=[sv], ins=[sc])
                    yield
                    c.op("dve", lambda e, m=m: e.max_index(si[:, m, 0:8], sv[:, m, 0:8], sc[:, m, :]), outs=[si], ins=[sv, sc])
                    c.op("dve", lambda e, m=m: e.match_replace(sc2[:], sv[:, m, 0:8], sc[:, m, :], -1e30), outs=[sc2], ins=[sv, sc])
                    yield
                    c.op("dve", lambda e, m=m: e.max(sv[:, m, 8:16], sc2[:]), outs=[sv], ins=[sc2])
                    yield
                    c.op("dve", lambda e, m=m: e.max_index(si[:, m, 8:16], sv[:, m, 8:16], sc2[:]), outs=[si], ins=[sv, sc2])
                    yield
                c.op("dve", lambda e: e.tensor_copy(sif[:], si[:]), outs=[sif], ins=[si])
                yield
                sv4 = sv[:].rearrange("p (h a) j -> p h a j", a=2)
                sif4 = sif[:].rearrange("p (h a) j -> p h a j", a=2)
                c.op("dve", lambda e: e.tensor_tensor(out=cand3.rearrange("p h (a b) -> p h a b", b=16),
                                                      in0=sv4[:, :, 0, :].unsqueeze(3).broadcast_to([128, 8, 16, 16]),
                                                      in1=sv4[:, :, 1, :].unsqueeze(2).broadcast_to([128, 8, 16, 16]), op=ALU.add),
                     outs=[candb], ins=[sv])
                yield
                for h in range(8):
                    c.op("dve", lambda e, h=h: e.max(fv[:, h, 0:8], cand3[:, h, :]), outs=[fv], ins=[candb])
                    yield
                    c.op("dve", lambda e, h=h: e.max_index(fp_[:, h, 0:8], fv[:, h, 0:8], cand3[:, h, :]), outs=[fp_], ins=[fv, candb])
                    c.op("dve", lambda e, h=h: e.match_replace(cand2[:], fv[:, h, 0:8], cand3[:, h, :], -1e30), outs=[cand2], ins=[fv, candb])
                    yield
                    c.op("dve", lambda e, h=h: e.max(fv[:, h, 8:16], cand2[:]), outs=[fv], ins=[cand2])
                    yield
                    c.op("dve", lambda e, h=h: e.max_index(fp_[:, h, 8:16], fv[:, h, 8:16], cand2[:]), outs=[fp_], ins=[fv, cand2])
                    yield
                fpi = fp_[:].bitcast(I32)
                c.op("dve", lambda e: e.tensor_single_scalar(j1[:], fpi, 4, ALU.logical_shift_right), outs=[j1], ins=[fp_])
                yield
                c.op("dve", lambda e: e.tensor_copy(j1f[:], j1[:]), outs=[j1f], ins=[j1])
                yield
                c.op("dve", lambda e: e.tensor_single_scalar(j1[:], fpi, 15, ALU.bitwise_and), outs=[j1], ins=[fp_])
                yield
                c.op("dve", lambda e: e.tensor_copy(j2f[:], j1[:]), outs=[j2f], ins=[j1])
                yield
                for (jf, a, dst) in ((j1f, 0, ai_), (j2f, 1, bi_)):
                    for h in range(8):
                        c.op("dve", lambda e, jf=jf, h=h: e.tensor_tensor(out=eq4[:, h, :, :], in0=iota16.unsqueeze(1).broadcast_to([128, 16, 16]),
                                                                        in1=jf[:, h, :].unsqueeze(2).broadcast_to([128, 16, 16]), op=ALU.is_equal),
                             outs=[eqb], ins=[iota_f, jf])
                        yield
                    for h in range(8):
                        c.op("dve", lambda e, a=a, h=h: e.tensor_tensor(out=eq4[:, h, :, :], in0=eq4[:, h, :, :],
                                                                      in1=sif4[:, h, a, :].unsqueeze(1).broadcast_to([128, 16, 16]), op=ALU.mult),
                             outs=[eqb], ins=[eqb, sif])
                        yield
                    c.op("dve", lambda e, dst=dst: e.tensor_reduce(out=dst[:].rearrange("p h j -> p (h j)"), in_=eq4.rearrange("p h a b -> p (h a) b"), axis=AX.X, op=ALU.add),
                         outs=[dst], ins=[eqb])
                    yield
                c.op("dve", lambda e: e.scalar_tensor_tensor(out=ai_[:], in0=ai_[:], scalar=128.0, in1=bi_[:], op0=ALU.mult, op1=ALU.add), outs=[ai_], ins=[ai_, bi_])
                yield
                if l > 0:
                    c.op("dve", lambda e: e.tensor_scalar(out=ai_[:], in0=ai_[:], scalar1=float(l * 16384), scalar2=None, op0=ALU.add), outs=[ai_], ins=[ai_])
                    yield
                c.op("dve", lambda e: e.tensor_copy(eidx[:], ai_[:].rearrange("p h j -> p (h j)")), outs=[eidx], ins=[ai_])
                yield
                c.op("dve", lambda e: e.tensor_tensor(out=gw[:], in0=fv[:], in1=fv[:, :, 0:1].broadcast_to([128, 8, 16]), op=ALU.subtract), outs=[gw], ins=[fv])
                c.op("act", lambda e: e.activation(gw[:], gw[:], AF.Exp), outs=[gw], ins=[gw])
                yield
                yield
                yield
                c.op("dve", lambda e: e.tensor_reduce(out=zs_[:], in_=gw[:], axis=AX.X, op=ALU.add), outs=[zs_], ins=[gw])
                yield
                c.op("dve", lambda e: e.reciprocal(zs_[:], zs_[:]), outs=[zs_], ins=[zs_])
                yield
                c.op("dve", lambda e: e.tensor_tensor(out=gw[:], in0=gw[:], in1=zs_[:].unsqueeze(2).broadcast_to([128, 8, 16]), op=ALU.mult), outs=[gw], ins=[gw, zs_])
                yield

            def pump(gen, n):
                if gen is None:
                    return None
                for _ in range(n):
                    try:
                        next(gen)
                    except StopIteration:
                        return None
                return gen

            def slots(b, gen):
                xt, htm, eidx, gw = xts[b % 2], htms[b % 2], eidxs[b % 2], gws[b % 2]
                gwf = gw[:].rearrange("p h j -> p (h j)")
                NG = 128 // GS

                def gather(sl):
                    ub = uvs[sl % NBUF]
                    c.dma("pool", ub[:], uv_d, outs=[ub], ins=[eidx], sb=ub,
                          indirect=dict(out_offset=None, in_offset=bass.IndirectOffsetOnAxis(ap=eidx[:, sl:sl + 1].bitcast(U32), axis=0)))

                def dot(g, j):
                    sl = g * GS + j
                    ub, ac, jk = uvs[sl % NBUF], actc[g % 2], junks[sl % 2]
                    if j == 1 and _EXP.get("split", False):
                        pr_ = prods[g % 2]
                        c.op("pool", lambda e: e.tensor_tensor(out=pr_[:], in0=ub[:, 0:D], in1=htm[:], op=ALU.mult), outs=[pr_], ins=[ub, htm])
                        c.op("act", lambda e: e.activation(pr_[:], pr_[:], AF.Copy, accum_out=ac[:, j:j + 1]), outs=[pr_, ac_tr[g % 2][j]], ins=[pr_])
                    else:
                        c.op("dve", lambda e: e.scalar_tensor_tensor(out=jk[:], in0=ub[:, 0:D], scalar=1.0, in1=htm[:], op0=ALU.mult, op1=ALU.mult,
                                                                     accum_out=ac[:, j:j + 1]), outs=[jk, ac_tr[g % 2][j]], ins=[ub, htm])
                    vb = vb16[sl % NV]
                    c.op("act", lambda e: e.copy(vb[:], ub[:, D:2 * D]), outs=[vb], ins=[ub])

                def p1(g):
                    ac, gt = actc[g % 2], gtc[g % 2]
                    c.op("dve", lambda e: e.scalar_tensor_tensor(out=gt[:], in0=ac[:], scalar=0.044715, in1=ac[:], op0=ALU.mult, op1=ALU.mult),
                         outs=[gt], ins=ac_tr[g % 2])

                def p2(g):
                    ac, gt = actc[g % 2], gtc[g % 2]
                    c.op("dve", lambda e: e.scalar_tensor_tensor(out=gt[:], in0=gt[:], scalar=1.0, in1=ac[:], op0=ALU.add, op1=ALU.mult),
                         outs=[gt], ins=[gt] + ac_tr[g % 2])
                    c.op("act", lambda e: e.activation(gt[:], gt[:], AF.Sigmoid, scale=1.5957691216057308), outs=[gt], ins=[gt])

                def f1(g):
                    ac, gt, ga_ = actc[g % 2], gtc[g % 2], gac[g % 2]
                    c.op("dve", lambda e: e.tensor_tensor(out=ga_[:], in0=ac[:], in1=gt[:], op=ALU.mult), outs=[ga_], ins=[gt] + ac_tr[g % 2])

                def f2(g):
                    ga_ = gac[g % 2]
                    c.op("dve", lambda e: e.tensor_tensor(out=ga_[:], in0=ga_[:], in1=gwf[:, g * GS:(g + 1) * GS], op=ALU.mult), outs=[ga_], ins=[ga_, gw])

                def vmm(g, j):
                    sl = g * GS + j
                    vb, ga_, dg_ = vb16[sl % NV], gac[g % 2], dgs[sl % 4]
                    c.op("act", lambda e: e.activation(dg_[:], identb[:], AF.Copy, scale=ga_[:, j:j + 1]), outs=[dg_], ins=[identb, ga_])
                    c.op("pe", lambda e: e.matmul(psA[:], dg_[:], vb[:, 0:512], start=(sl == 0), stop=(sl == 127)), outs=[psA], ins=[dg_, vb], nosame=True)
                    c.op("pe", lambda e: e.matmul(psB[:], dg_[:], vb[:, 512:1024], start=(sl == 0), stop=(sl == 127)), outs=[psB], ins=[dg_, vb], nosame=True)

                for sl in range(NBUF):
                    gather(sl)
                dot(0, 0)
                gather(NBUF)
                dot(0, 1)
                gather(NBUF + 1)
                p1(0); p2(0)
                for g in range(NG):
                    nx = g + 1 < NG
                    if nx:
                        dot(g + 1, 0)
                        if (g + 1) * GS + NBUF < 128:
                            gather((g + 1) * GS + NBUF)
                    f1(g)
                    if nx:
                        dot(g + 1, 1)
                        if (g + 1) * GS + 1 + NBUF < 128:
                            gather((g + 1) * GS + 1 + NBUF)
                    f2(g)
                    if nx:
                        p1(g + 1)
                    vmm(g, 0)
                    vmm(g, 1)
                    if nx:
                        p2(g + 1)
                    gen = pump(gen, 4)
                pump(gen, 100000)
                for half, ps_ in enumerate((psA, psB)):
                    hs = slice(half * 512, (half + 1) * 512)
                    c.op("dve", lambda e, hs=hs, ps_=ps_: e.tensor_tensor(out=htm[:, hs], in0=ps_[:], in1=G2[:, hs], op=ALU.mult), outs=[htm], ins=[ps_, G2])
                c.op("dve", lambda e: e.tensor_tensor(out=xt[:], in0=xt[:], in1=htm[:], op=ALU.add), outs=[xt], ins=[xt, htm])
                c.dma("sp", x_dst[b * 128:(b + 1) * 128, :], xt[:], ins=[xt], sb=xt)

            pump(pre_ops(0), 100000)
            for b in range(nb):
                slots(b, pre_ops(b + 1) if b + 1 < nb else None)
            c.barrier()
            c.psl = saved_psl

    def stage_final(s, x_src, y_dst=None, T=None):
        T = T or s["T"]
        y_dst = y_dst if y_dst is not None else s["y"]
        with contextlib.ExitStack() as st:
            fg = c.sbuf([128, D], F32, "fg", st)
            c.dma("sp", fg[:], W["final_g"].ap().partition_broadcast(128), outs=[fg], sb=fg)
            xts = [c.sbuf([128, D], F32, f"fx{i}", st) for i in range(2)]
            junk = c.sbuf([128, D], F32, "fjunk", st)
            ss = c.sbuf([128, 1], F32, "fss", st)
            for b in range(T // 128):
                xt = xts[b % 2]
                c.dma("sp", xt[:], x_src[b * 128:(b + 1) * 128, :], outs=[xt], sb=xt)
                c.op("dve", lambda e: e.scalar_tensor_tensor(out=junk[:], in0=xt[:], scalar=1.0, in1=xt[:], op0=ALU.mult, op1=ALU.mult, accum_out=ss[:]),
                     outs=[junk, ss], ins=[xt])
                c.op("dve", lambda e: e.tensor_scalar(out=ss[:], in0=ss[:], scalar1=1.0 / D, scalar2=1e-6, op0=ALU.mult, op1=ALU.add), outs=[ss], ins=[ss])
                c.op("act", lambda e: e.activation(ss[:], ss[:], AF.Sqrt), outs=[ss], ins=[ss])
                c.op("dve", lambda e: e.reciprocal(ss[:], ss[:]), outs=[ss], ins=[ss])
                c.op("dve", lambda e: e.scalar_tensor_tensor(out=xt[:], in0=xt[:], scalar=ss[:, 0:1], in1=fg[:], op0=ALU.mult, op1=ALU.mult),
                     outs=[xt], ins=[xt, ss, fg])
                c.dma("sp", y_dst[b * 128:(b + 1) * 128, :], xt[:], ins=[xt], sb=xt)
            c.barrier()

    c.barrier()
    if want("mod"):
        stage_mod()
    for s in secs:
        xin = s["x"]
        if want("in0"):
            stage_inproj(s, xin, 0, W["w_in_ab"][0], None, 1536, 1)
        if want("s5"):
            stage_s5(s)
        if want("glu"):
            stage_glu(s)
        if want("lru"):
            stage_lru(s)
        if want("out0"):
            stage_outproj(s, xin, s["xa"], 0, W["w_out_ab"][0], None)
        if want("peer0"):
            stage_peer(s, s["xa"], s["xb"], 0)
        if want("in1"):
            stage_inproj(s, s["xb"], 1, W["w_in_c"][0], W["b_in_c"][0, :], 3072, 1)
        if want("hyfilt"):
            stage_hyfilt(s)
        if want("hyena"):
            stage_hyena(s)
        if want("out1"):
            stage_outproj(s, s["xb"], s["xa"], 1, W["w_out_c"][0], W["b_out_c"][0, :])
        if s["n"] == "S":
            if want("peer1"):
                stage_peer(s, s["xa"], s["xq"], 1, rows=I["qrows"], nb=8)
            if want("final"):
                stage_final(s, s["xq"], y_dst=O["ysq"], T=1024)
        else:
            if want("peer1"):
                stage_peer(s, s["xa"], s["xb"], 1)
            if want("final"):
                stage_final(s, s["xb"])
    c.barrier()
    return nc, c


_NC_CACHE = {}
_EXP = {}


def kernel(**inputs):
    NP = 4
    if "nc" not in _NC_CACHE:
        _NC_CACHE["nc"] = build(NP=NP)[0]
    nc = _NC_CACHE["nc"]
    cs = _consts()
    f32 = lambda a: np.ascontiguousarray(np.asarray(a, dtype=np.float32))
    shared = {k: f32(inputs[k]) for k in WEIGHT_SPECS}
    shared.update(cs)
    shared["peer_uv"] = np.ascontiguousarray(
        np.concatenate([f32(inputs["peer_u"]), f32(inputs["peer_v"])], axis=-1).reshape(2 * 16384, 2 * D))
    xp = f32(inputs["x_prompt"])
    xs = f32(inputs["x_sample"])
    c_ = f32(inputs["c"])
    cctx = f32(inputs["c_ctx"])
    sre, sim, slr = f32(inputs["state_s5_re"]), f32(inputs["state_s5_im"]), f32(inputs["state_lru"])
    in_maps = []
    for core in range(NCORES):
        b = core % 2
        m = dict(shared)
        m["xp"] = np.ascontiguousarray(xp[core * NP:(core + 1) * NP].reshape(NP * 256, D))
        m["xs"] = np.ascontiguousarray(xs[b])
        m["cond"] = np.ascontiguousarray(np.stack([cctx, c_[b]], axis=0))
        m["h0re"] = np.ascontiguousarray(sre[b, 0])
        m["h0im"] = np.ascontiguousarray(sim[b, 0])
        m["h0lru"] = np.ascontiguousarray(slr[b, 0])
        q = core // 2
        m["qrows"] = np.ascontiguousarray((q * 1024 + np.arange(8)[None, :] * 128 + np.arange(128)[:, None]).astype(np.int32))
        in_maps.append(m)
    res = run_bass_kernel_spmd(nc, in_maps, core_ids=list(range(NCORES)))
    r = res.results
    y_prompt = np.concatenate([r[i]["yp"].reshape(NP, 256, D) for i in range(NCORES)], axis=0).astype(np.float32)
    y_sample = np.stack([np.concatenate([r[b + 2 * q]["ysq"] for q in range(4)], axis=0) for b in range(2)], axis=0).astype(np.float32)
    s5re = np.concatenate([r[i]["s5re"] for i in range(NCORES)], axis=0)[:, None].astype(np.float32)
    s5im = np.concatenate([r[i]["s5im"] for i in range(NCORES)], axis=0)[:, None].astype(np.float32)
    lru = np.concatenate([r[i]["lru"] for i in range(NCORES)], axis=0)[:, None].astype(np.float32)
    return (y_prompt, y_sample, s5re, s5im, lru)
```
